# Optimizing a Trainium2 kernel written in Bass

```python
import math
import jax
import jax.numpy as jnp
from jax import lax
import numpy as np

D_MODEL = 1024
BATCH = 4
SEQ = 4096
DEPTH = 2

PLE_DIM = 256
D_FF = 2816
LN_EPS = 1e-5
DEEPNORM_ALPHA = (2 * DEPTH) ** 0.25
DEEPNORM_BETA = (8 * DEPTH) ** -0.25
MACARON_W = 0.5

RW_HEADS = 8
RW_HEAD = 64
RW_WIDTH = RW_HEADS * RW_HEAD
RW_DECAY_LORA = 64
RW_AAA_LORA = 64
RW_GATE_LORA = 160
RW_GN_EPS = 1e-5 * RW_HEAD

GLA_HEADS = 4
GLA_DK = 64
GLA_DV = 128
GLA_GATE_LORA = 16
GLA_GATE_NORM = 16.0
GLA_CHUNK = 64

GDN_HEADS = 4
GDN_DK = 128
GDN_DV = 128
GDN_CONV = 4
GDN_CHUNK = 64

HEAD_NORM_EPS = 1e-6
L2_EPS = 1e-6
N_BRANCH = 3
BRANCH_WIDTH = 512

RW_SPLITS = (RW_WIDTH, RW_WIDTH, RW_WIDTH, RW_DECAY_LORA, RW_AAA_LORA, RW_GATE_LORA)
GLA_SPLITS = (GLA_HEADS * GLA_DK, GLA_HEADS * GLA_DK, GLA_HEADS * GLA_DV, GLA_GATE_LORA, GLA_HEADS * GLA_DV)
GDN_QKV = GDN_HEADS * (2 * GDN_DK + GDN_DV)
GDN_SPLITS = (GDN_QKV, GDN_HEADS, GDN_HEADS, GDN_HEADS * GDN_DV)
RW_IN = sum(RW_SPLITS)
GROUP_SPLITS = (RW_IN, sum(GLA_SPLITS), sum(GDN_SPLITS), N_BRANCH * D_MODEL)
D_IN = sum(GROUP_SPLITS)

kernel_name = "hybrid_rwkv7_gla_gdn_macaron_deepnorm"


def _split(h, sizes):
    idx, s = [], 0
    for w in sizes[:-1]:
        s += w
        idx.append(s)
    return jnp.split(h, idx, axis=-1)


def _heads(t, h, d):
    return t.reshape(t.shape[:-1] + (h, d))


def _layer_norm(x, g, b):
    xf = x.astype(jnp.float32)
    m = jnp.mean(xf, -1, keepdims=True)
    var = jnp.mean(jnp.square(xf - m), -1, keepdims=True)
    return ((xf - m) * lax.rsqrt(var + LN_EPS) * g + b).astype(x.dtype)


def _rms_norm(x, g):
    return x * lax.rsqrt(jnp.mean(jnp.square(x), -1, keepdims=True) + HEAD_NORM_EPS) * g


def _l2norm(x):
    return x * lax.rsqrt(jnp.sum(jnp.square(x), -1, keepdims=True) + L2_EPS)


def _swiglu(x, w1, w3, w2):
    return (jax.nn.silu(x @ w1) * (x @ w3)) @ w2


def rwkv7_time_mix(h_rw, mu, w0, w2, a0, a2, g2, k_k, k_a, r_k, gn_g, gn_b):
    B, T, _ = h_rw.shape
    prev = jnp.pad(h_rw, ((0, 0), (1, 0), (0, 0)))[:, :-1]
    h_rw = h_rw + (prev - h_rw) * mu
    r, k, v, w_lo, a_lo, g_lo = _split(h_rw, RW_SPLITS)
    w = -jax.nn.softplus(-(w0 + jnp.tanh(w_lo) @ w2)) - 0.5
    decay = jnp.exp(-jnp.exp(w))
    a = jax.nn.sigmoid(a0 + a_lo @ a2)
    g = jax.nn.sigmoid(g_lo) @ g2
    kk = _l2norm(_heads(k * k_k, RW_HEADS, RW_HEAD))
    k = k * (1.0 + (a - 1.0) * k_a)
    r, k, v, decay, a = (_heads(t, RW_HEADS, RW_HEAD) for t in (r, k, v, decay, a))

    def step(S, inp):
        r_t, k_t, v_t, d_t, a_t, b_t = inp
        sa = jnp.einsum('bhvk,bhk->bhv', S, a_t)
        S = S * d_t[:, :, None, :] + sa[..., None] * b_t[:, :, None, :] + v_t[..., None] * k_t[:, :, None, :]
        return S, jnp.einsum('bhvk,bhk->bhv', S, r_t)

    xs = tuple(jnp.moveaxis(t, 1, 0) for t in (r, k, v, decay, -kk, kk * a))
    S0 = jnp.zeros((B, RW_HEADS, RW_HEAD, RW_HEAD), jnp.float32)
    _, y = lax.scan(step, S0, xs)
    y = jnp.moveaxis(y, 0, 1)
    ym = jnp.mean(y, -1, keepdims=True)
    yv = jnp.mean(jnp.square(y - ym), -1, keepdims=True)
    y = ((y - ym) * lax.rsqrt(yv + RW_GN_EPS)).reshape(B, T, RW_WIDTH) * gn_g + gn_b
    bonus = jnp.sum(r * k * r_k, -1, keepdims=True) * v
    return (y + bonus.reshape(B, T, RW_WIDTH)) * g


def gla_mix(q, k, v, gk_lo, gate, gk_w2, gk_b, norm_g):
    B, T, _ = q.shape
    C, n, H = GLA_CHUNK, T // GLA_CHUNK, GLA_HEADS
    log_a = jax.nn.log_sigmoid(gk_lo @ gk_w2 + gk_b) / GLA_GATE_NORM

    def chunked(t, d):
        return t.reshape(B, n, C, H, d).transpose(1, 0, 3, 2, 4)

    q = chunked(q * GLA_DK ** -0.5, GLA_DK)
    k = chunked(k, GLA_DK)
    v = chunked(v, GLA_DV)
    b = jnp.cumsum(chunked(log_a, GLA_DK), axis=3)
    causal = jnp.tril(jnp.ones((C, C), dtype=bool))[:, :, None]

    def step(S, inp):
        q_c, k_c, v_c, b_c = inp
        rel = jnp.where(causal, b_c[:, :, :, None, :] - b_c[:, :, None, :, :], -jnp.inf)
        att = jnp.sum(q_c[:, :, :, None, :] * k_c[:, :, None, :, :] * jnp.exp(rel), axis=-1)
        o = att @ v_c + (q_c * jnp.exp(b_c)) @ S
        b_end = b_c[:, :, -1:, :]
        S = S * jnp.exp(b_end[:, :, 0, :, None]) + jnp.einsum('bhjd,bhjv->bhdv', k_c * jnp.exp(b_end - b_c), v_c)
        return S, o

    S0 = jnp.zeros((B, H, GLA_DK, GLA_DV), jnp.float32)
    _, o = lax.scan(step, S0, (q, k, v, b))
    o = o.transpose(1, 0, 3, 2, 4).reshape(B, T, H, GLA_DV)
    return _rms_norm(o, norm_g).reshape(B, T, H * GLA_DV) * jax.nn.silu(gate)


def gated_deltanet_mix(qkv, a_in, b_in, gate, conv_w, a_log, dt_bias, norm_g):
    B, T, _ = qkv.shape
    C, n, H = GDN_CHUNK, T // GDN_CHUNK, GDN_HEADS
    qkv = jax.nn.silu(lax.conv_general_dilated(
        qkv, conv_w[:, None, :].astype(qkv.dtype), (1,), [(GDN_CONV - 1, 0)],
        dimension_numbers=('NWC', 'WIO', 'NWC'), feature_group_count=qkv.shape[-1]))
    q, k, v = _split(qkv, (H * GDN_DK, H * GDN_DK, H * GDN_DV))
    q = _l2norm(_heads(q, H, GDN_DK)) * GDN_DK ** -0.5
    k = _l2norm(_heads(k, H, GDN_DK))
    v = _heads(v, H, GDN_DV)
    beta = jax.nn.sigmoid(b_in)
    g = -jnp.exp(a_log) * jax.nn.softplus(a_in + dt_bias)

    def chunked(t):
        t = jnp.moveaxis(t, 2, 1)
        return t.reshape((B, H, n, C) + t.shape[3:])

    q, k, v, beta, g = (chunked(t) for t in (q, k, v, beta, g))
    g = jnp.cumsum(g, axis=-1)
    causal = jnp.tril(jnp.ones((C, C), dtype=bool))
    strict = jnp.tril(jnp.ones((C, C), dtype=bool), -1)
    decay = jnp.exp(jnp.where(causal, g[..., :, None] - g[..., None, :], -jnp.inf))
    kb = k * beta[..., None]
    lower = jnp.where(strict, jnp.einsum('bhnid,bhnjd->bhnij', kb, k) * decay, 0.0)
    eye = jnp.eye(C, dtype=lower.dtype)
    t_inv = lax.linalg.triangular_solve(eye + lower, jnp.broadcast_to(eye, lower.shape),
                                        left_side=True, lower=True)
    u = t_inv @ (v * beta[..., None])
    w = t_inv @ (kb * jnp.exp(g)[..., None])
    att = jnp.einsum('bhnid,bhnjd->bhnij', q, k) * decay
    xs = tuple(jnp.moveaxis(t, 2, 0) for t in (q, k, u, w, g, att))

    def step(S, inp):
        q_c, k_c, u_c, w_c, g_c, att_c = inp
        v_new = u_c - w_c @ S
        o = (q_c * jnp.exp(g_c)[..., None]) @ S + att_c @ v_new
        g_end = g_c[..., -1:]
        S = S * jnp.exp(g_end)[..., None] + jnp.einsum('bhcd,bhcv->bhdv', k_c * jnp.exp(g_end - g_c)[..., None], v_new)
        return S, o

    S0 = jnp.zeros((B, H, GDN_DK, GDN_DV), jnp.float32)
    _, o = lax.scan(step, S0, xs)
    o = o.transpose(1, 0, 3, 2, 4).reshape(B, T, H, GDN_DV)
    return _rms_norm(o, norm_g).reshape(B, T, H * GDN_DV) * jax.nn.silu(gate)


def token_mixing(x, w_in, rw_mu, rw_w0, rw_w2, rw_a0, rw_a2, rw_g2, rw_k_k, rw_k_a, rw_r_k, rw_gn_g, rw_gn_b,
                 gla_gk_w2, gla_gk_b, gla_norm_g, gdn_conv_w, gdn_a_log, gdn_dt_bias, gdn_norm_g, w_branch, w_o):
    B, T, _ = x.shape
    h = (x @ w_in).astype(jnp.float32)
    h_rw, h_gla, h_gdn, h_gate = _split(h, GROUP_SPLITS)
    o_rw = rwkv7_time_mix(h_rw, rw_mu, rw_w0, rw_w2, rw_a0, rw_a2, rw_g2, rw_k_k, rw_k_a, rw_r_k, rw_gn_g, rw_gn_b)
    o_gla = gla_mix(*_split(h_gla, GLA_SPLITS), gla_gk_w2, gla_gk_b, gla_norm_g)
    o_gdn = gated_deltanet_mix(*_split(h_gdn, GDN_SPLITS), gdn_conv_w, gdn_a_log, gdn_dt_bias, gdn_norm_g)
    branches = jnp.stack([o_rw, o_gla, o_gdn], axis=2)
    proj = jnp.einsum('btnc,ncd->btnd', branches, w_branch)
    gates = jax.nn.sigmoid(h_gate).reshape(B, T, N_BRANCH, D_MODEL)
    merged = jnp.sum(gates * proj, axis=2)
    return merged.astype(x.dtype) @ w_o


def setup_inputs(seed: int = 0) -> dict:
    key = jax.random.key(seed)
    ks = iter(jax.random.split(key, 40))
    L, D = DEPTH, D_MODEL
    bt = DEEPNORM_BETA

    def nrm(shape, scale):
        return jax.random.normal(next(ks), shape, jnp.float32) * scale

    def uni(shape, lo, hi):
        return jax.random.uniform(next(ks), shape, jnp.float32, lo, hi)

    dt = jnp.exp(uni((L, GDN_HEADS), math.log(1e-3), math.log(1e-1)))
    return {
        "x": nrm((BATCH, SEQ, D), 1.0),
        "p": nrm((L, BATCH, SEQ, PLE_DIM), 1.0),
        "ln_g": 1.0 + nrm((L, 4, D), 0.02),
        "ln_b": nrm((L, 4, D), 0.02),
        "ffn_w1": nrm((L, 2, D, D_FF), D ** -0.5),
        "ffn_w3": nrm((L, 2, D, D_FF), D ** -0.5),
        "ffn_w2": nrm((L, 2, D_FF, D), bt * D_FF ** -0.5),
        "w_in": nrm((L, D, D_IN), D ** -0.5),
        "rw_mu": uni((L, RW_IN), 0.0, 1.0),
        "rw_w0": uni((L, RW_WIDTH), -6.0, -1.0),
        "rw_w2": nrm((L, RW_DECAY_LORA, RW_WIDTH), 0.5 * RW_DECAY_LORA ** -0.5),
        "rw_a0": nrm((L, RW_WIDTH), 0.1),
        "rw_a2": nrm((L, RW_AAA_LORA, RW_WIDTH), 0.5 * RW_AAA_LORA ** -0.5),
        "rw_g2": nrm((L, RW_GATE_LORA, RW_WIDTH), RW_GATE_LORA ** -0.5),
        "rw_k_k": 0.85 + nrm((L, RW_WIDTH), 0.05),
        "rw_k_a": 1.0 + nrm((L, RW_WIDTH), 0.05),
        "rw_r_k": nrm((L, RW_HEADS, RW_HEAD), 0.1),
        "rw_gn_g": 1.0 + nrm((L, RW_WIDTH), 0.02),
        "rw_gn_b": nrm((L, RW_WIDTH), 0.02),
        "gla_gk_w2": nrm((L, GLA_GATE_LORA, GLA_HEADS * GLA_DK), GLA_GATE_LORA ** -0.5),
        "gla_gk_b": uni((L, GLA_HEADS * GLA_DK), 0.0, 3.0),
        "gla_norm_g": 1.0 + nrm((L, GLA_DV), 0.02),
        "gdn_conv_w": nrm((L, GDN_CONV, GDN_QKV), GDN_CONV ** -0.5),
        "gdn_a_log": jnp.log(uni((L, GDN_HEADS), 1.0, 16.0)),
        "gdn_dt_bias": dt + jnp.log(-jnp.expm1(-dt)),
        "gdn_norm_g": 1.0 + nrm((L, GDN_DV), 0.02),
        "w_branch": nrm((L, N_BRANCH, BRANCH_WIDTH, D), bt * BRANCH_WIDTH ** -0.5),
        "w_o": nrm((L, D, D), bt * D ** -0.5),
        "ple_w_gate": nrm((L, D, D), D ** -0.5),
        "ple_w_proj": nrm((L, PLE_DIM, D), bt * PLE_DIM ** -0.5),
    }


def reference(x, p, ln_g, ln_b, ffn_w1, ffn_w3, ffn_w2, w_in, rw_mu, rw_w0, rw_w2, rw_a0, rw_a2, rw_g2,
              rw_k_k, rw_k_a, rw_r_k, rw_gn_g, rw_gn_b, gla_gk_w2, gla_gk_b, gla_norm_g, gdn_conv_w, gdn_a_log,
              gdn_dt_bias, gdn_norm_g, w_branch, w_o, ple_w_gate, ple_w_proj):
    a = DEEPNORM_ALPHA
    for i in range(DEPTH):
        x = _layer_norm(a * x + MACARON_W * _swiglu(x, ffn_w1[i, 0], ffn_w3[i, 0], ffn_w2[i, 0]), ln_g[i, 0], ln_b[i, 0])
        mix = token_mixing(x, w_in[i], rw_mu[i], rw_w0[i], rw_w2[i], rw_a0[i], rw_a2[i], rw_g2[i], rw_k_k[i],
                           rw_k_a[i], rw_r_k[i], rw_gn_g[i], rw_gn_b[i], gla_gk_w2[i], gla_gk_b[i], gla_norm_g[i],
                           gdn_conv_w[i], gdn_a_log[i], gdn_dt_bias[i], gdn_norm_g[i], w_branch[i], w_o[i])
        x = _layer_norm(a * x + mix, ln_g[i, 1], ln_b[i, 1])
        x = _layer_norm(a * x + MACARON_W * _swiglu(x, ffn_w1[i, 1], ffn_w3[i, 1], ffn_w2[i, 1]), ln_g[i, 2], ln_b[i, 2])
        ple = jax.nn.sigmoid(x @ ple_w_gate[i]) * (p[i] @ ple_w_proj[i])
        x = _layer_norm(a * x + ple, ln_g[i, 3], ln_b[i, 3])
    return x
```

```python
import contextlib
import numpy as np
import concourse.bass as bass
import concourse.mybir as mybir
from concourse.bass_utils import run_bass_kernel_spmd

F32 = mybir.dt.float32
BF16 = mybir.dt.bfloat16
AF = mybir.ActivationFunctionType
ALU = mybir.AluOpType
AX = mybir.AxisListType

D = 1024
DFF = 2816
TT = 512
C = 128
ALPHA = 4.0 ** 0.25
LN_EPS = 1e-5
SAME_ENGINE_SYNC = True
NSLOT = 5
SLOT = 2816


class V:
    __slots__ = ("b", "ap")

    def __init__(self, b, ap):
        self.b = b
        self.ap = ap


class Buf:
    __slots__ = ("t", "name", "last_write", "reads", "dsem", "dcnt", "c0", "owner")

    def __init__(self, t, name, c0=None, owner=None):
        self.owner = owner
        self.t = t
        self.name = name
        self.last_write = None
        self.reads = []
        self.dsem = None
        self.dcnt = 0
        self.c0 = c0

    def __getitem__(self, idx):
        if self.c0 is None:
            return V(self, self.t[idx])
        if not isinstance(idx, tuple):
            idx = (idx, slice(None))
        r, c = idx
        a = 0 if c.start is None else c.start
        b = 128 if c.stop is None else c.stop
        return V(self.owner or self, self.t[r, self.c0 + a: self.c0 + b])


class Sched:
    ENGS = ("pe", "act", "dve", "pool", "sp")

    def __init__(self, nc):
        self.nc = nc
        self.stack = contextlib.ExitStack()
        self.q = {e: [] for e in self.ENGS}
        self.sem = {}
        self.cnt = {e: 0 for e in self.ENGS}
        self.seen = {e: {} for e in self.ENGS}
        for e in self.ENGS:
            self.sem[e] = self.stack.enter_context(nc.semaphore("s_" + e))
        self.nbuf = 0
        self.ninst = 0
        self.rr = 0

    def sbuf(self, shape, dtype=F32, name=None):
        self.nbuf += 1
        name = (name or "b") + f"_{self.nbuf}"
        t = self.stack.enter_context(self.nc.sbuf_tensor(name, list(shape), dtype))
        return Buf(t, name)

    def psum(self, shape, dtype=F32, name=None):
        self.nbuf += 1
        name = (name or "p") + f"_{self.nbuf}"
        t = self.stack.enter_context(self.nc.psum_tensor(name, list(shape), dtype))
        return Buf(t, name)

    def dsem_for(self, buf):
        if buf.dsem is None:
            buf.dsem = self.stack.enter_context(self.nc.semaphore("d_" + buf.name))
        return buf.dsem

    def _deps(self, eng, reads, writes):
        waits = {}

        def add(rec):
            if rec is None:
                return
            s, v, owner = rec
            if owner == eng and (eng == "pe" or not SAME_ENGINE_SYNC):
                return
            k = id(s)
            if k not in waits or waits[k][1] < v:
                waits[k] = (s, v)

        for b in reads:
            add(b.last_write)
        for b in writes:
            add(b.last_write)
            for r in b.reads:
                add(r)
        out = []
        seen = self.seen[eng]
        for k, (s, v) in waits.items():
            if seen.get(k, 0) >= v:
                continue
            seen[k] = v
            out.append((s, v))
        return out

    def op(self, eng, fn, reads=(), writes=()):
        waits = self._deps(eng, reads, writes)
        self.cnt[eng] += 1
        v = self.cnt[eng]
        s = self.sem[eng]
        self.q[eng].append((fn, waits, (s, 1)))
        rec = (s, v, eng)
        for b in reads:
            if len(b.reads) > 24:
                b.reads = b.reads[-24:] if False else b.reads
            b.reads.append(rec)
        for b in writes:
            b.last_write = rec
            b.reads = []
        self.ninst += 1

    def dma(self, eng, out_ap, in_ap, reads=(), writes=(), sembuf=None):
        sembuf = sembuf or (writes[0] if writes else reads[0])
        ds = self.dsem_for(sembuf)
        waits = self._deps(eng, reads, writes)
        sembuf.dcnt += 16
        v = sembuf.dcnt
        self.q[eng].append((lambda e: e.dma_start(out=out_ap, in_=in_ap), waits, (ds, 16)))
        rec = (ds, v, "dma")
        for b in reads:
            b.reads.append(rec)
        for b in writes:
            b.last_write = rec
            b.reads = []
        self.ninst += 1

    def final_wait(self, eng, bufs):
        waits = self._deps(eng, bufs, bufs)
        self.q[eng].append((None, waits, None))

    def finish(self):
        nc = self.nc
        q = self.q

        def replay(name):
            def f(e):
                for fn, waits, inc in q[name]:
                    for s, v in waits:
                        e.wait_ge(s, v)
                    if fn is None:
                        continue
                    ins = fn(e)
                    if inc is not None:
                        ins.then_inc(inc[0], inc[1])
            return f

        with nc.Block() as block:
            block.tensor(replay("pe"))
            block.scalar(replay("act"))
            block.vector(replay("dve"))
            block.gpsimd(replay("pool"))
            block.sync(replay("sp"))
        self.stack.close()


def _bufs(*xs):
    out = []
    for x in xs:
        if isinstance(x, V) and x.b not in out:
            out.append(x.b)
    return out


def _ap(x):
    return x.ap if isinstance(x, V) else x


def slabify(W, wc):
    K, N = W.shape
    kc = K // 128
    return np.ascontiguousarray(
        W.reshape(kc, 128, N // wc, wc).transpose(2, 1, 0, 3).reshape(N // wc, 128, kc * wc))


def colvec(v):
    return np.ascontiguousarray(v.reshape(-1, 128).T)


def pad_cols(W, n):
    out = np.zeros((W.shape[0], n), np.float32)
    out[:, :W.shape[1]] = W
    return out


RW0 = 0
GLA0 = 1824
GDN0 = 3376
GATE0 = 5432
NFM = 42
PV_LNG, PV_LNB = 0, 32
PV_MU = 64
PV_A0, PV_KK, PV_KA, PV_RK, PV_GNG, PV_GNB = 80, 84, 88, 92, 96, 100
PV_GLAN, PV_GDNN = 104, 105
PV_CONV = 106
PV_ALOG, PV_DTB = 154, 158
NPV = 162
CI_ID, CI_ONE, CI_NEGONE, CI_LN, CI_V128, CI_BLK, CI_TLE, CI_TLT, CI_TGT, CI_NTLE, CI_MBSL, CI_MBIU = range(12)
NCI = 12
CB_ID, CB_MBSL, CB_MBIU = 0, 1, 2


def build_consts():
    p = np.arange(128)[:, None]
    j = np.arange(128)[None, :]
    m = np.zeros((NCI, 128, 128), np.float32)
    m[CI_ID] = (p == j)
    m[CI_ONE] = 1.0
    m[CI_NEGONE] = -1.0
    m[CI_LN] = 1.0 / 1024
    m[CI_V128] = 1.0 / 128
    m[CI_BLK] = ((p // 64) == (j // 64))
    m[CI_TLE] = (p <= j)
    m[CI_TLT] = (p < j)
    m[CI_TGT] = (p > j)
    m[CI_NTLE] = -(p <= j).astype(np.float32)
    m[CI_MBSL] = np.where(p > j, 0.0, -30000.0)
    m[CI_MBIU] = np.where(p <= j, 0.0, -30000.0)
    cf = np.ascontiguousarray(m.transpose(1, 0, 2).reshape(128, NCI * 128))
    cbm = np.stack([m[CI_ID], m[CI_MBSL], m[CI_MBIU]], 0)
    cb = np.ascontiguousarray(cbm.transpose(1, 0, 2).reshape(128, 3 * 128))
    return cf, cb


def regroup_win(w_in):
    fm = np.zeros((1024, NFM * 128), np.float32)

    def put(ch, src, n):
        fm[:, ch * 128: ch * 128 + n] = w_in[:, src: src + n]
    put(0, RW0 + 1536, 128)
    put(1, RW0 + 1664, 128)
    put(2, RW0 + 1792, 32)
    for j in range(4):
        put(4 + 3 * j, RW0 + j * 128, 128)
        put(5 + 3 * j, RW0 + 512 + j * 128, 128)
        put(6 + 3 * j, RW0 + 1024 + j * 128, 128)
    put(16, GLA0 + 1024, 16)
    for j in range(2):
        put(18 + 4 * j, GLA0 + j * 128, 128)
        put(19 + 4 * j, GLA0 + 256 + j * 128, 128)
        put(20 + 4 * j, GLA0 + 1040 + (2 * j) * 128, 128)
        put(21 + 4 * j, GLA0 + 1040 + (2 * j + 1) * 128, 128)
    for h in range(4):
        put(26 + 4 * h, GDN0 + h * 128, 128)
        put(27 + 4 * h, GDN0 + 512 + h * 128, 128)
        put(28 + 4 * h, GDN0 + 1024 + h * 128, 128)
        put(29 + 4 * h, GDN0 + 1544 + h * 128, 128)
    tm = np.zeros((1024, 6 * 256), np.float32)
    tm[:, 0:512] = w_in[:, RW0 + 1024: RW0 + 1536]
    tm[:, 512:768] = w_in[:, GLA0 + 256: GLA0 + 512]
    tm[:, 768:1280] = w_in[:, GLA0 + 512: GLA0 + 1024]
    tm[:, 1280:1288] = w_in[:, GDN0 + 1536: GDN0 + 1544]
    gate = w_in[:, GATE0: GATE0 + 3072]
    return fm, tm, gate


def prep_layer(inp, l, plan):
    A = {}
    for f in range(2):
        s1 = slabify(inp["ffn_w1"][l, f], 256)
        s3 = slabify(inp["ffn_w3"][l, f], 256)
        for g in range(11):
            A[("w1", f, g)] = s1[g]
            A[("w3", f, g)] = s3[g]
    fm, tm, gate = regroup_win(inp["w_in"][l])
    for i, s in enumerate(slabify(fm, 256)):
        A[("fm", i)] = s
    for i, s in enumerate(slabify(tm, 256)):
        A[("tm", i)] = s
    for i, s in enumerate(slabify(gate, 256)):
        A[("gate", i)] = s
    for b in range(3):
        sb = slabify(inp["w_branch"][l, b], 512)
        for half in range(2):
            A[("br", b, half)] = sb[half]
    for i, s in enumerate(slabify(inp["w_o"][l], 256)):
        A[("wo", i)] = s
    for i, s in enumerate(slabify(inp["ple_w_gate"][l], 256)):
        A[("pg", i)] = s
    A[("pp", 0)] = slabify(inp["ple_w_proj"][l], 1024)[0]
    lst = [A[k] for k in plan]
    while len(lst) < NA_PER_LAYER:
        lst.append(lst[0])
    wA = np.ascontiguousarray(np.stack(lst, 0))
    wB = np.ascontiguousarray(np.concatenate([slabify(inp["ffn_w2"][l, f], 128) for f in range(2)], 0))
    pv = np.zeros((128, NPV), np.float32)
    for i in range(4):
        pv[:, PV_LNG + i * 8: PV_LNG + i * 8 + 8] = colvec(inp["ln_g"][l, i])
        pv[:, PV_LNB + i * 8: PV_LNB + i * 8 + 8] = colvec(inp["ln_b"][l, i])
    mu = inp["rw_mu"][l]
    pv[:, PV_MU + 0] = mu[1536:1664]
    pv[:, PV_MU + 1] = mu[1664:1792]
    pv[0:32, PV_MU + 2] = mu[1792:1824]
    for j in range(4):
        pv[:, PV_MU + 4 + 3 * j] = mu[j * 128:(j + 1) * 128]
        pv[:, PV_MU + 5 + 3 * j] = mu[512 + j * 128: 512 + (j + 1) * 128]
        pv[:, PV_MU + 6 + 3 * j] = mu[1024 + j * 128: 1024 + (j + 1) * 128]
    pv[:, PV_A0: PV_A0 + 4] = colvec(inp["rw_a0"][l])
    pv[:, PV_KK: PV_KK + 4] = colvec(inp["rw_k_k"][l])
    pv[:, PV_KA: PV_KA + 4] = colvec(inp["rw_k_a"][l])
    pv[:, PV_RK: PV_RK + 4] = colvec(inp["rw_r_k"][l].reshape(-1))
    pv[:, PV_GNG: PV_GNG + 4] = colvec(inp["rw_gn_g"][l])
    pv[:, PV_GNB: PV_GNB + 4] = colvec(inp["rw_gn_b"][l])
    pv[:, PV_GLAN] = inp["gla_norm_g"][l]
    pv[:, PV_GDNN] = inp["gdn_norm_g"][l]
    for tap in range(4):
        pv[:, PV_CONV + tap * 12: PV_CONV + tap * 12 + 12] = colvec(inp["gdn_conv_w"][l, tap])
    pv[:, PV_ALOG: PV_ALOG + 4] = inp["gdn_a_log"][l][None, :]
    pv[:, PV_DTB: PV_DTB + 4] = inp["gdn_dt_bias"][l][None, :]
    sw = np.zeros((128, 2304), np.float32)
    sw[0:64, 0:512] = inp["rw_w2"][l]
    sw[64, 0:512] = inp["rw_w0"][l]
    sw[64:128, 512:1024] = inp["rw_a2"][l]
    sw[0:128, 1024:1536] = inp["rw_g2"][l][0:128]
    sw[0:32, 1536:2048] = inp["rw_g2"][l][128:160]
    sw[0:16, 2048:2304] = inp["gla_gk_w2"][l]
    sw[32, 2048:2304] = inp["gla_gk_b"][l]
    muv = np.ascontiguousarray(np.broadcast_to(mu[1024:1536][None, :], (128, 512))).astype(np.float32)
    return wA, wB, pv, sw, muv


NA_PER_LAYER = 22 + 21 + 6 + 18 + 4 + 22 + 4 + 1


class Prog:
    def __init__(self, T, layers, debug=(), stages=4, mixsel="rgd"):
        self.stages = stages
        self.mixsel = mixsel
        self.T = T
        self.NT = T // TT
        self.layers = list(layers)
        self.debug = set(debug)
        nc = self.nc = bass.Bass("TRN2", target_bir_lowering=False)
        self.xT = nc.dram_tensor("xT", [D, T], F32, kind="ExternalInput").ap()
        self.pT = nc.dram_tensor("pT", [2, 256, T], F32, kind="ExternalInput").ap()
        self.wA = nc.dram_tensor("wA", [2, NA_PER_LAYER, 128, 2048], F32, kind="ExternalInput").ap()
        self.wB = nc.dram_tensor("wB", [2, 16, 128, 2816], F32, kind="ExternalInput").ap()
        self.pvd = nc.dram_tensor("pv", [2, 128, NPV], F32, kind="ExternalInput").ap()
        self.swd = nc.dram_tensor("sw", [2, 128, 2304], F32, kind="ExternalInput").ap()
        self.muvd = nc.dram_tensor("muv", [2, 128, 512], F32, kind="ExternalInput").ap()
        self.cfd = nc.dram_tensor("cf", [128, NCI * 128], F32, kind="ExternalInput").ap()
        self.cbd = nc.dram_tensor("cb", [128, 3 * 128], F32, kind="ExternalInput").ap()
        self.outT = nc.dram_tensor("outT", [D, T], F32, kind="ExternalOutput").ap()
        self.dbg_out = {}
        self.plan = []
        self.S = Sched(nc)
        self.build()

    def mm(self, out, lhsT, rhs, start=True, stop=True):
        o, a, b = out.ap, lhsT.ap, rhs.ap
        self.S.op("pe", lambda e: e.matmul(o, lhsT=a, rhs=b, start=start, stop=stop),
                  reads=_bufs(lhsT, rhs), writes=[out.b])

    def tr(self, out, in_, ident):
        o, a, b = out.ap, in_.ap, ident.ap
        self.S.op("pe", lambda e: e.transpose(o, a, b), reads=_bufs(in_, ident), writes=[out.b])

    def act(self, out, in_, func, bias=0.0, scale=1.0):
        o, a, bi, sc = out.ap, in_.ap, _ap(bias), _ap(scale)
        self.S.op("act", lambda e: e.activation(out=o, in_=a, func=func, bias=bi, scale=sc),
                  reads=_bufs(in_, bias, scale), writes=[out.b])

    def tt(self, eng, out, a, b, op):
        o, x, y = out.ap, a.ap, b.ap
        self.S.op(eng, lambda e: e.tensor_tensor(out=o, in0=x, in1=y, op=op), reads=_bufs(a, b), writes=[out.b])

    def ts(self, eng, out, a, s1, s2, op0, op1=None):
        o, x, p1, p2 = out.ap, a.ap, _ap(s1), _ap(s2)
        if op1 is None:
            self.S.op(eng, lambda e: e.tensor_scalar(out=o, in0=x, scalar1=p1, scalar2=None, op0=op0),
                      reads=_bufs(a, s1), writes=[out.b])
        else:
            self.S.op(eng, lambda e: e.tensor_scalar(out=o, in0=x, scalar1=p1, scalar2=p2, op0=op0, op1=op1),
                      reads=_bufs(a, s1, s2), writes=[out.b])

    def stt(self, eng, out, a, s, b, op0, op1):
        o, x, sc, y = out.ap, a.ap, _ap(s), b.ap
        self.S.op(eng, lambda e: e.scalar_tensor_tensor(out=o, in0=x, scalar=sc, in1=y, op0=op0, op1=op1),
                  reads=_bufs(a, s, b), writes=[out.b])

    def cp(self, eng, out, a):
        o, x = out.ap, a.ap
        if eng == "act":
            self.S.op("act", lambda e: e.copy(out=o, in_=x), reads=[a.b], writes=[out.b])
        else:
            self.S.op(eng, lambda e: e.tensor_copy(out=o, in_=x), reads=[a.b], writes=[out.b])

    def memset(self, eng, out, val):
        o = out.ap
        self.S.op(eng, lambda e: e.memset(o, val), writes=[out.b])

    def recip(self, out, a):
        o, x = out.ap, a.ap
        self.S.op("dve", lambda e: e.reciprocal(out=o, in_=x), reads=[a.b], writes=[out.b])

    def ev(self):
        self.S.rr ^= 1
        return "dve" if self.S.rr else "act"

    def copy_any(self, out, a):
        self.cp(self.ev(), out, a)

    def bank(self):
        self._bi = (self._bi + 1) % len(self.MM)
        return self.MM[self._bi]

    def quarter(self):
        self._qi = (self._qi + 1) % len(self.Q)
        return self.Q[self._qi]

    def tbq(self):
        self._ti = (self._ti + 1) % len(self.TB)
        return self.TB[self._ti]

    def slabA(self, l, key):
        idx = self._ai[l]
        self._ai[l] += 1
        if idx >= len(self.plan):
            self.plan.append(key)
        assert self.plan[idx] == key, (idx, key, self.plan[idx])
        slot = self.ring[self._ri % NSLOT]
        self._ri += 1
        self.S.dma("pool", slot.t[:, 0:2048], self.wA[l, idx], writes=[slot])
        return slot

    def slabB(self, l, idx):
        slot = self.ring[self._ri % NSLOT]
        self._ri += 1
        self.S.dma("pool", slot.t[:, 0:2816], self.wB[l, idx], writes=[slot])
        return slot

    def fm_slab(self, l, g):
        if self._fmg[0] != (l, g):
            self._fmg = ((l, g), self.slabA(l, ("fm", g)))
        return self._fmg[1]

    def fm_chunk(self, l, ch):
        slab = self.fm_slab(l, ch // 2)
        return self.fm_chunk_psum(slab, ch % 2)

    def dump(self, name, view, shape):
        if name not in self.debug:
            return
        key = f"dbg_{name}_{len(self.dbg_out)}"
        d = self.nc.dram_tensor(key, list(shape), F32, kind="ExternalOutput").ap()
        self.dbg_out[key] = name
        if not self._dbg_bufs:
            self._dbg_bufs.append(self.S.sbuf(shape, F32, "dbgt"))
        tmp = self._dbg_bufs[0]
        self.cp("dve", tmp[:], view)
        self.S.dma("sp", d, tmp.t[:], reads=[tmp], sembuf=tmp)

    def build(self):
        S = self.S
        self._bi = self._qi = self._ti = 0
        self._ri = 0
        self._fmg = (None, None)
        self._dbg_bufs = []
        self.MM = [S.psum([128, 512], F32, "mm") for _ in range(3)]
        qbanks = [S.psum([128, 512], F32, "qb") for _ in range(4)]
        self.Q = [Buf(qbanks[i % 3].t, qbanks[i % 3].name + f"_{i}", c0=(i // 3) * 128, owner=qbanks[i % 3])
                  for i in range(12)]
        self.QF = [Buf(qbanks[3].t, qbanks[3].name + f"_f{i}", c0=i * 128, owner=qbanks[3]) for i in range(4)]
        tbb = S.psum([128, 1024], BF16, "tbb")
        self.TB = [Buf(tbb.t, tbb.name + f"_{i}", c0=i * 128, owner=tbb) for i in range(4)]
        cf = self.cf = S.sbuf([128, NCI, 128], F32, "cf")
        cb = self.cb = S.sbuf([128, 3, 128], BF16, "cb")
        S.dma("sp", cf.t[:], self.cfd.rearrange("p (n j) -> p n j", n=NCI), writes=[cf])
        S.dma("pool", cb.t[:], self.cbd.rearrange("p (n j) -> p n j", n=3), writes=[cb])
        self.pv = S.sbuf([128, 2, NPV], F32, "pv")
        S.dma("sp", self.pv.t[:], self.pvd.rearrange("l p n -> p l n"), writes=[self.pv])
        self.sw = S.sbuf([128, 2, 2304], BF16, "sw")
        S.dma("pool", self.sw.t[:], self.swd.rearrange("l p n -> p l n"), writes=[self.sw])
        self.epsb = S.sbuf([128, 4], F32, "epsb")
        self.memset("dve", self.epsb[:, 0:1], LN_EPS)
        self.memset("dve", self.epsb[:, 1:2], 1e-6)
        self.memset("dve", self.epsb[:, 2:3], 64e-5)
        self.memset("dve", self.epsb[:, 3:4], 1.0)
        self.nega = S.sbuf([128, 2, 4], F32, "nega")
        for l in range(2):
            self.act(self.nega[:, l, :], self.pv[:, l, PV_ALOG:PV_ALOG + 4], AF.Exp)
            self.ts("dve", self.nega[:, l, :], self.nega[:, l, :], -1.0, None, ALU.mult)
        self.ring = [S.sbuf([128, SLOT], BF16, "ring") for _ in range(NSLOT)]
        self.x = [S.sbuf([128, TT], F32, "x") for _ in range(8)]
        self.pool = [S.sbuf([128, TT], F32, "pool") for _ in range(8)]
        self.z = self.pool
        self.xb = [S.sbuf([128, TT], BF16, "xb") for _ in range(8)]
        self.G = [S.sbuf([128, TT], BF16, "G") for _ in range(22)]
        self.br = [self.G[0:4], self.G[4:8], self.G[8:12]]
        self.merged = self.G[12:20]
        self.dxb = self.G[12:20]
        self.pbt = self.G[20:22]
        self.t32 = [S.sbuf([128, TT], F32, "t32") for _ in range(5)]
        self._t32i = 0
        self.xcar = S.sbuf([128, 2, 8], BF16, "xcar")
        self.memset("dve", self.xcar[:], 0.0)
        self.hcar = S.sbuf([128, 2, 28, 3], F32, "hcar")
        self.memset("dve", self.hcar[:], 0.0)
        self.Srw = S.sbuf([128, 2, 4, 128], F32, "Srw")
        self.Srwb = S.sbuf([128, 2, 4, 128], BF16, "Srwb")
        self.Sgl = S.sbuf([128, 2, 2, 128], F32, "Sgl")
        self.Sglb = S.sbuf([128, 2, 2, 128], BF16, "Sglb")
        self.Sgd = S.sbuf([128, 2, 4, 128], F32, "Sgd")
        self.Sgdb = S.sbuf([128, 2, 4, 128], BF16, "Sgdb")
        for s_ in (self.Srw, self.Srwb, self.Sgl, self.Sglb, self.Sgd, self.Sgdb):
            self.memset("dve", s_[:], 0.0)
        self.alloc_mixer()
        self._ai = {0: 0, 1: 0}
        for ti in range(self.NT):
            t0 = ti * TT
            for m in range(8):
                S.dma("sp", self.x[m].t[:], self.xT[m * 128:(m + 1) * 128, t0:t0 + TT], writes=[self.x[m]])
            for l in self.layers:
                self._ai[l] = 0
                self.layer_tile(l, ti)
                assert self._ai[l] == NA_PER_LAYER or self.mixsel != "rgd", self._ai[l]
            for m in range(8):
                S.dma("sp", self.outT[m * 128:(m + 1) * 128, t0:t0 + TT], self.x[m].t[:], reads=[self.x[m]],
                      sembuf=self.x[m])
        S.final_wait("sp", self.x + self._dbg_bufs)
        S.finish()

    def tmp32(self):
        self._t32i = (self._t32i + 1) % len(self.t32)
        return self.t32[self._t32i]

    def make_xb(self):
        for m in range(8):
            self.copy_any(self.xb[m][:], self.x[m][:])

    def ffn(self, l, f):
        for g in range(11):
            w1 = self.slabA(l, ("w1", f, g))
            w3 = self.slabA(l, ("w3", f, g))
            for cc in range(2):
                c = g * 2 + cc
                p1 = self.bank()
                p3 = self.bank()
                for k in range(8):
                    self.mm(p1[:], w1[:, k * 256 + cc * 128: k * 256 + cc * 128 + 128], self.xb[k][:], k == 0, k == 7)
                for k in range(8):
                    self.mm(p3[:], w3[:, k * 256 + cc * 128: k * 256 + cc * 128 + 128], self.xb[k][:], k == 0, k == 7)
                t = self.tmp32()
                self.act(t[:], p1[:], AF.Silu)
                self.stt("dve", self.G[c][:], t[:], 0.5, p3[:], ALU.mult, ALU.mult)
        for m in range(8):
            w2 = self.slabB(l, f * 8 + m)
            p = self.bank()
            for c in range(22):
                self.mm(p[:], w2[:, c * 128:(c + 1) * 128], self.G[c][:], c == 0, c == 21)
            self.stt("dve", self.z[m][:], self.x[m][:], ALPHA, p[:], ALU.mult, ALU.add)

    def ln(self, l, i):
        pm = self.bank()
        pq = self.bank()
        lnm = self.cf[:, CI_LN, :]
        for m in range(8):
            self.mm(pm[:], lnm, self.z[m][:], m == 0, m == 7)
        for m in range(8):
            t = self.tmp32()
            self.act(t[:], self.z[m][:], AF.Square)
            self.mm(pq[:], lnm, t[:], m == 0, m == 7)
        mean = self.lnmean
        rstd = self.lnrstd
        self.cp("act", mean[:], pm[:])
        self.tt("dve", rstd[:], mean[:], mean[:], ALU.mult)
        self.tt("dve", rstd[:], pq[:], rstd[:], ALU.subtract)
        self.act(rstd[:], rstd[:], AF.Sqrt, bias=self.epsb[:, 0:1])
        self.recip(rstd[:], rstd[:])
        for m in range(8):
            t = self.tmp32()
            self.tt("dve", t[:], self.z[m][:], mean[:], ALU.subtract)
            self.tt("dve", t[:], t[:], rstd[:], ALU.mult)
            self.act(self.x[m][:], t[:], AF.Identity, bias=self.pv[:, l, PV_LNB + i * 8 + m: PV_LNB + i * 8 + m + 1],
                     scale=self.pv[:, l, PV_LNG + i * 8 + m: PV_LNG + i * 8 + m + 1])

    def layer_tile(self, l, ti):
        st = self.stages
        self.make_xb()
        self.ffn(l, 0)
        self.ln(l, 0)
        if st < 2:
            self._ai[l] = NA_PER_LAYER
            return
        self.make_xb()
        self.mixers(l, ti)
        self.ln(l, 1)
        if st < 3:
            self._ai[l] = NA_PER_LAYER
            return
        self.make_xb()
        self.ffn(l, 1)
        self.ln(l, 2)
        if st < 4:
            self._ai[l] = NA_PER_LAYER
            return
        self.make_xb()
        self.ple(l, ti)
        self.ln(l, 3)

    def ple(self, l, ti):
        t0 = ti * TT
        for k in range(2):
            pin = self.tmp32()
            self.S.dma("sp", pin.t[:], self.pT[l, k * 128:(k + 1) * 128, t0:t0 + TT], writes=[pin])
            self.copy_any(self.pbt[k][:], pin[:])
        slabs = [self.slabA(l, ("pg", i)) for i in range(4)]
        wp = self.slabA(l, ("pp", 0))
        for m in range(8):
            pg = self.bank()
            pp = self.bank()
            sl = slabs[m // 2]
            for k in range(8):
                self.mm(pg[:], sl[:, k * 256 + (m % 2) * 128: k * 256 + (m % 2) * 128 + 128], self.xb[k][:], k == 0, k == 7)
            for k in range(2):
                self.mm(pp[:], wp[:, k * 1024 + m * 128: k * 1024 + m * 128 + 128], self.pbt[k][:], k == 0, k == 1)
            t = self.tmp32()
            self.act(t[:], pg[:], AF.Sigmoid)
            self.tt("dve", t[:], t[:], pp[:], ALU.mult)
            self.stt("dve", self.z[m][:], self.x[m][:], ALPHA, t[:], ALU.mult, ALU.add)

    def alloc_mixer(self):
        S = self.S
        self.lnmean = S.sbuf([128, TT], F32, "lnmean")
        self.lnrstd = S.sbuf([128, TT], F32, "lnrstd")
        self.raw = [S.sbuf([128, 3 + TT], F32, "raw") for _ in range(3)]
        self._rawi = 0
        self.rw_tw = S.sbuf([128, TT], BF16, "rw_tw")
        self.rw_al = S.sbuf([128, TT], BF16, "rw_al")
        self.rw_gs = S.sbuf([128, 2, TT], BF16, "rw_gs")
        self.rw_kk = S.sbuf([128, TT], BF16, "rw_kk")
        self.rw_b = S.sbuf([128, TT], BF16, "rw_b")
        self.vt16 = S.sbuf([128, 4, 256], BF16, "vt16")
        self.kt16 = S.sbuf([128, 4, 128], BF16, "kt16")
        self.muvj = S.sbuf([128, 128], F32, "muvj")
        self.vpad = S.sbuf([128, 4, 2, 128], BF16, "vpad")
        self.memset("dve", self.vpad[:], 0.0)
        self.kpad = [S.sbuf([128, 128], BF16, "kpad") for _ in range(2)]
        for u_ in self.kpad:
            self.memset("dve", u_[:], 0.0)
        self.upad = [S.sbuf([128, 128], BF16, "upad") for _ in range(2)]
        for u_ in self.upad:
            self.memset("dve", u_[:], 0.0)
        self.gl_lo = S.sbuf([128, TT], BF16, "gl_lo")
        self.gl_l = S.sbuf([128, 4, 256], F32, "gl_l")
        self.gl_kt = S.sbuf([128, 4, 256], F32, "gl_kt")
        self.gd_ab = S.sbuf([128, 4, 8], F32, "gd_ab")
        self.gd_sc = S.sbuf([128, 4, 24], F32, "gd_sc")
        self.s16 = [S.sbuf([128, 128], BF16, "s16") for _ in range(26)]
        self.l16 = [S.sbuf([128, 128], BF16, "l16") for _ in range(20)]
        self.s32 = [S.sbuf([128, 128], F32, "s32") for _ in range(12)]
        self._s16i = self._l16i = self._s32i = 0
        self.s16w = [S.sbuf([128, 256], BF16, "s16w") for _ in range(4)]
        self._s16wi = 0

    def b16(self):
        self._s16i = (self._s16i + 1) % len(self.s16)
        return self.s16[self._s16i]

    def L16(self):
        self._l16i = (self._l16i + 1) % len(self.l16)
        return self.l16[self._l16i]

    def b32(self):
        self._s32i = (self._s32i + 1) % len(self.s32)
        return self.s32[self._s32i]

    def b16w(self):
        self._s16wi = (self._s16wi + 1) % len(self.s16w)
        return self.s16w[self._s16wi]

    def fm_chunk_psum(self, slab, cc):
        p = self.bank()
        for k in range(8):
            self.mm(p[:], slab[:, k * 256 + cc * 128: k * 256 + cc * 128 + 128], self.xb[k][:], k == 0, k == 7)
        return p

    def raw_tile(self, l, ci, p):
        self._rawi = (self._rawi + 1) % len(self.raw)
        r = self.raw[self._rawi]
        self.cp("dve", r[:, 0:3], self.hcar[:, l, ci, :])
        self.cp("act", r[:, 3:3 + TT], p[:])
        self.cp("dve", self.hcar[:, l, ci, :], r[:, TT:TT + 3])
        return r

    def shift_chunk(self, l, ci, p, out):
        r = self.raw_tile(l, ci, p)
        t = self.tmp32()
        self.tt("dve", t[:], r[:, 2:2 + TT], r[:, 3:3 + TT], ALU.subtract)
        mu = self.pv[:, l, PV_MU + ci: PV_MU + ci + 1]
        self.stt("dve", out, t[:], mu, r[:, 3:3 + TT], ALU.mult, ALU.add)

    def mixers(self, l, ti):
        for k in range(8):
            self.tt("dve", self.dxb[k][:, 1:TT], self.xb[k][:, 0:TT - 1], self.xb[k][:, 1:TT], ALU.subtract)
            self.tt("dve", self.dxb[k][:, 0:1], self.xcar[:, l, k:k + 1], self.xb[k][:, 0:1], ALU.subtract)
            self.cp("dve", self.xcar[:, l, k:k + 1], self.xb[k][:, TT - 1:TT])
        for name, fn, b in (("r", self.rwkv, 0), ("g", self.gla, 1), ("d", self.gdn, 2)):
            if name in self.mixsel:
                fn(l)
            else:
                for j in range(4):
                    self.memset("dve", self.br[b][j][:], 0.0)
        if self.mixsel != "rgd":
            pass
        self.merge(l)

    def tri_inverse(self, X, P):
        ident_f = self.cf[:, CI_ID, :]
        Z = self.b16()
        self.tt("dve", Z[:], X, ident_f, ALU.add)
        Xc, Pc = X, P
        for lvl in range(6):
            last = lvl == 5
            pq = self.quarter()
            self.mm(pq[:], Xc, Pc)
            Pn = self.b16()
            self.copy_any(Pn[:], pq[:])
            if not last:
                xq = self.quarter()
                self.mm(xq[:], Pc, Xc)
                Xn = self.b16()
                self.copy_any(Xn[:], xq[:])
            zq = self.quarter()
            self.mm(zq[:], Pn[:], Z[:])
            Zn = self.b16()
            self.tt("dve", Zn[:], zq[:], Z[:], ALU.add)
            Z = Zn
            Pc = Pn[:]
            if not last:
                Xc = Xn[:]
        return Z

    def rwkv(self, l):
        cf, cb, sw = self.cf, self.cb, self.sw
        pvl = lambda c0, j: self.pv[:, l, c0 + j: c0 + j + 1]
        blk = cf[:, CI_BLK, :]
        idb = cb[:, CB_ID, :]
        KD = -float(np.exp(-0.5))
        t = self.tmp32()
        self.shift_chunk(l, 0, self.fm_chunk(l, 0), t[:])
        self.act(self.rw_tw[0:64, :], t[0:64, :], AF.Tanh)
        self.memset("dve", self.rw_tw[64:65, :], 1.0)
        self.cp("dve", self.rw_al[:], t[:])
        for jj in range(2):
            t = self.tmp32()
            self.shift_chunk(l, 1 + jj, self.fm_chunk(l, 1 + jj), t[:])
            self.act(self.rw_gs[:, jj, :], t[:], AF.Sigmoid)
        r, k, v, y = self.pool[0], self.pool[1], self.pool[2], self.pool[3]
        tmv = None
        for j in range(4):
            self.shift_chunk(l, 4 + 3 * j, self.fm_chunk(l, 4 + 3 * j), r[:])
            self.shift_chunk(l, 5 + 3 * j, self.fm_chunk(l, 5 + 3 * j), k[:])
            self.shift_chunk(l, 6 + 3 * j, self.fm_chunk(l, 6 + 3 * j), v[:])
            if j % 2 == 0:
                tmv = self.slabA(l, ("tm", j // 2))
            self.S.dma("sp", self.muvj.t[:], self.muvd[l, :, j * 128:(j + 1) * 128], writes=[self.muvj])
            for s in range(4):
                ts_ = slice(s * C, (s + 1) * C)
                q1 = self.quarter()
                q2 = self.quarter()
                c0 = (j % 2) * 128
                for kk_ in range(8):
                    self.mm(q1[:], self.xb[kk_][:, ts_], tmv[:, kk_ * 256 + c0: kk_ * 256 + c0 + 128], kk_ == 0, kk_ == 7)
                for kk_ in range(8):
                    self.mm(q2[:], self.dxb[kk_][:, ts_], tmv[:, kk_ * 256 + c0: kk_ * 256 + c0 + 128], kk_ == 0, kk_ == 7)
                tq_ = self.b32()
                self.tt("dve", tq_[:], q2[:], self.muvj[:], ALU.mult)
                self.tt("dve", self.vpad[:, s, 0, 0:64], tq_[:, 0:64], q1[:, 0:64], ALU.add)
                self.tt("dve", self.vpad[:, s, 1, 64:128], tq_[:, 64:128], q1[:, 64:128], ALU.add)
            p = self.bank()
            self.mm(p[:], sw[:, l, 512 + j * 128: 512 + (j + 1) * 128], self.rw_al[:])
            a = self.tmp32()
            self.act(a[:], p[:], AF.Sigmoid, bias=pvl(PV_A0, j))
            t1 = self.tmp32()
            self.ts("dve", t1[:], k[:], pvl(PV_KK, j), None, ALU.mult)
            sq = self.tmp32()
            self.act(sq[:], t1[:], AF.Square)
            p = self.bank()
            self.mm(p[:], blk, sq[:])
            rn = sq
            self.act(rn[:], p[:], AF.Sqrt, bias=self.epsb[:, 1:2])
            self.recip(rn[:], rn[:])
            self.tt("dve", self.rw_kk[:], t1[:], rn[:], ALU.mult)
            self.tt("dve", self.rw_b[:], self.rw_kk[:], a[:], ALU.mult)
            self.ts("dve", a[:], a[:], -1.0, pvl(PV_KA, j), ALU.add, ALU.mult)
            self.stt("dve", k[:], a[:], 1.0, k[:], ALU.add, ALU.mult)
            t3 = self.tmp32()
            self.stt("dve", t3[:], r[:], pvl(PV_RK, j), k[:], ALU.mult, ALU.mult)
            p = self.bank()
            self.mm(p[:], blk, t3[:])
            self.tt("dve", v[:], p[:], v[:], ALU.mult)
            for s in range(4):
                ts_ = slice(s * C, (s + 1) * C)
                qz = self.quarter()
                self.mm(qz[:], self.rw_tw[0:65, ts_], sw[0:65, l, j * 128:(j + 1) * 128])
                sg = self.b32()
                self.act(sg[:], qz[:], AF.Sigmoid)
                qc = self.quarter()
                self.mm(qc[:], sg[:], cf[:, CI_TLE, :])
                qp = self.quarter()
                self.mm(qp[:], sg[:], cf[:, CI_TLT, :])
                E1 = self.b32()
                self.act(E1[:], qc[:], AF.Exp, scale=KD)
                Ei = self.b32()
                self.act(Ei[:], qc[:], AF.Exp, scale=-KD)
                E0 = self.b32()
                self.act(E0[:], qp[:], AF.Exp, scale=KD)
                ARt = self.b16w()
                self.stt("dve", ARt[:, 0:128], self.rw_kk[:, ts_], -1.0, E0[:], ALU.mult, ALU.mult)
                self.tt("dve", ARt[:, 128:256], r[:, ts_], E1[:], ALU.mult)
                Bt = self.L16()
                self.tt("dve", Bt[:], self.rw_b[:, ts_], Ei[:], ALU.mult)
                Kt = self.L16()
                self.tt("dve", Kt[:], k[:, ts_], Ei[:], ALU.mult)
                tq = self.tbq()
                self.tr(tq[:], Bt[:], idb)
                Btm = self.L16()
                self.copy_any(Btm[:], tq[:])
                tq = self.tbq()
                self.tr(tq[:], Kt[:], idb)
                Ktm = self.L16()
                self.copy_any(Ktm[:], tq[:])
                hm = (cf[:, CI_BLK, 0:1], cf[:, CI_BLK, 64:65])
                Sblk = self.Srwb[:, l, j, :]
                qy = self.QF[0]
                self.mm(qy[:], Sblk, ARt[:, 128:256], True, False)
                for hh in range(2):
                    hs = slice(hh * 64, hh * 64 + 64)
                    Bth = self.L16()
                    self.ts("dve", Bth[:], Bt[:], hm[hh], None, ALU.mult)
                    Kth = self.L16()
                    self.ts("dve", Kth[:], Kt[:], hm[hh], None, ALU.mult)
                    Ath = self.L16()
                    self.ts("dve", Ath[:], ARt[:, 0:128], hm[hh], None, ALU.mult)
                    pb_ = self.bank()
                    self.mm(pb_[:, 0:256], Bth[:], ARt[:])
                    self.mm(pb_[:, 256:512], Kth[:], ARt[:])
                    qP = self.quarter()
                    self.mm(qP[:], Ath[:], Bt[:])
                    X = self.L16()
                    self.tt("dve", X[:], pb_[:, 0:128], cf[:, CI_TLT, :], ALU.mult)
                    NBT = self.L16()
                    self.tt("dve", NBT[:], pb_[:, 128:256], cf[:, CI_TLE, :], ALU.mult)
                    MKT = self.L16()
                    self.tt("dve", MKT[:], pb_[:, 256:384], cf[:, CI_TLT, :], ALU.mult)
                    NKT = self.L16()
                    self.tt("dve", NKT[:], pb_[:, 384:512], cf[:, CI_TLE, :], ALU.mult)
                    P = self.L16()
                    self.tt("dve", P[:], qP[:], cf[:, CI_TGT, :], ALU.mult)
                    Z = self.tri_inverse(X[:], P[:])
                    V_ = self.vpad[:, s, hh, hs]
                    qr = self.quarter()
                    self.mm(qr[:, 0:64], ARt[:, 0:128], self.Srwb[:, l, j, hs], True, False)
                    self.mm(qr[:, 0:64], MKT[:], V_, False, True)
                    RHS = self.L16()
                    self.copy_any(RHS[:, 0:64], qr[:, 0:64])
                    qu = self.quarter()
                    self.mm(qu[:, 0:64], Z[:], RHS[:, 0:64])
                    self.copy_any(self.upad[hh][:, hs], qu[:, 0:64])
                    self.mm(qy[:], self.upad[hh][:], NBT[:], False, False)
                    self.mm(qy[:], self.vpad[:, s, hh, :], NKT[:], False, hh == 1)
                self.copy_any(y[:, ts_], qy[:])
                qs = self.QF[1]
                self.mm(qs[:], Btm[:], self.upad[0][:], True, False)
                self.mm(qs[:], Btm[:], self.upad[1][:], False, False)
                self.mm(qs[:], Ktm[:], self.vpad[:, s, 0, :], False, False)
                self.mm(qs[:], Ktm[:], self.vpad[:, s, 1, :], False, True)
                tmpS = self.b32()
                self.tt("dve", tmpS[:], qs[:], cf[:, CI_BLK, :], ALU.mult)
                Sf = self.Srw[:, l, j, :]
                self.tt("dve", Sf, tmpS[:], Sf, ALU.add)
                self.ts("dve", Sf, Sf, E1[:, 127:128], None, ALU.mult)
                self.cp("act", self.Srwb[:, l, j, :], Sf)
            pg_ = self.bank()
            self.mm(pg_[:], sw[:, l, 1024 + j * 128: 1024 + (j + 1) * 128], self.rw_gs[:, 0, :], True, False)
            self.mm(pg_[:], sw[0:32, l, 1536 + j * 128: 1536 + (j + 1) * 128], self.rw_gs[0:32, 1, :], False, True)
            p = self.bank()
            self.mm(p[:], blk, y[:])
            d = self.tmp32()
            self.stt("dve", d[:], p[:], -1.0 / 64, y[:], ALU.mult, ALU.add)
            sq = self.tmp32()
            self.act(sq[:], d[:], AF.Square)
            p2 = self.bank()
            self.mm(p2[:], blk, sq[:])
            rs = sq
            self.act(rs[:], p2[:], AF.Sqrt, bias=self.epsb[:, 2:3], scale=1.0 / 64)
            self.recip(rs[:], rs[:])
            self.tt("dve", d[:], d[:], rs[:], ALU.mult)
            self.act(d[:], d[:], AF.Identity, bias=pvl(PV_GNB, j), scale=pvl(PV_GNG, j))
            self.tt("dve", d[:], d[:], v[:], ALU.add)
            self.tt("dve", self.br[0][j][:], d[:], pg_[:], ALU.mult)
            self.dump("o_rw", d[:], [128, TT])

    def gla(self, l):
        cf, cb, sw = self.cf, self.cb, self.sw
        p = self.fm_chunk(l, 16)
        self.cp("act", self.gl_lo[0:32, :], p[0:32, :])
        self.memset("dve", self.gl_lo[32:33, :], 1.0)
        if self.cut(0, 1):
            return
        tmk = self.slabA(l, ("tm", 2))
        for s in range(4):
            ts_ = slice(s * C, (s + 1) * C)
            p = self.bank()
            self.mm(p[:, 0:256], self.gl_lo[0:33, ts_], sw[0:33, l, 2048:2304])
            e = self.tmp32()
            self.act(e[:, 0:256], p[:, 0:256], AF.Exp, scale=-1.0)
            self.act(self.gl_l[:, s, :], e[:, 0:256], AF.Ln, bias=self.epsb[:, 3:4])
            pk = self.bank()
            for k in range(8):
                self.mm(pk[:, 0:256], self.xb[k][:, ts_], tmk[:, k * 256:(k + 1) * 256], k == 0, k == 7)
            self.copy_any(self.gl_kt[:, s, :], pk[:, 0:256])
        if self.cut(1, 1):
            return
        q, k_, g0, g1, o0, o1 = self.pool[0:6]
        for j in range(2):
            self.copy_any(q[:], self.fm_chunk(l, 18 + 4 * j)[:])
            self.copy_any(k_[:], self.fm_chunk(l, 19 + 4 * j)[:])
            self.act(g0[:], self.fm_chunk(l, 20 + 4 * j)[:], AF.Silu)
            self.act(g1[:], self.fm_chunk(l, 21 + 4 * j)[:], AF.Silu)
            tmv = self.slabA(l, ("tm", 3 + j))
            for s in range(4):
                ts_ = slice(s * C, (s + 1) * C)
                pv_ = self.bank()
                for k in range(8):
                    self.mm(pv_[:, 0:256], self.xb[k][:, ts_], tmv[:, k * 256:(k + 1) * 256], k == 0, k == 7)
                self.copy_any(self.vt16[:, s, :], pv_[:, 0:256])
            if self.cut(2, 1):
                return
            for s in range(4):
                ts_ = slice(s * C, (s + 1) * C)
                lj = self.gl_l[:, s, j * 128:(j + 1) * 128]
                qk = self.quarter()
                self.mm(qk[:], cf[:, CI_TGT, :], lj)
                ek = self.b32()
                self.act(ek[:], qk[:], AF.Exp, scale=-1.0 / 16)
                for hh in range(2):
                    c0_, c1_ = hh * 64, hh * 64 + 64
                    self.tt("dve", self.kpad[hh][:, c0_:c1_], self.gl_kt[:, s, j * 128 + c0_: j * 128 + c1_],
                            ek[:, c0_:c1_], ALU.mult)
                qsf = self.QF[2]
                qb = self.quarter()
                self.mm(qb[:], lj, cf[:, CI_TLE, :])
                Eb = self.b32()
                self.act(Eb[:], qb[:], AF.Exp, scale=-1.0 / 16)
                Ein = self.b32()
                self.act(Ein[:], qb[:], AF.Exp, scale=1.0 / 16)
                qt = self.L16()
                self.stt("dve", qt[:], q[:, ts_], 0.125, Eb[:], ALU.mult, ALU.mult)
                hm = (cf[:, CI_BLK, 0:1], cf[:, CI_BLK, 64:65])
                for hh in range(2):
                    hs = slice(hh * 64, hh * 64 + 64)
                    if hh == 1 and self.cut(6, 1):
                        return
                    kth = self.L16()
                    self.stt("dve", kth[:], k_[:, ts_], hm[hh], Ein[:], ALU.mult, ALU.mult)
                    qth = self.L16()
                    self.ts("dve", qth[:], qt[:], hm[hh], None, ALU.mult)
                    if hh == 1 and self.cut(7, 1):
                        return
                    qa = self.quarter()
                    self.mm(qa[:], kth[:], qt[:])
                    attT = self.b16()
                    self.tt("dve", attT[:], qa[:], cf[:, CI_TLE, :], ALU.mult)
                    if hh == 1 and self.cut(8, 1):
                        return
                    V_ = self.vt16[:, s, hh * 128:(hh + 1) * 128]
                    qo = self.quarter()
                    self.mm(qo[:], V_, attT[:], True, False)
                    self.mm(qo[:], self.Sglb[:, l, j, :], qth[:], False, True)
                    self.copy_any((o0, o1)[hh][:, ts_], qo[:])
                    if hh == 1 and self.cut(9, 1):
                        return
                    self.mm(qsf[:], self.kpad[hh][:], V_, hh == 0, hh == 1)
                Sf = self.Sgl[:, l, j, :]
                self.stt("dve", Sf, Sf, Eb[:, 127:128], qsf[:], ALU.mult, ALU.add)
                self.cp("act", self.Sglb[:, l, j, :], Sf)
            self.head_rms(l, o0, g0, PV_GLAN, self.br[1][2 * j])
            self.head_rms(l, o1, g1, PV_GLAN, self.br[1][2 * j + 1])

    def cut(self, n, b):
        import os
        c = int(os.environ.get("GCUT", "99"))
        if n >= c:
            for j in range(4):
                self.memset("dve", self.br[b][j][:], 0.0)
            return True
        return False

    def head_rms(self, l, o, gate, pvcol, dst):
        sq = self.tmp32()
        self.act(sq[:], o[:], AF.Square)
        p = self.bank()
        self.mm(p[:], self.cf[:, CI_V128, :], sq[:])
        rs = sq
        self.act(rs[:], p[:], AF.Sqrt, bias=self.epsb[:, 1:2])
        self.recip(rs[:], rs[:])
        t = self.tmp32()
        self.stt("dve", t[:], o[:], self.pv[:, l, pvcol:pvcol + 1], rs[:], ALU.mult, ALU.mult)
        self.tt("dve", dst[:], t[:], gate[:], ALU.mult)

    def gdn(self, l):
        cf, cb = self.cf, self.cb
        ones_f = cf[:, CI_ONE, :]
        idb = cb[:, CB_ID, :]
        tmab = self.slabA(l, ("tm", 5))
        sc = self.gd_sc
        for s in range(4):
            ts_ = slice(s * C, (s + 1) * C)
            pa = self.quarter()
            for k in range(8):
                self.mm(pa[:, 0:8], self.xb[k][:, ts_], tmab[:, k * 256:k * 256 + 8], k == 0, k == 7)
            self.cp("dve", self.gd_ab[:, s, :], pa[:, 0:8])
            self.tt("dve", sc[:, s, 0:4], self.gd_ab[:, s, 0:4], self.pv[:, l, PV_DTB:PV_DTB + 4], ALU.add)
            self.act(sc[:, s, 0:4], sc[:, s, 0:4], AF.Exp)
            self.act(sc[:, s, 0:4], sc[:, s, 0:4], AF.Ln, bias=self.epsb[:, 3:4])
            self.tt("dve", sc[:, s, 0:4], sc[:, s, 0:4], self.nega[:, l, :], ALU.mult)
            self.act(sc[:, s, 4:8], self.gd_ab[:, s, 4:8], AF.Sigmoid)
            qg = self.quarter()
            self.mm(qg[:, 0:4], cf[:, CI_TLE, :], sc[:, s, 0:4])
            self.mm(qg[:, 4:8], cf[:, CI_TGT, :], sc[:, s, 0:4])
            self.act(sc[:, s, 8:16], qg[:, 0:8], AF.Exp)
            self.tt("dve", sc[:, s, 16:20], sc[:, s, 4:8], sc[:, s, 8:12], ALU.mult)
            self.ts("dve", sc[:, s, 20:24], sc[:, s, 4:8], -1.0, None, ALU.mult)
        q, k_, v, gate, o = self.pool[0:5]
        for h in range(4):
            for which, dst in ((0, q), (1, k_), (2, v)):
                cidx = which * 4 + h
                r = self.raw_tile(l, 16 + cidx, self.fm_chunk(l, 26 + 4 * h + which))
                t = self.tmp32()
                cw = lambda tap: self.pv[:, l, PV_CONV + tap * 12 + cidx: PV_CONV + tap * 12 + cidx + 1]
                self.ts("dve", t[:], r[:, 0:TT], cw(0), None, ALU.mult)
                for tap in (1, 2, 3):
                    self.stt("dve", t[:], r[:, tap:tap + TT], cw(tap), t[:], ALU.mult, ALU.add)
                self.act(dst[:], t[:], AF.Silu)
            self.act(gate[:], self.fm_chunk(l, 29 + 4 * h)[:], AF.Silu)
            for src, scale in ((q, 128.0 ** -0.5), (k_, 1.0)):
                sq = self.tmp32()
                self.act(sq[:], src[:], AF.Square)
                p = self.bank()
                self.mm(p[:], ones_f, sq[:])
                rs = sq
                self.act(rs[:], p[:], AF.Sqrt, bias=self.epsb[:, 1:2])
                self.recip(rs[:], rs[:])
                self.stt("dve", src[:], src[:], scale, rs[:], ALU.mult, ALU.mult)
            for src, dstt in ((k_, self.kt16), (v, self.vt16)):
                p = self.bank()
                for s in range(4):
                    self.tr(p[:, s * 128:(s + 1) * 128], src[:, s * C:(s + 1) * C], cf[:, CI_ID, :])
                for s in range(4):
                    self.copy_any(dstt[:, s, 0:128], p[:, s * 128:(s + 1) * 128])
            for s in range(4):
                ts_ = slice(s * C, (s + 1) * C)
                kn_tm = self.kt16[:, s, :]
                v_tm = self.vt16[:, s, 0:128]
                knT = self.L16()
                self.copy_any(knT[:], k_[:, ts_])
                GT = self.b32()
                self.ts("dve", GT[:], cf[:, CI_TLE, :], sc[:, s, h:h + 1], None, ALU.mult)
                GB = self.b32()
                self.ts("dve", GB[:], cf[:, CI_ONE, :], sc[:, s, h:h + 1], None, ALU.mult)
                qd = self.quarter()
                self.mm(qd[:], GT[:], cf[:, CI_ONE, :], True, False)
                self.mm(qd[:], GB[:], cf[:, CI_NTLE, :], False, False)
                self.mm(qd[:], idb, cb[:, CB_MBSL, :], False, True)
                Dsl = self.b32()
                self.act(Dsl[:], qd[:], AF.Exp)
                qe = self.quarter()
                self.mm(qe[:], GB[:], cf[:, CI_TLE, :], True, False)
                self.mm(qe[:], GT[:], cf[:, CI_NEGONE, :], False, False)
                self.mm(qe[:], idb, cb[:, CB_MBIU, :], False, True)
                Diu = self.b32()
                self.act(Diu[:], qe[:], AF.Exp)
                qbc = self.quarter()
                self.mm(qbc[:], GB[:], cf[:, CI_TLE, :])
                bcE = self.b32()
                self.act(bcE[:], qbc[:], AF.Exp)
                qG = self.quarter()
                self.mm(qG[:], knT[:], knT[:])
                P = self.L16()
                self.stt("dve", P[:], qG[:], sc[:, s, 20 + h:21 + h], Dsl[:], ALU.mult, ALU.mult)
                tq = self.tbq()
                self.tr(tq[:], P[:], idb)
                X = self.L16()
                self.copy_any(X[:], tq[:])
                Z = self.tri_inverse(X[:], P[:])
                qnT = self.L16()
                self.copy_any(qnT[:], q[:, ts_])
                qa = self.quarter()
                self.mm(qa[:], knT[:], qnT[:])
                attT = self.L16()
                self.tt("dve", attT[:], qa[:], Diu[:], ALU.mult)
                qgT = self.L16()
                self.tt("dve", qgT[:], q[:, ts_], bcE[:], ALU.mult)
                kbg = self.L16()
                self.ts("dve", kbg[:], kn_tm, sc[:, s, 16 + h:17 + h], None, ALU.mult)
                kdec = self.L16()
                self.ts("dve", kdec[:], kn_tm, sc[:, s, 12 + h:13 + h], None, ALU.mult)
                vb = self.L16()
                self.ts("dve", vb[:], v_tm, sc[:, s, 4 + h:5 + h], None, ALU.mult)
                qw = self.quarter()
                self.mm(qw[:], kbg[:], Z[:])
                nwT = self.L16()
                self.ts("dve", nwT[:], qw[:], -1.0, None, ALU.mult)
                Sb = self.Sgdb[:, l, h, :]
                qv = self.quarter()
                self.mm(qv[:], Z[:], vb[:], True, False)
                self.mm(qv[:], nwT[:], Sb, False, True)
                vnew = self.L16()
                self.copy_any(vnew[:], qv[:])
                qo = self.quarter()
                self.mm(qo[:], Sb, qgT[:], True, False)
                self.mm(qo[:], vnew[:], attT[:], False, True)
                self.copy_any(o[:, ts_], qo[:])
                qs = self.quarter()
                self.mm(qs[:], kdec[:], vnew[:])
                Sf = self.Sgd[:, l, h, :]
                self.stt("dve", Sf, Sf, bcE[:, 127:128], qs[:], ALU.mult, ALU.add)
                self.cp("act", self.Sgdb[:, l, h, :], Sf)
            self.head_rms(l, o, gate, PV_GDNN, self.br[2][h])

    def merge(self, l):
        macc = self.z
        for b in range(3):
            for j in range(4):
                self.dump("br", self.br[b][j][:], [128, TT])
        for b in range(3):
            for half in range(2):
                wb = self.slabA(l, ("br", b, half))
                for mm_ in range(2):
                    wg = self.slabA(l, ("gate", b * 4 + half * 2 + mm_))
                    for cc in range(2):
                        m = half * 4 + mm_ * 2 + cc
                        pg = self.fm_chunk_psum(wg, cc)
                        pp = self.bank()
                        for k in range(4):
                            c0 = k * 512 + (mm_ * 2 + cc) * 128
                            self.mm(pp[:], wb[:, c0:c0 + 128], self.br[b][k][:], k == 0, k == 3)
                        t = self.tmp32()
                        self.act(t[:], pg[:], AF.Sigmoid)
                        if b == 0:
                            self.tt("dve", macc[m][:], t[:], pp[:], ALU.mult)
                        else:
                            self.tt("dve", t[:], t[:], pp[:], ALU.mult)
                            if b == 1:
                                self.tt("dve", macc[m][:], macc[m][:], t[:], ALU.add)
                            else:
                                self.tt("dve", self.merged[m][:], macc[m][:], t[:], ALU.add)
        wos = [self.slabA(l, ("wo", i)) for i in range(4)]
        for m in range(8):
            p = self.bank()
            sl = wos[m // 2]
            for k in range(8):
                self.mm(p[:], sl[:, k * 256 + (m % 2) * 128: k * 256 + (m % 2) * 128 + 128], self.merged[k][:], k == 0, k == 7)
            self.stt("dve", self.z[m][:], self.x[m][:], ALPHA, p[:], ALU.mult, ALU.add)


_CACHE = {}


def prep_weights(inputs, plan):
    inp = {k: np.asarray(v, np.float32) for k, v in inputs.items() if k not in ("x", "p")}
    per = [prep_layer(inp, l, plan) for l in range(2)]
    cf, cb = build_consts()
    return dict(wA=np.ascontiguousarray(np.stack([p[0] for p in per], 0)),
                wB=np.ascontiguousarray(np.stack([p[1] for p in per], 0)),
                pv=np.ascontiguousarray(np.stack([p[2] for p in per], 0)),
                sw=np.ascontiguousarray(np.stack([p[3] for p in per], 0)),
                muv=np.ascontiguousarray(np.stack([p[4] for p in per], 0)),
                cf=cf, cb=cb)


def run(inputs, T, layers=(0, 1), n_cores=8, debug=(), stages=4, mixsel="rgd"):
    x = np.asarray(inputs["x"], np.float32)
    p = np.asarray(inputs["p"], np.float32)
    B = x.shape[0]
    key = (T, tuple(layers), tuple(debug), stages, mixsel)
    if key not in _CACHE:
        _CACHE[key] = Prog(T, list(layers), debug, stages, mixsel)
    prog = _CACHE[key]
    w = prep_weights(inputs, prog.plan)
    in_maps = []
    for c in range(n_cores):
        b = c % B
        m = dict(w)
        m["xT"] = np.ascontiguousarray(x[b, :T].T)
        m["pT"] = np.ascontiguousarray(p[:, b, :T].transpose(0, 2, 1))
        in_maps.append(m)
    res = run_bass_kernel_spmd(prog.nc, in_maps, core_ids=list(range(n_cores)))
    out = np.stack([np.ascontiguousarray(res.results[b]["outT"].T) for b in range(B)], 0)
    return out.astype(np.float32), res, prog


def kernel(**inputs):
    out, _, _ = run(inputs, T=4096)
    return out
```

```python
import contextlib
import numpy as np
import concourse.bass as bass
import concourse.mybir as mybir
from concourse.bass_utils import run_bass_kernel_spmd

F32 = mybir.dt.float32
BF16 = mybir.dt.bfloat16
AF = mybir.ActivationFunctionType
ALU = mybir.AluOpType
AX = mybir.AxisListType

D = 1024
DFF = 2816
TT = 512
C = 128
ALPHA = 4.0 ** 0.25
LN_EPS = 1e-5
SAME_ENGINE_SYNC = True
NSLOT = 5
SLOT = 2816


class V:
    __slots__ = ("b", "ap")

    def __init__(self, b, ap):
        self.b = b
        self.ap = ap


class Buf:
    __slots__ = ("t", "name", "last_write", "reads", "dsem", "dcnt", "c0", "owner", "excl")

    def __init__(self, t, name, c0=None, owner=None):
        self.owner = owner
        self.excl = False
        self.t = t
        self.name = name
        self.last_write = None
        self.reads = []
        self.dsem = None
        self.dcnt = 0
        self.c0 = c0

    def __getitem__(self, idx):
        if self.c0 is None:
            return V(self, self.t[idx])
        if not isinstance(idx, tuple):
            idx = (idx, slice(None))
        r, c = idx
        a = 0 if c.start is None else c.start
        b = 128 if c.stop is None else c.stop
        return V(self.owner or self, self.t[r, self.c0 + a: self.c0 + b])


class Sched:
    ENGS = ("pe", "act", "dve", "pool", "sp")

    def __init__(self, nc):
        self.nc = nc
        self.stack = contextlib.ExitStack()
        self.q = {e: [] for e in self.ENGS}
        self.sem = {}
        self.cnt = {e: 0 for e in self.ENGS}
        self.seen = {e: {} for e in self.ENGS}
        for e in self.ENGS:
            self.sem[e] = self.stack.enter_context(nc.semaphore("s_" + e))
        self.nbuf = 0
        self.ninst = 0
        self.rr = 0

    def sbuf(self, shape, dtype=F32, name=None):
        self.nbuf += 1
        name = (name or "b") + f"_{self.nbuf}"
        t = self.stack.enter_context(self.nc.sbuf_tensor(name, list(shape), dtype))
        return Buf(t, name)

    def psum(self, shape, dtype=F32, name=None):
        self.nbuf += 1
        name = (name or "p") + f"_{self.nbuf}"
        t = self.stack.enter_context(self.nc.psum_tensor(name, list(shape), dtype))
        b = Buf(t, name)
        b.excl = True
        return b

    def dsem_for(self, buf):
        if buf.dsem is None:
            buf.dsem = self.stack.enter_context(self.nc.semaphore("d_" + buf.name))
        return buf.dsem

    def _deps(self, eng, reads, writes):
        waits = {}

        def add(rec):
            if rec is None:
                return
            s, v, owner = rec
            if owner == eng and (eng == "pe" or not SAME_ENGINE_SYNC):
                return
            k = id(s)
            if k not in waits or waits[k][1] < v:
                waits[k] = (s, v)

        for b in reads:
            add(b.last_write)
            if b.excl:
                for r in b.reads:
                    if r[2] != eng:
                        add(r)
        for b in writes:
            add(b.last_write)
            for r in b.reads:
                add(r)
        out = []
        seen = self.seen[eng]
        for k, (s, v) in waits.items():
            if seen.get(k, 0) >= v:
                continue
            seen[k] = v
            out.append((s, v))
        return out

    def op(self, eng, fn, reads=(), writes=()):
        waits = self._deps(eng, reads, writes)
        self.cnt[eng] += 1
        v = self.cnt[eng]
        s = self.sem[eng]
        self.q[eng].append((fn, waits, (s, 1)))
        rec = (s, v, eng)
        for b in reads:
            if len(b.reads) > 24:
                b.reads = b.reads[-24:] if False else b.reads
            b.reads.append(rec)
        for b in writes:
            b.last_write = rec
            b.reads = []
        self.ninst += 1

    def dma(self, eng, out_ap, in_ap, reads=(), writes=(), sembuf=None):
        sembuf = sembuf or (writes[0] if writes else reads[0])
        ds = self.dsem_for(sembuf)
        waits = self._deps(eng, reads, writes)
        sembuf.dcnt += 16
        v = sembuf.dcnt
        self.q[eng].append((lambda e: e.dma_start(out=out_ap, in_=in_ap), waits, (ds, 16)))
        rec = (ds, v, "dma")
        for b in reads:
            b.reads.append(rec)
        for b in writes:
            b.last_write = rec
            b.reads = []
        self.ninst += 1

    def final_wait(self, eng, bufs):
        waits = self._deps(eng, bufs, bufs)
        self.q[eng].append((None, waits, None))

    def finish(self):
        nc = self.nc
        q = self.q

        def replay(name):
            def f(e):
                for fn, waits, inc in q[name]:
                    for s, v in waits:
                        e.wait_ge(s, v)
                    if fn is None:
                        continue
                    ins = fn(e)
                    if inc is not None:
                        ins.then_inc(inc[0], inc[1])
            return f

        with nc.Block() as block:
            block.tensor(replay("pe"))
            block.scalar(replay("act"))
            block.vector(replay("dve"))
            block.gpsimd(replay("pool"))
            block.sync(replay("sp"))
        self.stack.close()


def _bufs(*xs):
    out = []
    for x in xs:
        if isinstance(x, V) and x.b not in out:
            out.append(x.b)
    return out


def _ap(x):
    return x.ap if isinstance(x, V) else x


def slabify(W, wc):
    K, N = W.shape
    kc = K // 128
    return np.ascontiguousarray(
        W.reshape(kc, 128, N // wc, wc).transpose(2, 1, 0, 3).reshape(N // wc, 128, kc * wc))


def colvec(v):
    return np.ascontiguousarray(v.reshape(-1, 128).T)


def pad_cols(W, n):
    out = np.zeros((W.shape[0], n), np.float32)
    out[:, :W.shape[1]] = W
    return out


RW0 = 0
GLA0 = 1824
GDN0 = 3376
GATE0 = 5432
NFM = 42
PV_LNG, PV_LNB = 0, 32
PV_MU = 64
PV_A0, PV_KK, PV_KA, PV_RK, PV_GNG, PV_GNB = 80, 84, 88, 92, 96, 100
PV_GLAN, PV_GDNN = 104, 105
PV_CONV = 106
PV_ALOG, PV_DTB = 154, 158
NPV = 162
CI_ID, CI_ONE, CI_NEGONE, CI_LN, CI_V128, CI_BLK, CI_TLE, CI_TLT, CI_TGT, CI_NTLE, CI_MBSL, CI_MBIU = range(12)
NCI = 12
CB_ID, CB_MBSL, CB_MBIU = 0, 1, 2


def build_consts():
    p = np.arange(128)[:, None]
    j = np.arange(128)[None, :]
    m = np.zeros((NCI, 128, 128), np.float32)
    m[CI_ID] = (p == j)
    m[CI_ONE] = 1.0
    m[CI_NEGONE] = -1.0
    m[CI_LN] = 1.0 / 1024
    m[CI_V128] = 1.0 / 128
    m[CI_BLK] = ((p // 64) == (j // 64))
    m[CI_TLE] = (p <= j)
    m[CI_TLT] = (p < j)
    m[CI_TGT] = (p > j)
    m[CI_NTLE] = -(p <= j).astype(np.float32)
    m[CI_MBSL] = np.where(p > j, 0.0, -30000.0)
    m[CI_MBIU] = np.where(p <= j, 0.0, -30000.0)
    cf = np.ascontiguousarray(m.transpose(1, 0, 2).reshape(128, NCI * 128))
    cbm = np.stack([m[CI_ID], m[CI_MBSL], m[CI_MBIU]], 0)
    cb = np.ascontiguousarray(cbm.transpose(1, 0, 2).reshape(128, 3 * 128))
    return cf, cb


def regroup_win(w_in):
    fm = np.zeros((1024, NFM * 128), np.float32)

    def put(ch, src, n):
        fm[:, ch * 128: ch * 128 + n] = w_in[:, src: src + n]
    put(0, RW0 + 1536, 128)
    put(1, RW0 + 1664, 128)
    put(2, RW0 + 1792, 32)
    for j in range(4):
        put(4 + 3 * j, RW0 + j * 128, 128)
        put(5 + 3 * j, RW0 + 512 + j * 128, 128)
        put(6 + 3 * j, RW0 + 1024 + j * 128, 128)
    put(16, GLA0 + 1024, 16)
    for j in range(2):
        put(18 + 4 * j, GLA0 + j * 128, 128)
        put(19 + 4 * j, GLA0 + 256 + j * 128, 128)
        put(20 + 4 * j, GLA0 + 1040 + (2 * j) * 128, 128)
        put(21 + 4 * j, GLA0 + 1040 + (2 * j + 1) * 128, 128)
    for h in range(4):
        put(26 + 4 * h, GDN0 + h * 128, 128)
        put(27 + 4 * h, GDN0 + 512 + h * 128, 128)
        put(28 + 4 * h, GDN0 + 1024 + h * 128, 128)
        put(29 + 4 * h, GDN0 + 1544 + h * 128, 128)
    tm = np.zeros((1024, 6 * 256), np.float32)
    tm[:, 0:512] = w_in[:, RW0 + 1024: RW0 + 1536]
    tm[:, 512:768] = w_in[:, GLA0 + 256: GLA0 + 512]
    tm[:, 768:1280] = w_in[:, GLA0 + 512: GLA0 + 1024]
    tm[:, 1280:1288] = w_in[:, GDN0 + 1536: GDN0 + 1544]
    gate = w_in[:, GATE0: GATE0 + 3072]
    return fm, tm, gate


def prep_layer(inp, l, plan):
    A = {}
    for f in range(2):
        s1 = slabify(inp["ffn_w1"][l, f], 256)
        s3 = slabify(inp["ffn_w3"][l, f], 256)
        for g in range(11):
            A[("w1", f, g)] = s1[g]
            A[("w3", f, g)] = s3[g]
    fm, tm, gate = regroup_win(inp["w_in"][l])
    for i, s in enumerate(slabify(fm, 256)):
        A[("fm", i)] = s
    for i, s in enumerate(slabify(tm, 256)):
        A[("tm", i)] = s
    for i, s in enumerate(slabify(gate, 256)):
        A[("gate", i)] = s
    for b in range(3):
        sb = slabify(inp["w_branch"][l, b], 512)
        for half in range(2):
            A[("br", b, half)] = sb[half]
    for i, s in enumerate(slabify(inp["w_o"][l], 256)):
        A[("wo", i)] = s
    for i, s in enumerate(slabify(inp["ple_w_gate"][l], 256)):
        A[("pg", i)] = s
    A[("pp", 0)] = slabify(inp["ple_w_proj"][l], 1024)[0]
    lst = [A[k] for k in plan]
    while len(lst) < NA_PER_LAYER:
        lst.append(lst[0])
    wA = np.ascontiguousarray(np.stack(lst, 0))
    wB = np.ascontiguousarray(np.concatenate([slabify(inp["ffn_w2"][l, f], 128) for f in range(2)], 0))
    pv = np.zeros((128, NPV), np.float32)
    for i in range(4):
        pv[:, PV_LNG + i * 8: PV_LNG + i * 8 + 8] = colvec(inp["ln_g"][l, i])
        pv[:, PV_LNB + i * 8: PV_LNB + i * 8 + 8] = colvec(inp["ln_b"][l, i])
    mu = inp["rw_mu"][l]
    pv[:, PV_MU + 0] = mu[1536:1664]
    pv[:, PV_MU + 1] = mu[1664:1792]
    pv[0:32, PV_MU + 2] = mu[1792:1824]
    for j in range(4):
        pv[:, PV_MU + 4 + 3 * j] = mu[j * 128:(j + 1) * 128]
        pv[:, PV_MU + 5 + 3 * j] = mu[512 + j * 128: 512 + (j + 1) * 128]
        pv[:, PV_MU + 6 + 3 * j] = mu[1024 + j * 128: 1024 + (j + 1) * 128]
    pv[:, PV_A0: PV_A0 + 4] = colvec(inp["rw_a0"][l])
    pv[:, PV_KK: PV_KK + 4] = colvec(inp["rw_k_k"][l])
    pv[:, PV_KA: PV_KA + 4] = colvec(inp["rw_k_a"][l])
    pv[:, PV_RK: PV_RK + 4] = colvec(inp["rw_r_k"][l].reshape(-1))
    pv[:, PV_GNG: PV_GNG + 4] = colvec(inp["rw_gn_g"][l])
    pv[:, PV_GNB: PV_GNB + 4] = colvec(inp["rw_gn_b"][l])
    pv[:, PV_GLAN] = inp["gla_norm_g"][l]
    pv[:, PV_GDNN] = inp["gdn_norm_g"][l]
    for tap in range(4):
        pv[:, PV_CONV + tap * 12: PV_CONV + tap * 12 + 12] = colvec(inp["gdn_conv_w"][l, tap])
    pv[:, PV_ALOG: PV_ALOG + 4] = inp["gdn_a_log"][l][None, :]
    pv[:, PV_DTB: PV_DTB + 4] = inp["gdn_dt_bias"][l][None, :]
    sw = np.zeros((128, 2304), np.float32)
    sw[0:64, 0:512] = inp["rw_w2"][l]
    sw[64, 0:512] = inp["rw_w0"][l]
    sw[64:128, 512:1024] = inp["rw_a2"][l]
    sw[0:128, 1024:1536] = inp["rw_g2"][l][0:128]
    sw[0:32, 1536:2048] = inp["rw_g2"][l][128:160]
    sw[0:16, 2048:2304] = inp["gla_gk_w2"][l]
    sw[32, 2048:2304] = inp["gla_gk_b"][l]
    muv = np.ascontiguousarray(np.broadcast_to(mu[1024:1536][None, :], (128, 512))).astype(np.float32)
    return wA, wB, pv, sw, muv


NA_PER_LAYER = 22 + 21 + 6 + 18 + 4 + 22 + 4 + 1


class Prog:
    def __init__(self, T, layers, debug=(), stages=4, mixsel="rgd"):
        self.stages = stages
        self.mixsel = mixsel
        self.T = T
        self.NT = T // TT
        self.layers = list(layers)
        self.debug = set(debug)
        nc = self.nc = bass.Bass("TRN2", target_bir_lowering=False)
        self.xT = nc.dram_tensor("xT", [D, T], F32, kind="ExternalInput").ap()
        self.pT = nc.dram_tensor("pT", [2, 256, T], F32, kind="ExternalInput").ap()
        self.wA = nc.dram_tensor("wA", [2, NA_PER_LAYER, 128, 2048], F32, kind="ExternalInput").ap()
        self.wB = nc.dram_tensor("wB", [2, 16, 128, 2816], F32, kind="ExternalInput").ap()
        self.pvd = nc.dram_tensor("pv", [2, 128, NPV], F32, kind="ExternalInput").ap()
        self.swd = nc.dram_tensor("sw", [2, 128, 2304], F32, kind="ExternalInput").ap()
        self.muvd = nc.dram_tensor("muv", [2, 128, 512], F32, kind="ExternalInput").ap()
        self.cfd = nc.dram_tensor("cf", [128, NCI * 128], F32, kind="ExternalInput").ap()
        self.cbd = nc.dram_tensor("cb", [128, 3 * 128], F32, kind="ExternalInput").ap()
        self.outT = nc.dram_tensor("outT", [D, T], F32, kind="ExternalOutput").ap()
        self.dbg_out = {}
        self.plan = []
        self.S = Sched(nc)
        self.build()

    def mm(self, out, lhsT, rhs, start=True, stop=True):
        o, a, b = out.ap, lhsT.ap, rhs.ap
        self.S.op("pe", lambda e: e.matmul(o, lhsT=a, rhs=b, start=start, stop=stop),
                  reads=_bufs(lhsT, rhs), writes=[out.b])

    def tr(self, out, in_, ident):
        o, a, b = out.ap, in_.ap, ident.ap
        self.S.op("pe", lambda e: e.transpose(o, a, b), reads=_bufs(in_, ident), writes=[out.b])

    def act(self, out, in_, func, bias=0.0, scale=1.0):
        o, a, bi, sc = out.ap, in_.ap, _ap(bias), _ap(scale)
        self.S.op("act", lambda e: e.activation(out=o, in_=a, func=func, bias=bi, scale=sc),
                  reads=_bufs(in_, bias, scale), writes=[out.b])

    def tt(self, eng, out, a, b, op):
        o, x, y = out.ap, a.ap, b.ap
        self.S.op(eng, lambda e: e.tensor_tensor(out=o, in0=x, in1=y, op=op), reads=_bufs(a, b), writes=[out.b])

    def ts(self, eng, out, a, s1, s2, op0, op1=None):
        o, x, p1, p2 = out.ap, a.ap, _ap(s1), _ap(s2)
        if op1 is None:
            self.S.op(eng, lambda e: e.tensor_scalar(out=o, in0=x, scalar1=p1, scalar2=None, op0=op0),
                      reads=_bufs(a, s1), writes=[out.b])
        else:
            self.S.op(eng, lambda e: e.tensor_scalar(out=o, in0=x, scalar1=p1, scalar2=p2, op0=op0, op1=op1),
                      reads=_bufs(a, s1, s2), writes=[out.b])

    def stt(self, eng, out, a, s, b, op0, op1):
        o, x, sc, y = out.ap, a.ap, _ap(s), b.ap
        self.S.op(eng, lambda e: e.scalar_tensor_tensor(out=o, in0=x, scalar=sc, in1=y, op0=op0, op1=op1),
                  reads=_bufs(a, s, b), writes=[out.b])

    def cp(self, eng, out, a):
        o, x = out.ap, a.ap
        if eng == "act":
            self.S.op("act", lambda e: e.copy(out=o, in_=x), reads=[a.b], writes=[out.b])
        else:
            self.S.op(eng, lambda e: e.tensor_copy(out=o, in_=x), reads=[a.b], writes=[out.b])

    def memset(self, eng, out, val):
        o = out.ap
        self.S.op(eng, lambda e: e.memset(o, val), writes=[out.b])

    def recip(self, out, a):
        o, x = out.ap, a.ap
        self.S.op("dve", lambda e: e.reciprocal(out=o, in_=x), reads=[a.b], writes=[out.b])

    def ev(self):
        self.S.rr ^= 1
        return "dve" if self.S.rr else "act"

    def copy_any(self, out, a):
        self.cp(self.ev(), out, a)

    def bank(self):
        self._bi = (self._bi + 1) % len(self.MM)
        return self.MM[self._bi]

    def bankx(self):
        import os
        if os.environ.get("NO_BANKX"):
            return self.bank()
        self._bxi = (self._bxi + 1) % len(self.MMX)
        return self.MMX[self._bxi]

    def quarter(self):
        self._qi = (self._qi + 1) % len(self.Q)
        return self.Q[self._qi]

    def tbq(self):
        self._ti = (self._ti + 1) % len(self.TB)
        return self.TB[self._ti]

    def slabA(self, l, key):
        idx = self._ai[l]
        self._ai[l] += 1
        if idx >= len(self.plan):
            self.plan.append(key)
        assert self.plan[idx] == key, (idx, key, self.plan[idx])
        slot = self.ring[self._ri % NSLOT]
        self._ri += 1
        self.S.dma("pool", slot.t[:, 0:2048], self.wA[l, idx], writes=[slot])
        return slot

    def slabB(self, l, idx):
        slot = self.ring[self._ri % NSLOT]
        self._ri += 1
        self.S.dma("pool", slot.t[:, 0:2816], self.wB[l, idx], writes=[slot])
        return slot

    def fm_slab(self, l, g):
        if self._fmg[0] != (l, g):
            self._fmg = ((l, g), self.slabA(l, ("fm", g)))
        return self._fmg[1]

    def fm_chunk(self, l, ch):
        slab = self.fm_slab(l, ch // 2)
        return self.fm_chunk_psum(slab, ch % 2)

    def dump(self, name, view, shape):
        if name not in self.debug:
            return
        key = f"dbg_{name}_{len(self.dbg_out)}"
        d = self.nc.dram_tensor(key, list(shape), F32, kind="ExternalOutput").ap()
        self.dbg_out[key] = name
        if not self._dbg_bufs:
            self._dbg_bufs.append(self.S.sbuf(shape, F32, "dbgt"))
        tmp = self._dbg_bufs[0]
        self.cp("dve", tmp[:], view)
        self.S.dma("sp", d, tmp.t[:], reads=[tmp], sembuf=tmp)

    def build(self):
        S = self.S
        self._bi = self._qi = self._ti = 0
        self._ri = 0
        self._fmg = (None, None)
        self._dbg_bufs = []
        self.MM = [S.psum([128, 512], F32, "mm") for _ in range(3)]
        qbanks = [S.psum([128, 512], F32, "qb") for _ in range(4)]
        self.MMX = [self.MM[0], qbanks[0], self.MM[1], qbanks[1], self.MM[2], qbanks[2]]
        self._bxi = 0
        self.Q = [Buf(qbanks[i % 3].t, qbanks[i % 3].name + f"_{i}", c0=(i // 3) * 128, owner=qbanks[i % 3])
                  for i in range(12)]
        self.QF = [Buf(qbanks[3].t, qbanks[3].name + f"_f{i}", c0=i * 128, owner=qbanks[3]) for i in range(4)]
        tbb = S.psum([128, 1024], BF16, "tbb")
        self.TB = [Buf(tbb.t, tbb.name + f"_{i}", c0=i * 128, owner=tbb) for i in range(4)]
        cf = self.cf = S.sbuf([128, NCI, 128], F32, "cf")
        cb = self.cb = S.sbuf([128, 3, 128], BF16, "cb")
        S.dma("sp", cf.t[:], self.cfd.rearrange("p (n j) -> p n j", n=NCI), writes=[cf])
        S.dma("pool", cb.t[:], self.cbd.rearrange("p (n j) -> p n j", n=3), writes=[cb])
        self.pv = S.sbuf([128, 2, NPV], F32, "pv")
        S.dma("sp", self.pv.t[:], self.pvd.rearrange("l p n -> p l n"), writes=[self.pv])
        self.sw = S.sbuf([128, 2, 2304], BF16, "sw")
        S.dma("pool", self.sw.t[:], self.swd.rearrange("l p n -> p l n"), writes=[self.sw])
        self.epsb = S.sbuf([128, 4], F32, "epsb")
        self.memset("dve", self.epsb[:, 0:1], LN_EPS)
        self.memset("dve", self.epsb[:, 1:2], 1e-6)
        self.memset("dve", self.epsb[:, 2:3], 64e-5)
        self.memset("dve", self.epsb[:, 3:4], 1.0)
        self.nega = S.sbuf([128, 2, 4], F32, "nega")
        for l in range(2):
            self.act(self.nega[:, l, :], self.pv[:, l, PV_ALOG:PV_ALOG + 4], AF.Exp)
            self.ts("dve", self.nega[:, l, :], self.nega[:, l, :], -1.0, None, ALU.mult)
        self.ring = [S.sbuf([128, SLOT], BF16, "ring") for _ in range(NSLOT)]
        self.x = [S.sbuf([128, TT], F32, "x") for _ in range(8)]
        self.pool = [S.sbuf([128, TT], F32, "pool") for _ in range(8)]
        self.z = self.pool
        self.xb = [S.sbuf([128, TT], BF16, "xb") for _ in range(8)]
        self.G = [S.sbuf([128, TT], BF16, "G") for _ in range(22)]
        self.br = [self.G[0:4], self.G[4:8], self.G[8:12]]
        self.merged = self.G[12:20]
        self.dxb = self.G[12:20]
        self.pbt = self.G[20:22]
        self.t32 = [S.sbuf([128, TT], F32, "t32") for _ in range(5)]
        self._t32i = 0
        self.xcar = S.sbuf([128, 2, 8], BF16, "xcar")
        self.memset("dve", self.xcar[:], 0.0)
        self.hcar = S.sbuf([128, 2, 28, 3], F32, "hcar")
        self.memset("dve", self.hcar[:], 0.0)
        self.Srw = S.sbuf([128, 2, 4, 128], F32, "Srw")
        self.Srwb = S.sbuf([128, 2, 4, 128], BF16, "Srwb")
        self.Sgl = S.sbuf([128, 2, 2, 128], F32, "Sgl")
        self.Sglb = S.sbuf([128, 2, 2, 128], BF16, "Sglb")
        self.Sgd = S.sbuf([128, 2, 4, 128], F32, "Sgd")
        self.Sgdb = S.sbuf([128, 2, 4, 128], BF16, "Sgdb")
        for s_ in (self.Srw, self.Srwb, self.Sgl, self.Sglb, self.Sgd, self.Sgdb):
            self.memset("dve", s_[:], 0.0)
        self.alloc_mixer()
        self._ai = {0: 0, 1: 0}
        for ti in range(self.NT):
            t0 = ti * TT
            for m in range(8):
                S.dma("sp", self.x[m].t[:], self.xT[m * 128:(m + 1) * 128, t0:t0 + TT], writes=[self.x[m]])
            for l in self.layers:
                self._ai[l] = 0
                self.layer_tile(l, ti)
                assert self._ai[l] == NA_PER_LAYER or self.mixsel != "rgd", self._ai[l]
            for m in range(8):
                S.dma("sp", self.outT[m * 128:(m + 1) * 128, t0:t0 + TT], self.x[m].t[:], reads=[self.x[m]],
                      sembuf=self.x[m])
        S.final_wait("sp", self.x + self._dbg_bufs)
        S.finish()

    def tmp32(self):
        self._t32i = (self._t32i + 1) % len(self.t32)
        return self.t32[self._t32i]

    def make_xb(self):
        for m in range(8):
            self.copy_any(self.xb[m][:], self.x[m][:])

    def ffn(self, l, f):
        for g in range(11):
            w1 = self.slabA(l, ("w1", f, g))
            w3 = self.slabA(l, ("w3", f, g))
            for cc in range(2):
                c = g * 2 + cc
                p1 = self.bankx()
                p3 = self.bankx()
                for k in range(8):
                    self.mm(p1[:], w1[:, k * 256 + cc * 128: k * 256 + cc * 128 + 128], self.xb[k][:], k == 0, k == 7)
                for k in range(8):
                    self.mm(p3[:], w3[:, k * 256 + cc * 128: k * 256 + cc * 128 + 128], self.xb[k][:], k == 0, k == 7)
                t = self.tmp32()
                self.act(t[:], p1[:], AF.Silu)
                self.stt("dve", self.G[c][:], t[:], 0.5, p3[:], ALU.mult, ALU.mult)
        for m in range(8):
            w2 = self.slabB(l, f * 8 + m)
            p = self.bankx()
            for c in range(22):
                self.mm(p[:], w2[:, c * 128:(c + 1) * 128], self.G[c][:], c == 0, c == 21)
            self.stt("dve", self.z[m][:], self.x[m][:], ALPHA, p[:], ALU.mult, ALU.add)

    def ln(self, l, i):
        pm = self.bank()
        pq = self.bank()
        lnm = self.cf[:, CI_LN, :]
        for m in range(8):
            self.mm(pm[:], lnm, self.z[m][:], m == 0, m == 7)
        for m in range(8):
            t = self.tmp32()
            self.act(t[:], self.z[m][:], AF.Square)
            self.mm(pq[:], lnm, t[:], m == 0, m == 7)
        mean = self.lnmean
        rstd = self.lnrstd
        self.cp("act", mean[:], pm[:])
        self.tt("dve", rstd[:], mean[:], mean[:], ALU.mult)
        self.tt("dve", rstd[:], pq[:], rstd[:], ALU.subtract)
        self.act(rstd[:], rstd[:], AF.Ln, bias=self.epsb[:, 0:1])
        self.act(rstd[:], rstd[:], AF.Exp, scale=-0.5)
        for m in range(8):
            t = self.tmp32()
            self.tt("dve", t[:], self.z[m][:], mean[:], ALU.subtract)
            self.tt("dve", t[:], t[:], rstd[:], ALU.mult)
            self.act(self.x[m][:], t[:], AF.Identity, bias=self.pv[:, l, PV_LNB + i * 8 + m: PV_LNB + i * 8 + m + 1],
                     scale=self.pv[:, l, PV_LNG + i * 8 + m: PV_LNG + i * 8 + m + 1])

    def layer_tile(self, l, ti):
        st = self.stages
        self.make_xb()
        self.ffn(l, 0)
        self.ln(l, 0)
        if st < 2:
            self._ai[l] = NA_PER_LAYER
            return
        self.make_xb()
        self.mixers(l, ti)
        self.ln(l, 1)
        if st < 3:
            self._ai[l] = NA_PER_LAYER
            return
        self.make_xb()
        self.ffn(l, 1)
        self.ln(l, 2)
        if st < 4:
            self._ai[l] = NA_PER_LAYER
            return
        self.make_xb()
        self.ple(l, ti)
        self.ln(l, 3)

    def ple(self, l, ti):
        t0 = ti * TT
        for k in range(2):
            pin = self.tmp32()
            self.S.dma("sp", pin.t[:], self.pT[l, k * 128:(k + 1) * 128, t0:t0 + TT], writes=[pin])
            self.copy_any(self.pbt[k][:], pin[:])
        slabs = [self.slabA(l, ("pg", i)) for i in range(4)]
        wp = self.slabA(l, ("pp", 0))
        for m in range(8):
            pg = self.bankx()
            pp = self.bankx()
            sl = slabs[m // 2]
            for k in range(8):
                self.mm(pg[:], sl[:, k * 256 + (m % 2) * 128: k * 256 + (m % 2) * 128 + 128], self.xb[k][:], k == 0, k == 7)
            for k in range(2):
                self.mm(pp[:], wp[:, k * 1024 + m * 128: k * 1024 + m * 128 + 128], self.pbt[k][:], k == 0, k == 1)
            t = self.tmp32()
            self.act(t[:], pg[:], AF.Sigmoid)
            self.tt("dve", t[:], t[:], pp[:], ALU.mult)
            self.stt("dve", self.z[m][:], self.x[m][:], ALPHA, t[:], ALU.mult, ALU.add)

    def alloc_mixer(self):
        S = self.S
        self.lnmean = S.sbuf([128, TT], F32, "lnmean")
        self.lnrstd = S.sbuf([128, TT], F32, "lnrstd")
        self.raw = [S.sbuf([128, 3 + TT], F32, "raw") for _ in range(2)]
        self._rawi = 0
        self.rw_tw = S.sbuf([128, TT], BF16, "rw_tw")
        self.rw_al = S.sbuf([128, TT], BF16, "rw_al")
        self.rw_gs = S.sbuf([128, 2, TT], BF16, "rw_gs")
        self.rw_kk = S.sbuf([128, TT], BF16, "rw_kk")
        self.rw_b = S.sbuf([128, TT], BF16, "rw_b")
        self.vt16 = S.sbuf([128, 4, 256], BF16, "vt16")
        self.kt16 = S.sbuf([128, 4, 128], BF16, "kt16")
        self.muvj = S.sbuf([128, 128], F32, "muvj")
        self.vpad = S.sbuf([128, 4, 2, 128], BF16, "vpad")
        self.memset("dve", self.vpad[:], 0.0)
        self.kpad = [S.sbuf([128, 128], BF16, "kpad") for _ in range(2)]
        for u_ in self.kpad:
            self.memset("dve", u_[:], 0.0)
        self.upad = [S.sbuf([128, 128], BF16, "upad") for _ in range(2)]
        for u_ in self.upad:
            self.memset("dve", u_[:], 0.0)
        self.gl_lo = S.sbuf([128, TT], BF16, "gl_lo")
        self.gl_l = S.sbuf([128, 4, 256], F32, "gl_l")
        self.gl_kt = S.sbuf([128, 4, 256], F32, "gl_kt")
        self.gd_ab = S.sbuf([128, 4, 8], F32, "gd_ab")
        self.gd_sc = S.sbuf([128, 4, 24], F32, "gd_sc")
        self.s16 = [S.sbuf([128, 128], BF16, "s16") for _ in range(26)]
        self.l16 = [S.sbuf([128, 128], BF16, "l16") for _ in range(26)]
        self.s32 = [S.sbuf([128, 128], F32, "s32") for _ in range(12)]
        self._s16i = self._l16i = self._s32i = 0
        self.s16w = [S.sbuf([128, 256], BF16, "s16w") for _ in range(4)]
        self._s16wi = 0

    def b16(self):
        self._s16i = (self._s16i + 1) % len(self.s16)
        return self.s16[self._s16i]

    def L16(self):
        self._l16i = (self._l16i + 1) % len(self.l16)
        return self.l16[self._l16i]

    def b32(self):
        self._s32i = (self._s32i + 1) % len(self.s32)
        return self.s32[self._s32i]

    def b16w(self):
        self._s16wi = (self._s16wi + 1) % len(self.s16w)
        return self.s16w[self._s16wi]

    def fm_chunk_psum(self, slab, cc, wide=False):
        p = self.bankx() if wide else self.bank()
        for k in range(8):
            self.mm(p[:], slab[:, k * 256 + cc * 128: k * 256 + cc * 128 + 128], self.xb[k][:], k == 0, k == 7)
        return p

    def raw_tile(self, l, ci, p):
        self._rawi = (self._rawi + 1) % len(self.raw)
        r = self.raw[self._rawi]
        self.cp("dve", r[:, 0:3], self.hcar[:, l, ci, :])
        self.cp("act", r[:, 3:3 + TT], p[:])
        self.cp("dve", self.hcar[:, l, ci, :], r[:, TT:TT + 3])
        return r

    def shift_chunk(self, l, ci, p, out):
        r = self.raw_tile(l, ci, p)
        t = self.tmp32()
        self.tt("dve", t[:], r[:, 2:2 + TT], r[:, 3:3 + TT], ALU.subtract)
        mu = self.pv[:, l, PV_MU + ci: PV_MU + ci + 1]
        self.stt("dve", out, t[:], mu, r[:, 3:3 + TT], ALU.mult, ALU.add)

    def mixers(self, l, ti):
        for k in range(8):
            self.tt("dve", self.dxb[k][:, 1:TT], self.xb[k][:, 0:TT - 1], self.xb[k][:, 1:TT], ALU.subtract)
            self.tt("dve", self.dxb[k][:, 0:1], self.xcar[:, l, k:k + 1], self.xb[k][:, 0:1], ALU.subtract)
            self.cp("dve", self.xcar[:, l, k:k + 1], self.xb[k][:, TT - 1:TT])
        for name, fn, b in (("r", self.rwkv, 0), ("g", self.gla, 1), ("d", self.gdn, 2)):
            if name in self.mixsel:
                fn(l)
            else:
                for j in range(4):
                    self.memset("dve", self.br[b][j][:], 0.0)
        if self.mixsel != "rgd":
            pass
        self.merge(l)

    def tri_inverse_multi(self, XPs):
        ident_f = self.cf[:, CI_ID, :]
        st = []
        for X, P in XPs:
            Z = self.b16()
            self.tt("dve", Z[:], X, ident_f, ALU.add)
            st.append([X, P, Z])
        for lvl in range(6):
            last = lvl == 5
            pqs, xqs = [], []
            for c in st:
                pq = self.quarter()
                self.mm(pq[:], c[0], c[1])
                pqs.append(pq)
                if not last:
                    xq = self.quarter()
                    self.mm(xq[:], c[1], c[0])
                    xqs.append(xq)
            Pns = []
            for ci, c in enumerate(st):
                Pn = self.b16()
                self.cp("act", Pn[:], pqs[ci][:])
                Pns.append(Pn)
                if not last:
                    Xn = self.b16()
                    self.cp("dve", Xn[:], xqs[ci][:])
                    c[0] = Xn[:]
                c[1] = Pn[:]
            zqs = []
            for ci, c in enumerate(st):
                zq = self.quarter()
                self.mm(zq[:], Pns[ci][:], c[2][:])
                zqs.append(zq)
            for ci, c in enumerate(st):
                Zn = self.b16()
                self.tt("dve", Zn[:], zqs[ci][:], c[2][:], ALU.add)
                c[2] = Zn
        return [c[2] for c in st]

    def tri_inverse(self, X, P):
        return self.tri_inverse_multi([(X, P)])[0]

    def rwkv(self, l):
        cf, cb, sw = self.cf, self.cb, self.sw
        pvl = lambda c0, j: self.pv[:, l, c0 + j: c0 + j + 1]
        blk = cf[:, CI_BLK, :]
        idb = cb[:, CB_ID, :]
        KD = -float(np.exp(-0.5))
        t = self.tmp32()
        self.shift_chunk(l, 0, self.fm_chunk(l, 0), t[:])
        self.act(self.rw_tw[0:64, :], t[0:64, :], AF.Tanh)
        self.memset("dve", self.rw_tw[64:65, :], 1.0)
        self.cp("dve", self.rw_al[:], t[:])
        for jj in range(2):
            t = self.tmp32()
            self.shift_chunk(l, 1 + jj, self.fm_chunk(l, 1 + jj), t[:])
            self.act(self.rw_gs[:, jj, :], t[:], AF.Sigmoid)
        r, k, v, y = self.pool[0], self.pool[1], self.pool[2], self.pool[3]
        tmv = None
        for j in range(4):
            self.shift_chunk(l, 4 + 3 * j, self.fm_chunk(l, 4 + 3 * j), r[:])
            self.shift_chunk(l, 5 + 3 * j, self.fm_chunk(l, 5 + 3 * j), k[:])
            self.shift_chunk(l, 6 + 3 * j, self.fm_chunk(l, 6 + 3 * j), v[:])
            if j % 2 == 0:
                tmv = self.slabA(l, ("tm", j // 2))
            self.S.dma("sp", self.muvj.t[:], self.muvd[l, :, j * 128:(j + 1) * 128], writes=[self.muvj])
            for s in range(4):
                ts_ = slice(s * C, (s + 1) * C)
                q1 = self.quarter()
                q2 = self.quarter()
                c0 = (j % 2) * 128
                for kk_ in range(8):
                    self.mm(q1[:], self.xb[kk_][:, ts_], tmv[:, kk_ * 256 + c0: kk_ * 256 + c0 + 128], kk_ == 0, kk_ == 7)
                for kk_ in range(8):
                    self.mm(q2[:], self.dxb[kk_][:, ts_], tmv[:, kk_ * 256 + c0: kk_ * 256 + c0 + 128], kk_ == 0, kk_ == 7)
                tq_ = self.b32()
                self.tt("dve", tq_[:], q2[:], self.muvj[:], ALU.mult)
                self.tt("dve", self.vpad[:, s, 0, 0:64], tq_[:, 0:64], q1[:, 0:64], ALU.add)
                self.tt("dve", self.vpad[:, s, 1, 64:128], tq_[:, 64:128], q1[:, 64:128], ALU.add)
            p = self.bank()
            self.mm(p[:], sw[:, l, 512 + j * 128: 512 + (j + 1) * 128], self.rw_al[:])
            a = self.tmp32()
            self.act(a[:], p[:], AF.Sigmoid, bias=pvl(PV_A0, j))
            t1 = self.tmp32()
            self.ts("dve", t1[:], k[:], pvl(PV_KK, j), None, ALU.mult)
            sq = self.tmp32()
            self.act(sq[:], t1[:], AF.Square)
            p = self.bank()
            self.mm(p[:], blk, sq[:])
            rn = sq
            self.act(rn[:], p[:], AF.Ln, bias=self.epsb[:, 1:2])
            self.act(rn[:], rn[:], AF.Exp, scale=-0.5)
            self.tt("dve", self.rw_kk[:], t1[:], rn[:], ALU.mult)
            self.tt("dve", self.rw_b[:], self.rw_kk[:], a[:], ALU.mult)
            self.ts("dve", a[:], a[:], -1.0, pvl(PV_KA, j), ALU.add, ALU.mult)
            self.stt("dve", k[:], a[:], 1.0, k[:], ALU.add, ALU.mult)
            t3 = self.tmp32()
            self.stt("dve", t3[:], r[:], pvl(PV_RK, j), k[:], ALU.mult, ALU.mult)
            p = self.bank()
            self.mm(p[:], blk, t3[:])
            self.tt("dve", v[:], p[:], v[:], ALU.mult)
            for s in range(4):
                ts_ = slice(s * C, (s + 1) * C)
                qz = self.quarter()
                self.mm(qz[:], self.rw_tw[0:65, ts_], sw[0:65, l, j * 128:(j + 1) * 128])
                sg = self.b32()
                self.act(sg[:], qz[:], AF.Sigmoid)
                qc = self.quarter()
                self.mm(qc[:], sg[:], cf[:, CI_TLE, :])
                qp = self.quarter()
                self.mm(qp[:], sg[:], cf[:, CI_TLT, :])
                E1 = self.b32()
                self.act(E1[:], qc[:], AF.Exp, scale=KD)
                Ei = self.b32()
                self.act(Ei[:], qc[:], AF.Exp, scale=-KD)
                E0 = self.b32()
                self.act(E0[:], qp[:], AF.Exp, scale=KD)
                ARt = self.b16w()
                self.stt("dve", ARt[:, 0:128], self.rw_kk[:, ts_], -1.0, E0[:], ALU.mult, ALU.mult)
                self.tt("dve", ARt[:, 128:256], r[:, ts_], E1[:], ALU.mult)
                Bt = self.L16()
                self.tt("dve", Bt[:], self.rw_b[:, ts_], Ei[:], ALU.mult)
                Kt = self.L16()
                self.tt("dve", Kt[:], k[:, ts_], Ei[:], ALU.mult)
                tq = self.tbq()
                self.tr(tq[:], Bt[:], idb)
                Btm = self.L16()
                self.copy_any(Btm[:], tq[:])
                tq = self.tbq()
                self.tr(tq[:], Kt[:], idb)
                Ktm = self.L16()
                self.copy_any(Ktm[:], tq[:])
                hm = (cf[:, CI_BLK, 0:1], cf[:, CI_BLK, 64:65])
                Sblk = self.Srwb[:, l, j, :]
                qy = self.QF[0]
                self.mm(qy[:], Sblk, ARt[:, 128:256], True, False)
                pre = []
                for hh in range(2):
                    Bth = self.L16()
                    self.ts("dve", Bth[:], Bt[:], hm[hh], None, ALU.mult)
                    Kth = self.L16()
                    self.ts("dve", Kth[:], Kt[:], hm[hh], None, ALU.mult)
                    Ath = self.L16()
                    self.ts("dve", Ath[:], ARt[:, 0:128], hm[hh], None, ALU.mult)
                    pb_ = self.bank()
                    self.mm(pb_[:, 0:256], Bth[:], ARt[:])
                    self.mm(pb_[:, 256:512], Kth[:], ARt[:])
                    qP = self.quarter()
                    self.mm(qP[:], Ath[:], Bt[:])
                    X = self.L16()
                    self.tt("dve", X[:], pb_[:, 0:128], cf[:, CI_TLT, :], ALU.mult)
                    NBT = self.L16()
                    self.tt("dve", NBT[:], pb_[:, 128:256], cf[:, CI_TLE, :], ALU.mult)
                    MKT = self.L16()
                    self.tt("dve", MKT[:], pb_[:, 256:384], cf[:, CI_TLT, :], ALU.mult)
                    NKT = self.L16()
                    self.tt("dve", NKT[:], pb_[:, 384:512], cf[:, CI_TLE, :], ALU.mult)
                    P = self.L16()
                    self.tt("dve", P[:], qP[:], cf[:, CI_TGT, :], ALU.mult)
                    pre.append((X, P, NBT, MKT, NKT))
                Zs = self.tri_inverse_multi([(pr[0][:], pr[1][:]) for pr in pre])
                for hh in range(2):
                    hs = slice(hh * 64, hh * 64 + 64)
                    X, P, NBT, MKT, NKT = pre[hh]
                    Z = Zs[hh]
                    V_ = self.vpad[:, s, hh, hs]
                    qr = self.quarter()
                    self.mm(qr[:, 0:64], ARt[:, 0:128], self.Srwb[:, l, j, hs], True, False)
                    self.mm(qr[:, 0:64], MKT[:], V_, False, True)
                    RHS = self.L16()
                    self.copy_any(RHS[:, 0:64], qr[:, 0:64])
                    qu = self.quarter()
                    self.mm(qu[:, 0:64], Z[:], RHS[:, 0:64])
                    self.copy_any(self.upad[hh][:, hs], qu[:, 0:64])
                    self.mm(qy[:], self.upad[hh][:], NBT[:], False, False)
                    self.mm(qy[:], self.vpad[:, s, hh, :], NKT[:], False, hh == 1)
                self.copy_any(y[:, ts_], qy[:])
                qs = self.QF[1]
                self.mm(qs[:], Btm[:], self.upad[0][:], True, False)
                self.mm(qs[:], Btm[:], self.upad[1][:], False, False)
                self.mm(qs[:], Ktm[:], self.vpad[:, s, 0, :], False, False)
                self.mm(qs[:], Ktm[:], self.vpad[:, s, 1, :], False, True)
                tmpS = self.b32()
                self.tt("dve", tmpS[:], qs[:], cf[:, CI_BLK, :], ALU.mult)
                Sf = self.Srw[:, l, j, :]
                self.tt("dve", Sf, tmpS[:], Sf, ALU.add)
                self.ts("dve", Sf, Sf, E1[:, 127:128], None, ALU.mult)
                self.cp("act", self.Srwb[:, l, j, :], Sf)
            pg_ = self.bank()
            self.mm(pg_[:], sw[:, l, 1024 + j * 128: 1024 + (j + 1) * 128], self.rw_gs[:, 0, :], True, False)
            self.mm(pg_[:], sw[0:32, l, 1536 + j * 128: 1536 + (j + 1) * 128], self.rw_gs[0:32, 1, :], False, True)
            p = self.bank()
            self.mm(p[:], blk, y[:])
            d = self.tmp32()
            self.stt("dve", d[:], p[:], -1.0 / 64, y[:], ALU.mult, ALU.add)
            sq = self.tmp32()
            self.act(sq[:], d[:], AF.Square)
            p2 = self.bank()
            self.mm(p2[:], blk, sq[:])
            rs = sq
            self.act(rs[:], p2[:], AF.Ln, bias=self.epsb[:, 2:3], scale=1.0 / 64)
            self.act(rs[:], rs[:], AF.Exp, scale=-0.5)
            self.tt("dve", d[:], d[:], rs[:], ALU.mult)
            self.act(d[:], d[:], AF.Identity, bias=pvl(PV_GNB, j), scale=pvl(PV_GNG, j))
            self.tt("dve", d[:], d[:], v[:], ALU.add)
            self.tt("dve", self.br[0][j][:], d[:], pg_[:], ALU.mult)
            self.dump("o_rw", d[:], [128, TT])

    def gla(self, l):
        cf, cb, sw = self.cf, self.cb, self.sw
        p = self.fm_chunk(l, 16)
        self.cp("act", self.gl_lo[0:32, :], p[0:32, :])
        self.memset("dve", self.gl_lo[32:33, :], 1.0)
        if self.cut(0, 1):
            return
        tmk = self.slabA(l, ("tm", 2))
        for s in range(4):
            ts_ = slice(s * C, (s + 1) * C)
            p = self.bank()
            self.mm(p[:, 0:256], self.gl_lo[0:33, ts_], sw[0:33, l, 2048:2304])
            e = self.tmp32()
            self.act(e[:, 0:256], p[:, 0:256], AF.Exp, scale=-1.0)
            self.act(self.gl_l[:, s, :], e[:, 0:256], AF.Ln, bias=self.epsb[:, 3:4])
            pk = self.bank()
            for k in range(8):
                self.mm(pk[:, 0:256], self.xb[k][:, ts_], tmk[:, k * 256:(k + 1) * 256], k == 0, k == 7)
            self.copy_any(self.gl_kt[:, s, :], pk[:, 0:256])
        if self.cut(1, 1):
            return
        q, k_, g0, g1, o0, o1 = self.pool[0:6]
        for j in range(2):
            self.copy_any(q[:], self.fm_chunk(l, 18 + 4 * j)[:])
            self.copy_any(k_[:], self.fm_chunk(l, 19 + 4 * j)[:])
            self.act(g0[:], self.fm_chunk(l, 20 + 4 * j)[:], AF.Silu)
            self.act(g1[:], self.fm_chunk(l, 21 + 4 * j)[:], AF.Silu)
            tmv = self.slabA(l, ("tm", 3 + j))
            for s in range(4):
                ts_ = slice(s * C, (s + 1) * C)
                pv_ = self.bank()
                for k in range(8):
                    self.mm(pv_[:, 0:256], self.xb[k][:, ts_], tmv[:, k * 256:(k + 1) * 256], k == 0, k == 7)
                self.copy_any(self.vt16[:, s, :], pv_[:, 0:256])
            if self.cut(2, 1):
                return
            for s in range(4):
                ts_ = slice(s * C, (s + 1) * C)
                lj = self.gl_l[:, s, j * 128:(j + 1) * 128]
                qk = self.quarter()
                self.mm(qk[:], cf[:, CI_TGT, :], lj)
                ek = self.b32()
                self.act(ek[:], qk[:], AF.Exp, scale=-1.0 / 16)
                for hh in range(2):
                    c0_, c1_ = hh * 64, hh * 64 + 64
                    self.tt("dve", self.kpad[hh][:, c0_:c1_], self.gl_kt[:, s, j * 128 + c0_: j * 128 + c1_],
                            ek[:, c0_:c1_], ALU.mult)
                qsf = self.QF[2]
                qb = self.quarter()
                self.mm(qb[:], lj, cf[:, CI_TLE, :])
                Eb = self.b32()
                self.act(Eb[:], qb[:], AF.Exp, scale=-1.0 / 16)
                Ein = self.b32()
                self.act(Ein[:], qb[:], AF.Exp, scale=1.0 / 16)
                qt = self.L16()
                self.stt("dve", qt[:], q[:, ts_], 0.125, Eb[:], ALU.mult, ALU.mult)
                hm = (cf[:, CI_BLK, 0:1], cf[:, CI_BLK, 64:65])
                for hh in range(2):
                    hs = slice(hh * 64, hh * 64 + 64)
                    if hh == 1 and self.cut(6, 1):
                        return
                    kth = self.L16()
                    self.stt("dve", kth[:], k_[:, ts_], hm[hh], Ein[:], ALU.mult, ALU.mult)
                    qth = self.L16()
                    self.ts("dve", qth[:], qt[:], hm[hh], None, ALU.mult)
                    if hh == 1 and self.cut(7, 1):
                        return
                    qa = self.quarter()
                    self.mm(qa[:], kth[:], qt[:])
                    attT = self.b16()
                    self.tt("dve", attT[:], qa[:], cf[:, CI_TLE, :], ALU.mult)
                    if hh == 1 and self.cut(8, 1):
                        return
                    V_ = self.vt16[:, s, hh * 128:(hh + 1) * 128]
                    qo = self.quarter()
                    self.mm(qo[:], V_, attT[:], True, False)
                    self.mm(qo[:], self.Sglb[:, l, j, :], qth[:], False, True)
                    self.copy_any((o0, o1)[hh][:, ts_], qo[:])
                    if hh == 1 and self.cut(9, 1):
                        return
                    self.mm(qsf[:], self.kpad[hh][:], V_, hh == 0, hh == 1)
                Sf = self.Sgl[:, l, j, :]
                self.stt("dve", Sf, Sf, Eb[:, 127:128], qsf[:], ALU.mult, ALU.add)
                self.cp("act", self.Sglb[:, l, j, :], Sf)
            self.head_rms(l, o0, g0, PV_GLAN, self.br[1][2 * j])
            self.head_rms(l, o1, g1, PV_GLAN, self.br[1][2 * j + 1])

    def cut(self, n, b):
        import os
        c = int(os.environ.get("GCUT", "99"))
        if n >= c:
            for j in range(4):
                self.memset("dve", self.br[b][j][:], 0.0)
            return True
        return False

    def head_rms(self, l, o, gate, pvcol, dst):
        sq = self.tmp32()
        self.act(sq[:], o[:], AF.Square)
        p = self.bank()
        self.mm(p[:], self.cf[:, CI_V128, :], sq[:])
        rs = sq
        self.act(rs[:], p[:], AF.Ln, bias=self.epsb[:, 1:2])
        self.act(rs[:], rs[:], AF.Exp, scale=-0.5)
        t = self.tmp32()
        self.stt("dve", t[:], o[:], self.pv[:, l, pvcol:pvcol + 1], rs[:], ALU.mult, ALU.mult)
        self.tt("dve", dst[:], t[:], gate[:], ALU.mult)

    def gdn(self, l):
        cf, cb = self.cf, self.cb
        ones_f = cf[:, CI_ONE, :]
        idb = cb[:, CB_ID, :]
        tmab = self.slabA(l, ("tm", 5))
        sc = self.gd_sc
        for s in range(4):
            ts_ = slice(s * C, (s + 1) * C)
            pa = self.quarter()
            for k in range(8):
                self.mm(pa[:, 0:8], self.xb[k][:, ts_], tmab[:, k * 256:k * 256 + 8], k == 0, k == 7)
            self.cp("dve", self.gd_ab[:, s, :], pa[:, 0:8])
            self.tt("dve", sc[:, s, 0:4], self.gd_ab[:, s, 0:4], self.pv[:, l, PV_DTB:PV_DTB + 4], ALU.add)
            self.act(sc[:, s, 0:4], sc[:, s, 0:4], AF.Exp)
            self.act(sc[:, s, 0:4], sc[:, s, 0:4], AF.Ln, bias=self.epsb[:, 3:4])
            self.tt("dve", sc[:, s, 0:4], sc[:, s, 0:4], self.nega[:, l, :], ALU.mult)
            self.act(sc[:, s, 4:8], self.gd_ab[:, s, 4:8], AF.Sigmoid)
            qg = self.quarter()
            self.mm(qg[:, 0:4], cf[:, CI_TLE, :], sc[:, s, 0:4])
            self.mm(qg[:, 4:8], cf[:, CI_TGT, :], sc[:, s, 0:4])
            self.act(sc[:, s, 8:16], qg[:, 0:8], AF.Exp)
            self.tt("dve", sc[:, s, 16:20], sc[:, s, 4:8], sc[:, s, 8:12], ALU.mult)
            self.ts("dve", sc[:, s, 20:24], sc[:, s, 4:8], -1.0, None, ALU.mult)
        q, k_, v, gate, o = self.pool[0:5]
        for h in range(4):
            for which, dst in ((0, q), (1, k_), (2, v)):
                cidx = which * 4 + h
                r = self.raw_tile(l, 16 + cidx, self.fm_chunk(l, 26 + 4 * h + which))
                t = self.tmp32()
                cw = lambda tap: self.pv[:, l, PV_CONV + tap * 12 + cidx: PV_CONV + tap * 12 + cidx + 1]
                self.ts("dve", t[:], r[:, 0:TT], cw(0), None, ALU.mult)
                for tap in (1, 2, 3):
                    self.stt("dve", t[:], r[:, tap:tap + TT], cw(tap), t[:], ALU.mult, ALU.add)
                self.act(dst[:], t[:], AF.Silu)
            self.act(gate[:], self.fm_chunk(l, 29 + 4 * h)[:], AF.Silu)
            for src, scale in ((q, 128.0 ** -0.5), (k_, 1.0)):
                sq = self.tmp32()
                self.act(sq[:], src[:], AF.Square)
                p = self.bank()
                self.mm(p[:], ones_f, sq[:])
                rs = sq
                self.act(rs[:], p[:], AF.Ln, bias=self.epsb[:, 1:2])
                self.act(rs[:], rs[:], AF.Exp, scale=-0.5)
                self.stt("dve", src[:], src[:], scale, rs[:], ALU.mult, ALU.mult)
            for src, dstt in ((k_, self.kt16), (v, self.vt16)):
                p = self.bank()
                for s in range(4):
                    self.tr(p[:, s * 128:(s + 1) * 128], src[:, s * C:(s + 1) * C], cf[:, CI_ID, :])
                for s in range(4):
                    self.copy_any(dstt[:, s, 0:128], p[:, s * 128:(s + 1) * 128])
            for s in range(4):
                ts_ = slice(s * C, (s + 1) * C)
                kn_tm = self.kt16[:, s, :]
                v_tm = self.vt16[:, s, 0:128]
                knT = self.L16()
                self.copy_any(knT[:], k_[:, ts_])
                GT = self.b32()
                self.ts("dve", GT[:], cf[:, CI_TLE, :], sc[:, s, h:h + 1], None, ALU.mult)
                GB = self.b32()
                self.ts("dve", GB[:], cf[:, CI_ONE, :], sc[:, s, h:h + 1], None, ALU.mult)
                qd = self.quarter()
                self.mm(qd[:], GT[:], cf[:, CI_ONE, :], True, False)
                self.mm(qd[:], GB[:], cf[:, CI_NTLE, :], False, False)
                self.mm(qd[:], idb, cb[:, CB_MBSL, :], False, True)
                Dsl = self.b32()
                self.act(Dsl[:], qd[:], AF.Exp)
                qe = self.quarter()
                self.mm(qe[:], GB[:], cf[:, CI_TLE, :], True, False)
                self.mm(qe[:], GT[:], cf[:, CI_NEGONE, :], False, False)
                self.mm(qe[:], idb, cb[:, CB_MBIU, :], False, True)
                Diu = self.b32()
                self.act(Diu[:], qe[:], AF.Exp)
                qbc = self.quarter()
                self.mm(qbc[:], GB[:], cf[:, CI_TLE, :])
                bcE = self.b32()
                self.act(bcE[:], qbc[:], AF.Exp)
                qG = self.quarter()
                self.mm(qG[:], knT[:], knT[:])
                P = self.L16()
                self.stt("dve", P[:], qG[:], sc[:, s, 20 + h:21 + h], Dsl[:], ALU.mult, ALU.mult)
                tq = self.tbq()
                self.tr(tq[:], P[:], idb)
                X = self.L16()
                self.copy_any(X[:], tq[:])
                Z = self.tri_inverse(X[:], P[:])
                qnT = self.L16()
                self.copy_any(qnT[:], q[:, ts_])
                qa = self.quarter()
                self.mm(qa[:], knT[:], qnT[:])
                attT = self.L16()
                self.tt("dve", attT[:], qa[:], Diu[:], ALU.mult)
                qgT = self.L16()
                self.tt("dve", qgT[:], q[:, ts_], bcE[:], ALU.mult)
                kbg = self.L16()
                self.ts("dve", kbg[:], kn_tm, sc[:, s, 16 + h:17 + h], None, ALU.mult)
                kdec = self.L16()
                self.ts("dve", kdec[:], kn_tm, sc[:, s, 12 + h:13 + h], None, ALU.mult)
                vb = self.L16()
                self.ts("dve", vb[:], v_tm, sc[:, s, 4 + h:5 + h], None, ALU.mult)
                qw = self.quarter()
                self.mm(qw[:], kbg[:], Z[:])
                nwT = self.L16()
                self.ts("dve", nwT[:], qw[:], -1.0, None, ALU.mult)
                Sb = self.Sgdb[:, l, h, :]
                qv = self.quarter()
                self.mm(qv[:], Z[:], vb[:], True, False)
                self.mm(qv[:], nwT[:], Sb, False, True)
                vnew = self.L16()
                self.copy_any(vnew[:], qv[:])
                qo = self.quarter()
                self.mm(qo[:], Sb, qgT[:], True, False)
                self.mm(qo[:], vnew[:], attT[:], False, True)
                self.copy_any(o[:, ts_], qo[:])
                qs = self.quarter()
                self.mm(qs[:], kdec[:], vnew[:])
                Sf = self.Sgd[:, l, h, :]
                self.stt("dve", Sf, Sf, bcE[:, 127:128], qs[:], ALU.mult, ALU.add)
                self.cp("act", self.Sgdb[:, l, h, :], Sf)
            self.head_rms(l, o, gate, PV_GDNN, self.br[2][h])

    def merge(self, l):
        macc = self.z
        for b in range(3):
            for j in range(4):
                self.dump("br", self.br[b][j][:], [128, TT])
        for b in range(3):
            for half in range(2):
                wb = self.slabA(l, ("br", b, half))
                for mm_ in range(2):
                    wg = self.slabA(l, ("gate", b * 4 + half * 2 + mm_))
                    for cc in range(2):
                        m = half * 4 + mm_ * 2 + cc
                        pg = self.fm_chunk_psum(wg, cc, wide=True)
                        pp = self.bankx()
                        for k in range(4):
                            c0 = k * 512 + (mm_ * 2 + cc) * 128
                            self.mm(pp[:], wb[:, c0:c0 + 128], self.br[b][k][:], k == 0, k == 3)
                        t = self.tmp32()
                        self.act(t[:], pg[:], AF.Sigmoid)
                        if b == 0:
                            self.tt("dve", macc[m][:], t[:], pp[:], ALU.mult)
                        else:
                            self.tt("dve", t[:], t[:], pp[:], ALU.mult)
                            if b == 1:
                                self.tt("dve", macc[m][:], macc[m][:], t[:], ALU.add)
                            else:
                                self.tt("dve", self.merged[m][:], macc[m][:], t[:], ALU.add)
        wos = [self.slabA(l, ("wo", i)) for i in range(4)]
        for m in range(8):
            p = self.bankx()
            sl = wos[m // 2]
            for k in range(8):
                self.mm(p[:], sl[:, k * 256 + (m % 2) * 128: k * 256 + (m % 2) * 128 + 128], self.merged[k][:], k == 0, k == 7)
            self.stt("dve", self.z[m][:], self.x[m][:], ALPHA, p[:], ALU.mult, ALU.add)


_CACHE = {}


def prep_weights(inputs, plan):
    inp = {k: np.asarray(v, np.float32) for k, v in inputs.items() if k not in ("x", "p")}
    per = [prep_layer(inp, l, plan) for l in range(2)]
    cf, cb = build_consts()
    return dict(wA=np.ascontiguousarray(np.stack([p[0] for p in per], 0)),
                wB=np.ascontiguousarray(np.stack([p[1] for p in per], 0)),
                pv=np.ascontiguousarray(np.stack([p[2] for p in per], 0)),
                sw=np.ascontiguousarray(np.stack([p[3] for p in per], 0)),
                muv=np.ascontiguousarray(np.stack([p[4] for p in per], 0)),
                cf=cf, cb=cb)


def run(inputs, T, layers=(0, 1), n_cores=8, debug=(), stages=4, mixsel="rgd"):
    x = np.asarray(inputs["x"], np.float32)
    p = np.asarray(inputs["p"], np.float32)
    B = x.shape[0]
    key = (T, tuple(layers), tuple(debug), stages, mixsel)
    if key not in _CACHE:
        _CACHE[key] = Prog(T, list(layers), debug, stages, mixsel)
    prog = _CACHE[key]
    w = prep_weights(inputs, prog.plan)
    in_maps = []
    for c in range(n_cores):
        b = c % B
        m = dict(w)
        m["xT"] = np.ascontiguousarray(x[b, :T].T)
        m["pT"] = np.ascontiguousarray(p[:, b, :T].transpose(0, 2, 1))
        in_maps.append(m)
    res = run_bass_kernel_spmd(prog.nc, in_maps, core_ids=list(range(n_cores)))
    out = np.stack([np.ascontiguousarray(res.results[b]["outT"].T) for b in range(B)], 0)
    return out.astype(np.float32), res, prog


def kernel(**inputs):
    out, _, _ = run(inputs, T=4096)
    return out
```

```python
import contextlib
import numpy as np
import concourse.bass as bass
import concourse.mybir as mybir
from concourse.bass_utils import run_bass_kernel_spmd

F32 = mybir.dt.float32
BF16 = mybir.dt.bfloat16
AF = mybir.ActivationFunctionType
ALU = mybir.AluOpType
AX = mybir.AxisListType

D = 1024
DFF = 2816
TT = 512
C = 128
ALPHA = 4.0 ** 0.25
LN_EPS = 1e-5
SAME_ENGINE_SYNC = True
NSLOT = 5
SLOT = 2816


class V:
    __slots__ = ("b", "ap")

    def __init__(self, b, ap):
        self.b = b
        self.ap = ap


class Buf:
    __slots__ = ("t", "name", "last_write", "reads", "dsem", "dcnt", "c0", "owner", "excl")

    def __init__(self, t, name, c0=None, owner=None):
        self.owner = owner
        self.excl = False
        self.t = t
        self.name = name
        self.last_write = None
        self.reads = []
        self.dsem = None
        self.dcnt = 0
        self.c0 = c0

    def __getitem__(self, idx):
        if self.c0 is None:
            return V(self, self.t[idx])
        if not isinstance(idx, tuple):
            idx = (idx, slice(None))
        r, c = idx
        a = 0 if c.start is None else c.start
        b = 128 if c.stop is None else c.stop
        return V(self.owner or self, self.t[r, self.c0 + a: self.c0 + b])


class Sched:
    ENGS = ("pe", "act", "dve", "pool", "sp")

    def __init__(self, nc):
        self.nc = nc
        self.stack = contextlib.ExitStack()
        self.q = {e: [] for e in self.ENGS}
        self.sem = {}
        self.cnt = {e: 0 for e in self.ENGS}
        self.seen = {e: {} for e in self.ENGS}
        for e in self.ENGS:
            self.sem[e] = self.stack.enter_context(nc.semaphore("s_" + e))
        self.nbuf = 0
        self.ninst = 0
        self.rr = 0

    def sbuf(self, shape, dtype=F32, name=None):
        self.nbuf += 1
        name = (name or "b") + f"_{self.nbuf}"
        t = self.stack.enter_context(self.nc.sbuf_tensor(name, list(shape), dtype))
        return Buf(t, name)

    def psum(self, shape, dtype=F32, name=None):
        self.nbuf += 1
        name = (name or "p") + f"_{self.nbuf}"
        t = self.stack.enter_context(self.nc.psum_tensor(name, list(shape), dtype))
        b = Buf(t, name)
        b.excl = True
        return b

    def dsem_for(self, buf):
        if buf.dsem is None:
            buf.dsem = self.stack.enter_context(self.nc.semaphore("d_" + buf.name))
        return buf.dsem

    def _deps(self, eng, reads, writes):
        waits = {}

        def add(rec):
            if rec is None:
                return
            s, v, owner = rec
            if owner == eng and (eng == "pe" or not SAME_ENGINE_SYNC):
                return
            k = id(s)
            if k not in waits or waits[k][1] < v:
                waits[k] = (s, v)

        for b in reads:
            add(b.last_write)
            if b.excl:
                for r in b.reads:
                    if r[2] != eng:
                        add(r)
        for b in writes:
            add(b.last_write)
            for r in b.reads:
                add(r)
        out = []
        seen = self.seen[eng]
        for k, (s, v) in waits.items():
            if seen.get(k, 0) >= v:
                continue
            seen[k] = v
            out.append((s, v))
        return out

    def op(self, eng, fn, reads=(), writes=()):
        waits = self._deps(eng, reads, writes)
        self.cnt[eng] += 1
        v = self.cnt[eng]
        s = self.sem[eng]
        self.q[eng].append((fn, waits, (s, 1)))
        rec = (s, v, eng)
        for b in reads:
            if len(b.reads) > 24:
                b.reads = b.reads[-24:] if False else b.reads
            b.reads.append(rec)
        for b in writes:
            b.last_write = rec
            b.reads = []
        self.ninst += 1

    def dma(self, eng, out_ap, in_ap, reads=(), writes=(), sembuf=None):
        sembuf = sembuf or (writes[0] if writes else reads[0])
        ds = self.dsem_for(sembuf)
        waits = self._deps(eng, reads, writes)
        sembuf.dcnt += 16
        v = sembuf.dcnt
        self.q[eng].append((lambda e: e.dma_start(out=out_ap, in_=in_ap), waits, (ds, 16)))
        rec = (ds, v, "dma")
        for b in reads:
            b.reads.append(rec)
        for b in writes:
            b.last_write = rec
            b.reads = []
        self.ninst += 1

    def final_wait(self, eng, bufs):
        waits = self._deps(eng, bufs, bufs)
        self.q[eng].append((None, waits, None))

    def finish(self):
        nc = self.nc
        q = self.q

        def replay(name):
            def f(e):
                for fn, waits, inc in q[name]:
                    for s, v in waits:
                        e.wait_ge(s, v)
                    if fn is None:
                        continue
                    ins = fn(e)
                    if inc is not None:
                        ins.then_inc(inc[0], inc[1])
            return f

        with nc.Block() as block:
            block.tensor(replay("pe"))
            block.scalar(replay("act"))
            block.vector(replay("dve"))
            block.gpsimd(replay("pool"))
            block.sync(replay("sp"))
        self.stack.close()


def _bufs(*xs):
    out = []
    for x in xs:
        if isinstance(x, V) and x.b not in out:
            out.append(x.b)
    return out


def _ap(x):
    return x.ap if isinstance(x, V) else x


def slabify(W, wc):
    K, N = W.shape
    kc = K // 128
    return np.ascontiguousarray(
        W.reshape(kc, 128, N // wc, wc).transpose(2, 1, 0, 3).reshape(N // wc, 128, kc * wc))


def colvec(v):
    return np.ascontiguousarray(v.reshape(-1, 128).T)


def pad_cols(W, n):
    out = np.zeros((W.shape[0], n), np.float32)
    out[:, :W.shape[1]] = W
    return out


RW0 = 0
GLA0 = 1824
GDN0 = 3376
GATE0 = 5432
NFM = 42
PV_LNG, PV_LNB = 0, 32
PV_MU = 64
PV_A0, PV_KK, PV_KA, PV_RK, PV_GNG, PV_GNB = 80, 84, 88, 92, 96, 100
PV_GLAN, PV_GDNN = 104, 105
PV_CONV = 106
PV_ALOG, PV_DTB = 154, 158
NPV = 162
CI_ID, CI_ONE, CI_NEGONE, CI_LN, CI_V128, CI_BLK, CI_TLE, CI_TLT, CI_TGT, CI_NTLE, CI_MBSL, CI_MBIU = range(12)
NCI = 12
CB_ID, CB_MBSL, CB_MBIU = 0, 1, 2


def build_consts():
    p = np.arange(128)[:, None]
    j = np.arange(128)[None, :]
    m = np.zeros((NCI, 128, 128), np.float32)
    m[CI_ID] = (p == j)
    m[CI_ONE] = 1.0
    m[CI_NEGONE] = -1.0
    m[CI_LN] = 1.0 / 1024
    m[CI_V128] = 1.0 / 128
    m[CI_BLK] = ((p // 64) == (j // 64))
    m[CI_TLE] = (p <= j)
    m[CI_TLT] = (p < j)
    m[CI_TGT] = (p > j)
    m[CI_NTLE] = -(p <= j).astype(np.float32)
    m[CI_MBSL] = np.where(p > j, 0.0, -30000.0)
    m[CI_MBIU] = np.where(p <= j, 0.0, -30000.0)
    cf = np.ascontiguousarray(m.transpose(1, 0, 2).reshape(128, NCI * 128))
    cbm = np.stack([m[CI_ID], m[CI_MBSL], m[CI_MBIU]], 0)
    cb = np.ascontiguousarray(cbm.transpose(1, 0, 2).reshape(128, 3 * 128))
    return cf, cb


def regroup_win(w_in):
    fm = np.zeros((1024, NFM * 128), np.float32)

    def put(ch, src, n):
        fm[:, ch * 128: ch * 128 + n] = w_in[:, src: src + n]
    put(0, RW0 + 1536, 128)
    put(1, RW0 + 1664, 128)
    put(2, RW0 + 1792, 32)
    for j in range(4):
        put(4 + 3 * j, RW0 + j * 128, 128)
        put(5 + 3 * j, RW0 + 512 + j * 128, 128)
        put(6 + 3 * j, RW0 + 1024 + j * 128, 128)
    put(16, GLA0 + 1024, 16)
    for j in range(2):
        put(18 + 4 * j, GLA0 + j * 128, 128)
        put(19 + 4 * j, GLA0 + 256 + j * 128, 128)
        put(20 + 4 * j, GLA0 + 1040 + (2 * j) * 128, 128)
        put(21 + 4 * j, GLA0 + 1040 + (2 * j + 1) * 128, 128)
    for h in range(4):
        put(26 + 4 * h, GDN0 + h * 128, 128)
        put(27 + 4 * h, GDN0 + 512 + h * 128, 128)
        put(28 + 4 * h, GDN0 + 1024 + h * 128, 128)
        put(29 + 4 * h, GDN0 + 1544 + h * 128, 128)
    tm = np.zeros((1024, 6 * 256), np.float32)
    tm[:, 0:512] = w_in[:, RW0 + 1024: RW0 + 1536]
    tm[:, 512:768] = w_in[:, GLA0 + 256: GLA0 + 512]
    tm[:, 768:1280] = w_in[:, GLA0 + 512: GLA0 + 1024]
    tm[:, 1280:1288] = w_in[:, GDN0 + 1536: GDN0 + 1544]
    gate = w_in[:, GATE0: GATE0 + 3072]
    return fm, tm, gate


def prep_layer(inp, l, plan):
    A = {}
    for f in range(2):
        s1 = slabify(inp["ffn_w1"][l, f], 256)
        s3 = slabify(inp["ffn_w3"][l, f], 256)
        for g in range(11):
            A[("w1", f, g)] = s1[g]
            A[("w3", f, g)] = s3[g]
    fm, tm, gate = regroup_win(inp["w_in"][l])
    for i, s in enumerate(slabify(fm, 256)):
        A[("fm", i)] = s
    for i, s in enumerate(slabify(tm, 256)):
        A[("tm", i)] = s
    for i, s in enumerate(slabify(gate, 256)):
        A[("gate", i)] = s
    for b in range(3):
        sb = slabify(inp["w_branch"][l, b], 512)
        for half in range(2):
            A[("br", b, half)] = sb[half]
    for i, s in enumerate(slabify(inp["w_o"][l], 256)):
        A[("wo", i)] = s
    for i, s in enumerate(slabify(inp["ple_w_gate"][l], 256)):
        A[("pg", i)] = s
    A[("pp", 0)] = slabify(inp["ple_w_proj"][l], 1024)[0]
    lst = [A[k] for k in plan]
    while len(lst) < NA_PER_LAYER:
        lst.append(lst[0])
    wA = np.ascontiguousarray(np.stack(lst, 0))
    wB = np.ascontiguousarray(np.concatenate([slabify(inp["ffn_w2"][l, f], 128) for f in range(2)], 0))
    pv = np.zeros((128, NPV), np.float32)
    for i in range(4):
        pv[:, PV_LNG + i * 8: PV_LNG + i * 8 + 8] = colvec(inp["ln_g"][l, i])
        pv[:, PV_LNB + i * 8: PV_LNB + i * 8 + 8] = colvec(inp["ln_b"][l, i])
    mu = inp["rw_mu"][l]
    pv[:, PV_MU + 0] = mu[1536:1664]
    pv[:, PV_MU + 1] = mu[1664:1792]
    pv[0:32, PV_MU + 2] = mu[1792:1824]
    for j in range(4):
        pv[:, PV_MU + 4 + 3 * j] = mu[j * 128:(j + 1) * 128]
        pv[:, PV_MU + 5 + 3 * j] = mu[512 + j * 128: 512 + (j + 1) * 128]
        pv[:, PV_MU + 6 + 3 * j] = mu[1024 + j * 128: 1024 + (j + 1) * 128]
    pv[:, PV_A0: PV_A0 + 4] = colvec(inp["rw_a0"][l])
    pv[:, PV_KK: PV_KK + 4] = colvec(inp["rw_k_k"][l])
    pv[:, PV_KA: PV_KA + 4] = colvec(inp["rw_k_a"][l])
    pv[:, PV_RK: PV_RK + 4] = colvec(inp["rw_r_k"][l].reshape(-1))
    pv[:, PV_GNG: PV_GNG + 4] = colvec(inp["rw_gn_g"][l])
    pv[:, PV_GNB: PV_GNB + 4] = colvec(inp["rw_gn_b"][l])
    pv[:, PV_GLAN] = inp["gla_norm_g"][l]
    pv[:, PV_GDNN] = inp["gdn_norm_g"][l]
    for tap in range(4):
        pv[:, PV_CONV + tap * 12: PV_CONV + tap * 12 + 12] = colvec(inp["gdn_conv_w"][l, tap])
    pv[:, PV_ALOG: PV_ALOG + 4] = inp["gdn_a_log"][l][None, :]
    pv[:, PV_DTB: PV_DTB + 4] = inp["gdn_dt_bias"][l][None, :]
    sw = np.zeros((128, 2304), np.float32)
    sw[0:64, 0:512] = inp["rw_w2"][l]
    sw[64, 0:512] = inp["rw_w0"][l]
    sw[64:128, 512:1024] = inp["rw_a2"][l]
    sw[0:128, 1024:1536] = inp["rw_g2"][l][0:128]
    sw[0:32, 1536:2048] = inp["rw_g2"][l][128:160]
    sw[0:16, 2048:2304] = inp["gla_gk_w2"][l]
    sw[32, 2048:2304] = inp["gla_gk_b"][l]
    muv = np.ascontiguousarray(np.broadcast_to(mu[1024:1536][None, :], (128, 512))).astype(np.float32)
    return wA, wB, pv, sw, muv


NA_PER_LAYER = 22 + 21 + 6 + 18 + 4 + 22 + 4 + 1


class Prog:
    def __init__(self, T, layers, debug=(), stages=4, mixsel="rgd"):
        self.stages = stages
        self.mixsel = mixsel
        self.T = T
        self.NT = T // TT
        self.layers = list(layers)
        self.debug = set(debug)
        nc = self.nc = bass.Bass("TRN2", target_bir_lowering=False)
        self.xT = nc.dram_tensor("xT", [D, T], F32, kind="ExternalInput").ap()
        self.pT = nc.dram_tensor("pT", [2, 256, T], F32, kind="ExternalInput").ap()
        self.wA = nc.dram_tensor("wA", [2, NA_PER_LAYER, 128, 2048], F32, kind="ExternalInput").ap()
        self.wB = nc.dram_tensor("wB", [2, 16, 128, 2816], F32, kind="ExternalInput").ap()
        self.pvd = nc.dram_tensor("pv", [2, 128, NPV], F32, kind="ExternalInput").ap()
        self.swd = nc.dram_tensor("sw", [2, 128, 2304], F32, kind="ExternalInput").ap()
        self.muvd = nc.dram_tensor("muv", [2, 128, 512], F32, kind="ExternalInput").ap()
        self.cfd = nc.dram_tensor("cf", [128, NCI * 128], F32, kind="ExternalInput").ap()
        self.cbd = nc.dram_tensor("cb", [128, 3 * 128], F32, kind="ExternalInput").ap()
        self.outT = nc.dram_tensor("outT", [D, T], F32, kind="ExternalOutput").ap()
        self.dbg_out = {}
        self.plan = []
        self.S = Sched(nc)
        self.build()

    def mm(self, out, lhsT, rhs, start=True, stop=True):
        o, a, b = out.ap, lhsT.ap, rhs.ap
        self.S.op("pe", lambda e: e.matmul(o, lhsT=a, rhs=b, start=start, stop=stop),
                  reads=_bufs(lhsT, rhs), writes=[out.b])

    def tr(self, out, in_, ident):
        o, a, b = out.ap, in_.ap, ident.ap
        self.S.op("pe", lambda e: e.transpose(o, a, b), reads=_bufs(in_, ident), writes=[out.b])

    def act(self, out, in_, func, bias=0.0, scale=1.0):
        o, a, bi, sc = out.ap, in_.ap, _ap(bias), _ap(scale)
        self.S.op("act", lambda e: e.activation(out=o, in_=a, func=func, bias=bi, scale=sc),
                  reads=_bufs(in_, bias, scale), writes=[out.b])

    def tt(self, eng, out, a, b, op):
        o, x, y = out.ap, a.ap, b.ap
        self.S.op(eng, lambda e: e.tensor_tensor(out=o, in0=x, in1=y, op=op), reads=_bufs(a, b), writes=[out.b])

    def ts(self, eng, out, a, s1, s2, op0, op1=None):
        o, x, p1, p2 = out.ap, a.ap, _ap(s1), _ap(s2)
        if op1 is None:
            self.S.op(eng, lambda e: e.tensor_scalar(out=o, in0=x, scalar1=p1, scalar2=None, op0=op0),
                      reads=_bufs(a, s1), writes=[out.b])
        else:
            self.S.op(eng, lambda e: e.tensor_scalar(out=o, in0=x, scalar1=p1, scalar2=p2, op0=op0, op1=op1),
                      reads=_bufs(a, s1, s2), writes=[out.b])

    def stt(self, eng, out, a, s, b, op0, op1):
        o, x, sc, y = out.ap, a.ap, _ap(s), b.ap
        self.S.op(eng, lambda e: e.scalar_tensor_tensor(out=o, in0=x, scalar=sc, in1=y, op0=op0, op1=op1),
                  reads=_bufs(a, s, b), writes=[out.b])

    def cp(self, eng, out, a):
        o, x = out.ap, a.ap
        if eng == "act":
            self.S.op("act", lambda e: e.copy(out=o, in_=x), reads=[a.b], writes=[out.b])
        else:
            self.S.op(eng, lambda e: e.tensor_copy(out=o, in_=x), reads=[a.b], writes=[out.b])

    def memset(self, eng, out, val):
        o = out.ap
        self.S.op(eng, lambda e: e.memset(o, val), writes=[out.b])

    def recip(self, out, a):
        o, x = out.ap, a.ap
        self.S.op("dve", lambda e: e.reciprocal(out=o, in_=x), reads=[a.b], writes=[out.b])

    def ev(self):
        self.S.rr ^= 1
        return "dve" if self.S.rr else "act"

    def copy_any(self, out, a):
        self.cp(self.ev(), out, a)

    def bank(self):
        self._bi = (self._bi + 1) % len(self.MM)
        return self.MM[self._bi]

    def bankx(self):
        import os
        if os.environ.get("NO_BANKX"):
            return self.bank()
        self._bxi = (self._bxi + 1) % len(self.MMX)
        return self.MMX[self._bxi]

    def quarter(self):
        self._qi = (self._qi + 1) % len(self.Q)
        return self.Q[self._qi]

    def tbq(self):
        self._ti = (self._ti + 1) % len(self.TB)
        return self.TB[self._ti]

    def slabA(self, l, key):
        idx = self._ai[l]
        self._ai[l] += 1
        if idx >= len(self.plan):
            self.plan.append(key)
        assert self.plan[idx] == key, (idx, key, self.plan[idx])
        slot = self.ring[self._ri % NSLOT]
        self._ri += 1
        self.S.dma("pool", slot.t[:, 0:2048], self.wA[l, idx], writes=[slot])
        return slot

    def slabB(self, l, idx):
        slot = self.ring[self._ri % NSLOT]
        self._ri += 1
        self.S.dma("pool", slot.t[:, 0:2816], self.wB[l, idx], writes=[slot])
        return slot

    def fm_slab(self, l, g):
        if self._fmg[0] != (l, g):
            self._fmg = ((l, g), self.slabA(l, ("fm", g)))
        return self._fmg[1]

    def fm_chunk(self, l, ch):
        slab = self.fm_slab(l, ch // 2)
        return self.fm_chunk_psum(slab, ch % 2)

    def dump(self, name, view, shape):
        if name not in self.debug:
            return
        key = f"dbg_{name}_{len(self.dbg_out)}"
        d = self.nc.dram_tensor(key, list(shape), F32, kind="ExternalOutput").ap()
        self.dbg_out[key] = name
        if not self._dbg_bufs:
            self._dbg_bufs.append(self.S.sbuf(shape, F32, "dbgt"))
        tmp = self._dbg_bufs[0]
        self.cp("dve", tmp[:], view)
        self.S.dma("sp", d, tmp.t[:], reads=[tmp], sembuf=tmp)

    def build(self):
        S = self.S
        self._bi = self._qi = self._ti = 0
        self._ri = 0
        self._fmg = (None, None)
        self._dbg_bufs = []
        self.MM = [S.psum([128, 512], F32, "mm") for _ in range(3)]
        qbanks = [S.psum([128, 512], F32, "qb") for _ in range(4)]
        self.MMX = [self.MM[0], qbanks[0], self.MM[1], qbanks[1], self.MM[2], qbanks[2]]
        self._bxi = 0
        self.Q = [Buf(qbanks[i % 3].t, qbanks[i % 3].name + f"_{i}", c0=(i // 3) * 128, owner=qbanks[i % 3])
                  for i in range(12)]
        self.QF = [Buf(qbanks[3].t, qbanks[3].name + f"_f{i}", c0=i * 128, owner=qbanks[3]) for i in range(4)]
        tbb = S.psum([128, 1024], BF16, "tbb")
        self.TB = [Buf(tbb.t, tbb.name + f"_{i}", c0=i * 128, owner=tbb) for i in range(4)]
        cf = self.cf = S.sbuf([128, NCI, 128], F32, "cf")
        cb = self.cb = S.sbuf([128, 3, 128], BF16, "cb")
        S.dma("sp", cf.t[:], self.cfd.rearrange("p (n j) -> p n j", n=NCI), writes=[cf])
        S.dma("pool", cb.t[:], self.cbd.rearrange("p (n j) -> p n j", n=3), writes=[cb])
        self.pv = S.sbuf([128, 2, NPV], F32, "pv")
        S.dma("sp", self.pv.t[:], self.pvd.rearrange("l p n -> p l n"), writes=[self.pv])
        self.sw = S.sbuf([128, 2, 2304], BF16, "sw")
        S.dma("pool", self.sw.t[:], self.swd.rearrange("l p n -> p l n"), writes=[self.sw])
        self.epsb = S.sbuf([128, 4], F32, "epsb")
        self.memset("dve", self.epsb[:, 0:1], LN_EPS)
        self.memset("dve", self.epsb[:, 1:2], 1e-6)
        self.memset("dve", self.epsb[:, 2:3], 64e-5)
        self.memset("dve", self.epsb[:, 3:4], 1.0)
        self.nega = S.sbuf([128, 2, 4], F32, "nega")
        for l in range(2):
            self.act(self.nega[:, l, :], self.pv[:, l, PV_ALOG:PV_ALOG + 4], AF.Exp)
            self.ts("dve", self.nega[:, l, :], self.nega[:, l, :], -1.0, None, ALU.mult)
        self.ring = [S.sbuf([128, SLOT], BF16, "ring") for _ in range(NSLOT)]
        self.x = [S.sbuf([128, TT], F32, "x") for _ in range(8)]
        self.pool = [S.sbuf([128, TT], F32, "pool") for _ in range(8)]
        self.z = self.pool
        self.xb = [S.sbuf([128, TT], BF16, "xb") for _ in range(8)]
        self.G = [S.sbuf([128, TT], BF16, "G") for _ in range(22)]
        self.br = [self.G[0:4], self.G[4:8], self.G[8:12]]
        self.merged = self.G[12:20]
        self.dxb = self.G[12:20]
        self.pbt = self.G[20:22]
        self.t32 = [S.sbuf([128, TT], F32, "t32") for _ in range(5)]
        self._t32i = 0
        self.xcar = S.sbuf([128, 2, 8], BF16, "xcar")
        self.memset("dve", self.xcar[:], 0.0)
        self.hcar = S.sbuf([128, 2, 28, 3], F32, "hcar")
        self.memset("dve", self.hcar[:], 0.0)
        self.Srw = S.sbuf([128, 2, 4, 128], F32, "Srw")
        self.Srwb = S.sbuf([128, 2, 4, 128], BF16, "Srwb")
        self.Sgl = S.sbuf([128, 2, 2, 128], F32, "Sgl")
        self.Sglb = S.sbuf([128, 2, 2, 128], BF16, "Sglb")
        self.Sgd = S.sbuf([128, 2, 4, 128], F32, "Sgd")
        self.Sgdb = S.sbuf([128, 2, 4, 128], BF16, "Sgdb")
        for s_ in (self.Srw, self.Srwb, self.Sgl, self.Sglb, self.Sgd, self.Sgdb):
            self.memset("dve", s_[:], 0.0)
        self.alloc_mixer()
        self._ai = {0: 0, 1: 0}
        for ti in range(self.NT):
            t0 = ti * TT
            for m in range(8):
                S.dma("sp", self.x[m].t[:], self.xT[m * 128:(m + 1) * 128, t0:t0 + TT], writes=[self.x[m]])
            for l in self.layers:
                self._ai[l] = 0
                self.layer_tile(l, ti)
                assert self._ai[l] == NA_PER_LAYER or self.mixsel != "rgd", self._ai[l]
            for m in range(8):
                S.dma("sp", self.outT[m * 128:(m + 1) * 128, t0:t0 + TT], self.x[m].t[:], reads=[self.x[m]],
                      sembuf=self.x[m])
        S.final_wait("sp", self.x + self._dbg_bufs)
        S.finish()

    def tmp32(self):
        self._t32i = (self._t32i + 1) % len(self.t32)
        return self.t32[self._t32i]

    def make_xb(self):
        for m in range(8):
            self.copy_any(self.xb[m][:], self.x[m][:])

    def ffn(self, l, f):
        for g in range(11):
            w1 = self.slabA(l, ("w1", f, g))
            w3 = self.slabA(l, ("w3", f, g))
            for cc in range(2):
                c = g * 2 + cc
                p1 = self.bankx()
                p3 = self.bankx()
                for k in range(8):
                    self.mm(p1[:], w1[:, k * 256 + cc * 128: k * 256 + cc * 128 + 128], self.xb[k][:], k == 0, k == 7)
                for k in range(8):
                    self.mm(p3[:], w3[:, k * 256 + cc * 128: k * 256 + cc * 128 + 128], self.xb[k][:], k == 0, k == 7)
                t = self.tmp32()
                self.act(t[:], p1[:], AF.Silu)
                self.stt("dve", self.G[c][:], t[:], 0.5, p3[:], ALU.mult, ALU.mult)
        for m in range(8):
            w2 = self.slabB(l, f * 8 + m)
            p = self.bankx()
            for c in range(22):
                self.mm(p[:], w2[:, c * 128:(c + 1) * 128], self.G[c][:], c == 0, c == 21)
            self.stt("dve", self.z[m][:], self.x[m][:], ALPHA, p[:], ALU.mult, ALU.add)

    def ln(self, l, i):
        pm = self.bank()
        pq = self.bank()
        lnm = self.cf[:, CI_LN, :]
        for m in range(8):
            self.mm(pm[:], lnm, self.z[m][:], m == 0, m == 7)
        for m in range(8):
            t = self.tmp32()
            self.act(t[:], self.z[m][:], AF.Square)
            self.mm(pq[:], lnm, t[:], m == 0, m == 7)
        mean = self.lnmean
        rstd = self.lnrstd
        self.cp("act", mean[:], pm[:])
        self.tt("dve", rstd[:], mean[:], mean[:], ALU.mult)
        self.tt("dve", rstd[:], pq[:], rstd[:], ALU.subtract)
        self.act(rstd[:], rstd[:], AF.Ln, bias=self.epsb[:, 0:1])
        self.act(rstd[:], rstd[:], AF.Exp, scale=-0.5)
        for m in range(8):
            t = self.tmp32()
            self.tt("dve", t[:], self.z[m][:], mean[:], ALU.subtract)
            self.tt("dve", t[:], t[:], rstd[:], ALU.mult)
            self.act(self.x[m][:], t[:], AF.Identity, bias=self.pv[:, l, PV_LNB + i * 8 + m: PV_LNB + i * 8 + m + 1],
                     scale=self.pv[:, l, PV_LNG + i * 8 + m: PV_LNG + i * 8 + m + 1])

    def layer_tile(self, l, ti):
        st = self.stages
        self.make_xb()
        self.ffn(l, 0)
        self.ln(l, 0)
        if st < 2:
            self._ai[l] = NA_PER_LAYER
            return
        self.make_xb()
        self.mixers(l, ti)
        self.ln(l, 1)
        if st < 3:
            self._ai[l] = NA_PER_LAYER
            return
        self.make_xb()
        self.ffn(l, 1)
        self.ln(l, 2)
        if st < 4:
            self._ai[l] = NA_PER_LAYER
            return
        self.make_xb()
        self.ple(l, ti)
        self.ln(l, 3)

    def ple(self, l, ti):
        t0 = ti * TT
        for k in range(2):
            pin = self.tmp32()
            self.S.dma("sp", pin.t[:], self.pT[l, k * 128:(k + 1) * 128, t0:t0 + TT], writes=[pin])
            self.copy_any(self.pbt[k][:], pin[:])
        slabs = [self.slabA(l, ("pg", i)) for i in range(4)]
        wp = self.slabA(l, ("pp", 0))
        for m in range(8):
            pg = self.bankx()
            pp = self.bankx()
            sl = slabs[m // 2]
            for k in range(8):
                self.mm(pg[:], sl[:, k * 256 + (m % 2) * 128: k * 256 + (m % 2) * 128 + 128], self.xb[k][:], k == 0, k == 7)
            for k in range(2):
                self.mm(pp[:], wp[:, k * 1024 + m * 128: k * 1024 + m * 128 + 128], self.pbt[k][:], k == 0, k == 1)
            t = self.tmp32()
            self.act(t[:], pg[:], AF.Sigmoid)
            self.tt("dve", t[:], t[:], pp[:], ALU.mult)
            self.stt("dve", self.z[m][:], self.x[m][:], ALPHA, t[:], ALU.mult, ALU.add)

    def alloc_mixer(self):
        S = self.S
        self.lnmean = S.sbuf([128, TT], F32, "lnmean")
        self.lnrstd = S.sbuf([128, TT], F32, "lnrstd")
        self.raw = [S.sbuf([128, 3 + TT], F32, "raw") for _ in range(2)]
        self._rawi = 0
        self.rw_tw = S.sbuf([128, TT], BF16, "rw_tw")
        self.rw_al = S.sbuf([128, TT], BF16, "rw_al")
        self.rw_gs = S.sbuf([128, 2, TT], BF16, "rw_gs")
        self.rw_kk = S.sbuf([128, TT], BF16, "rw_kk")
        self.rw_b = S.sbuf([128, TT], BF16, "rw_b")
        self.vt16 = S.sbuf([128, 4, 256], BF16, "vt16")
        self.kt16 = S.sbuf([128, 4, 128], BF16, "kt16")
        self.muvj = S.sbuf([128, 128], F32, "muvj")
        self.vpad = S.sbuf([128, 4, 2, 128], BF16, "vpad")
        self.memset("dve", self.vpad[:], 0.0)
        self.kpad = [S.sbuf([128, 128], BF16, "kpad") for _ in range(2)]
        for u_ in self.kpad:
            self.memset("dve", u_[:], 0.0)
        self.upad = [S.sbuf([128, 128], BF16, "upad") for _ in range(2)]
        for u_ in self.upad:
            self.memset("dve", u_[:], 0.0)
        self.gl_lo = S.sbuf([128, TT], BF16, "gl_lo")
        self.gl_l = S.sbuf([128, 4, 256], F32, "gl_l")
        self.gl_kt = S.sbuf([128, 4, 256], F32, "gl_kt")
        self.gd_ab = S.sbuf([128, 4, 8], F32, "gd_ab")
        self.gd_sc = S.sbuf([128, 4, 24], F32, "gd_sc")
        self.s16 = [S.sbuf([128, 128], BF16, "s16") for _ in range(26)]
        self.l16 = [S.sbuf([128, 128], BF16, "l16") for _ in range(26)]
        self.s32 = [S.sbuf([128, 128], F32, "s32") for _ in range(12)]
        self._s16i = self._l16i = self._s32i = 0
        self.s16w = [S.sbuf([128, 256], BF16, "s16w") for _ in range(4)]
        self._s16wi = 0

    def b16(self):
        self._s16i = (self._s16i + 1) % len(self.s16)
        return self.s16[self._s16i]

    def L16(self):
        self._l16i = (self._l16i + 1) % len(self.l16)
        return self.l16[self._l16i]

    def b32(self):
        self._s32i = (self._s32i + 1) % len(self.s32)
        return self.s32[self._s32i]

    def b16w(self):
        self._s16wi = (self._s16wi + 1) % len(self.s16w)
        return self.s16w[self._s16wi]

    def fm_chunk_psum(self, slab, cc, wide=False):
        p = self.bankx() if wide else self.bank()
        for k in range(8):
            self.mm(p[:], slab[:, k * 256 + cc * 128: k * 256 + cc * 128 + 128], self.xb[k][:], k == 0, k == 7)
        return p

    def raw_tile(self, l, ci, p):
        self._rawi = (self._rawi + 1) % len(self.raw)
        r = self.raw[self._rawi]
        self.cp("dve", r[:, 0:3], self.hcar[:, l, ci, :])
        self.cp("act", r[:, 3:3 + TT], p[:])
        self.cp("dve", self.hcar[:, l, ci, :], r[:, TT:TT + 3])
        return r

    def shift_chunk(self, l, ci, p, out):
        r = self.raw_tile(l, ci, p)
        t = self.tmp32()
        self.tt("dve", t[:], r[:, 2:2 + TT], r[:, 3:3 + TT], ALU.subtract)
        mu = self.pv[:, l, PV_MU + ci: PV_MU + ci + 1]
        self.stt("dve", out, t[:], mu, r[:, 3:3 + TT], ALU.mult, ALU.add)

    def mixers(self, l, ti):
        for k in range(8):
            self.tt("dve", self.dxb[k][:, 1:TT], self.xb[k][:, 0:TT - 1], self.xb[k][:, 1:TT], ALU.subtract)
            self.tt("dve", self.dxb[k][:, 0:1], self.xcar[:, l, k:k + 1], self.xb[k][:, 0:1], ALU.subtract)
            self.cp("dve", self.xcar[:, l, k:k + 1], self.xb[k][:, TT - 1:TT])
        for name, fn, b in (("r", self.rwkv, 0), ("g", self.gla, 1), ("d", self.gdn, 2)):
            if name in self.mixsel:
                fn(l)
            else:
                for j in range(4):
                    self.memset("dve", self.br[b][j][:], 0.0)
        if self.mixsel != "rgd":
            pass
        self.merge(l)

    def tri_inverse_multi(self, XPs):
        ident_f = self.cf[:, CI_ID, :]
        st = []
        for X, P in XPs:
            Z = self.b16()
            self.tt("dve", Z[:], X, ident_f, ALU.add)
            st.append([X, P, Z])
        for lvl in range(7):
            need_p = lvl <= 5
            need_x = lvl <= 4
            need_z = lvl >= 1
            pqs, xqs, zqs = [], [], []
            for c in st:
                if need_z:
                    zq = self.quarter()
                    self.mm(zq[:], c[1], c[2][:])
                    zqs.append(zq)
                if need_p:
                    pq = self.quarter()
                    self.mm(pq[:], c[0], c[1])
                    pqs.append(pq)
                if need_x:
                    xq = self.quarter()
                    self.mm(xq[:], c[1], c[0])
                    xqs.append(xq)
            for ci, c in enumerate(st):
                if need_z:
                    Zn = self.b16()
                    self.tt("dve", Zn[:], zqs[ci][:], c[2][:], ALU.add)
                    c[2] = Zn
                if need_p:
                    Pn = self.b16()
                    self.cp("act", Pn[:], pqs[ci][:])
                if need_x:
                    Xn = self.b16()
                    self.cp("act" if need_z else "dve", Xn[:], xqs[ci][:])
                    c[0] = Xn[:]
                if need_p:
                    c[1] = Pn[:]
        return [c[2] for c in st]

    def tri_inverse(self, X, P):
        return self.tri_inverse_multi([(X, P)])[0]

    def rwkv(self, l):
        cf, cb, sw = self.cf, self.cb, self.sw
        pvl = lambda c0, j: self.pv[:, l, c0 + j: c0 + j + 1]
        blk = cf[:, CI_BLK, :]
        idb = cb[:, CB_ID, :]
        KD = -float(np.exp(-0.5))
        t = self.tmp32()
        self.shift_chunk(l, 0, self.fm_chunk(l, 0), t[:])
        self.act(self.rw_tw[0:64, :], t[0:64, :], AF.Tanh)
        self.memset("dve", self.rw_tw[64:65, :], 1.0)
        self.cp("dve", self.rw_al[:], t[:])
        for jj in range(2):
            t = self.tmp32()
            self.shift_chunk(l, 1 + jj, self.fm_chunk(l, 1 + jj), t[:])
            self.act(self.rw_gs[:, jj, :], t[:], AF.Sigmoid)
        r, k, v, y = self.pool[0], self.pool[1], self.pool[2], self.pool[3]
        tmv = None
        for j in range(4):
            self.shift_chunk(l, 4 + 3 * j, self.fm_chunk(l, 4 + 3 * j), r[:])
            self.shift_chunk(l, 5 + 3 * j, self.fm_chunk(l, 5 + 3 * j), k[:])
            self.shift_chunk(l, 6 + 3 * j, self.fm_chunk(l, 6 + 3 * j), v[:])
            if j % 2 == 0:
                tmv = self.slabA(l, ("tm", j // 2))
            self.S.dma("sp", self.muvj.t[:], self.muvd[l, :, j * 128:(j + 1) * 128], writes=[self.muvj])
            for s in range(4):
                ts_ = slice(s * C, (s + 1) * C)
                q1 = self.quarter()
                q2 = self.quarter()
                c0 = (j % 2) * 128
                for kk_ in range(8):
                    self.mm(q1[:], self.xb[kk_][:, ts_], tmv[:, kk_ * 256 + c0: kk_ * 256 + c0 + 128], kk_ == 0, kk_ == 7)
                for kk_ in range(8):
                    self.mm(q2[:], self.dxb[kk_][:, ts_], tmv[:, kk_ * 256 + c0: kk_ * 256 + c0 + 128], kk_ == 0, kk_ == 7)
                tq_ = self.b32()
                self.tt("dve", tq_[:], q2[:], self.muvj[:], ALU.mult)
                self.tt("dve", self.vpad[:, s, 0, 0:64], tq_[:, 0:64], q1[:, 0:64], ALU.add)
                self.tt("dve", self.vpad[:, s, 1, 64:128], tq_[:, 64:128], q1[:, 64:128], ALU.add)
            p = self.bank()
            self.mm(p[:], sw[:, l, 512 + j * 128: 512 + (j + 1) * 128], self.rw_al[:])
            a = self.tmp32()
            self.act(a[:], p[:], AF.Sigmoid, bias=pvl(PV_A0, j))
            t1 = self.tmp32()
            self.ts("dve", t1[:], k[:], pvl(PV_KK, j), None, ALU.mult)
            sq = self.tmp32()
            self.act(sq[:], t1[:], AF.Square)
            p = self.bank()
            self.mm(p[:], blk, sq[:])
            rn = sq
            self.act(rn[:], p[:], AF.Ln, bias=self.epsb[:, 1:2])
            self.act(rn[:], rn[:], AF.Exp, scale=-0.5)
            self.tt("dve", self.rw_kk[:], t1[:], rn[:], ALU.mult)
            self.tt("dve", self.rw_b[:], self.rw_kk[:], a[:], ALU.mult)
            self.ts("dve", a[:], a[:], -1.0, pvl(PV_KA, j), ALU.add, ALU.mult)
            self.stt("dve", k[:], a[:], 1.0, k[:], ALU.add, ALU.mult)
            t3 = self.tmp32()
            self.stt("dve", t3[:], r[:], pvl(PV_RK, j), k[:], ALU.mult, ALU.mult)
            p = self.bank()
            self.mm(p[:], blk, t3[:])
            self.tt("dve", v[:], p[:], v[:], ALU.mult)
            for s in range(4):
                ts_ = slice(s * C, (s + 1) * C)
                qz = self.quarter()
                self.mm(qz[:], self.rw_tw[0:65, ts_], sw[0:65, l, j * 128:(j + 1) * 128])
                sg = self.b32()
                self.act(sg[:], qz[:], AF.Sigmoid)
                qc = self.quarter()
                self.mm(qc[:], sg[:], cf[:, CI_TLE, :])
                qp = self.quarter()
                self.mm(qp[:], sg[:], cf[:, CI_TLT, :])
                E1 = self.b32()
                self.act(E1[:], qc[:], AF.Exp, scale=KD)
                Ei = self.b32()
                self.act(Ei[:], qc[:], AF.Exp, scale=-KD)
                E0 = self.b32()
                self.act(E0[:], qp[:], AF.Exp, scale=KD)
                ARt = self.b16w()
                self.stt("dve", ARt[:, 0:128], self.rw_kk[:, ts_], -1.0, E0[:], ALU.mult, ALU.mult)
                self.tt("dve", ARt[:, 128:256], r[:, ts_], E1[:], ALU.mult)
                Bt = self.L16()
                self.tt("dve", Bt[:], self.rw_b[:, ts_], Ei[:], ALU.mult)
                Kt = self.L16()
                self.tt("dve", Kt[:], k[:, ts_], Ei[:], ALU.mult)
                tq = self.tbq()
                self.tr(tq[:], Bt[:], idb)
                Btm = self.L16()
                self.copy_any(Btm[:], tq[:])
                tq = self.tbq()
                self.tr(tq[:], Kt[:], idb)
                Ktm = self.L16()
                self.copy_any(Ktm[:], tq[:])
                hm = (cf[:, CI_BLK, 0:1], cf[:, CI_BLK, 64:65])
                Sblk = self.Srwb[:, l, j, :]
                qy = self.QF[0]
                self.mm(qy[:], Sblk, ARt[:, 128:256], True, False)
                pre = []
                for hh in range(2):
                    Bth = self.L16()
                    self.ts("dve", Bth[:], Bt[:], hm[hh], None, ALU.mult)
                    Kth = self.L16()
                    self.ts("dve", Kth[:], Kt[:], hm[hh], None, ALU.mult)
                    Ath = self.L16()
                    self.ts("dve", Ath[:], ARt[:, 0:128], hm[hh], None, ALU.mult)
                    pb_ = self.bank()
                    self.mm(pb_[:, 0:256], Bth[:], ARt[:])
                    self.mm(pb_[:, 256:512], Kth[:], ARt[:])
                    qP = self.quarter()
                    self.mm(qP[:], Ath[:], Bt[:])
                    X = self.L16()
                    self.tt("dve", X[:], pb_[:, 0:128], cf[:, CI_TLT, :], ALU.mult)
                    NBT = self.L16()
                    self.tt("dve", NBT[:], pb_[:, 128:256], cf[:, CI_TLE, :], ALU.mult)
                    MKT = self.L16()
                    self.tt("dve", MKT[:], pb_[:, 256:384], cf[:, CI_TLT, :], ALU.mult)
                    NKT = self.L16()
                    self.tt("dve", NKT[:], pb_[:, 384:512], cf[:, CI_TLE, :], ALU.mult)
                    P = self.L16()
                    self.tt("dve", P[:], qP[:], cf[:, CI_TGT, :], ALU.mult)
                    pre.append((X, P, NBT, MKT, NKT))
                Zs = self.tri_inverse_multi([(pr[0][:], pr[1][:]) for pr in pre])
                for hh in range(2):
                    hs = slice(hh * 64, hh * 64 + 64)
                    X, P, NBT, MKT, NKT = pre[hh]
                    Z = Zs[hh]
                    V_ = self.vpad[:, s, hh, hs]
                    qr = self.quarter()
                    self.mm(qr[:, 0:64], ARt[:, 0:128], self.Srwb[:, l, j, hs], True, False)
                    self.mm(qr[:, 0:64], MKT[:], V_, False, True)
                    RHS = self.L16()
                    self.copy_any(RHS[:, 0:64], qr[:, 0:64])
                    qu = self.quarter()
                    self.mm(qu[:, 0:64], Z[:], RHS[:, 0:64])
                    self.copy_any(self.upad[hh][:, hs], qu[:, 0:64])
                    self.mm(qy[:], self.upad[hh][:], NBT[:], False, False)
                    self.mm(qy[:], self.vpad[:, s, hh, :], NKT[:], False, hh == 1)
                self.copy_any(y[:, ts_], qy[:])
                qs = self.QF[1]
                self.mm(qs[:], Btm[:], self.upad[0][:], True, False)
                self.mm(qs[:], Btm[:], self.upad[1][:], False, False)
                self.mm(qs[:], Ktm[:], self.vpad[:, s, 0, :], False, False)
                self.mm(qs[:], Ktm[:], self.vpad[:, s, 1, :], False, True)
                tmpS = self.b32()
                self.tt("dve", tmpS[:], qs[:], cf[:, CI_BLK, :], ALU.mult)
                Sf = self.Srw[:, l, j, :]
                self.tt("dve", Sf, tmpS[:], Sf, ALU.add)
                self.ts("dve", Sf, Sf, E1[:, 127:128], None, ALU.mult)
                self.cp("act", self.Srwb[:, l, j, :], Sf)
            pg_ = self.bank()
            self.mm(pg_[:], sw[:, l, 1024 + j * 128: 1024 + (j + 1) * 128], self.rw_gs[:, 0, :], True, False)
            self.mm(pg_[:], sw[0:32, l, 1536 + j * 128: 1536 + (j + 1) * 128], self.rw_gs[0:32, 1, :], False, True)
            p = self.bank()
            self.mm(p[:], blk, y[:])
            d = self.tmp32()
            self.stt("dve", d[:], p[:], -1.0 / 64, y[:], ALU.mult, ALU.add)
            sq = self.tmp32()
            self.act(sq[:], d[:], AF.Square)
            p2 = self.bank()
            self.mm(p2[:], blk, sq[:])
            rs = sq
            self.act(rs[:], p2[:], AF.Ln, bias=self.epsb[:, 2:3], scale=1.0 / 64)
            self.act(rs[:], rs[:], AF.Exp, scale=-0.5)
            self.tt("dve", d[:], d[:], rs[:], ALU.mult)
            self.act(d[:], d[:], AF.Identity, bias=pvl(PV_GNB, j), scale=pvl(PV_GNG, j))
            self.tt("dve", d[:], d[:], v[:], ALU.add)
            self.tt("dve", self.br[0][j][:], d[:], pg_[:], ALU.mult)
            self.dump("o_rw", d[:], [128, TT])

    def gla(self, l):
        cf, cb, sw = self.cf, self.cb, self.sw
        p = self.fm_chunk(l, 16)
        self.cp("act", self.gl_lo[0:32, :], p[0:32, :])
        self.memset("dve", self.gl_lo[32:33, :], 1.0)
        if self.cut(0, 1):
            return
        tmk = self.slabA(l, ("tm", 2))
        for s in range(4):
            ts_ = slice(s * C, (s + 1) * C)
            p = self.bank()
            self.mm(p[:, 0:256], self.gl_lo[0:33, ts_], sw[0:33, l, 2048:2304])
            e = self.tmp32()
            self.act(e[:, 0:256], p[:, 0:256], AF.Exp, scale=-1.0)
            self.act(self.gl_l[:, s, :], e[:, 0:256], AF.Ln, bias=self.epsb[:, 3:4])
            pk = self.bank()
            for k in range(8):
                self.mm(pk[:, 0:256], self.xb[k][:, ts_], tmk[:, k * 256:(k + 1) * 256], k == 0, k == 7)
            self.copy_any(self.gl_kt[:, s, :], pk[:, 0:256])
        if self.cut(1, 1):
            return
        q, k_, g0, g1, o0, o1 = self.pool[0:6]
        for j in range(2):
            self.copy_any(q[:], self.fm_chunk(l, 18 + 4 * j)[:])
            self.copy_any(k_[:], self.fm_chunk(l, 19 + 4 * j)[:])
            self.act(g0[:], self.fm_chunk(l, 20 + 4 * j)[:], AF.Silu)
            self.act(g1[:], self.fm_chunk(l, 21 + 4 * j)[:], AF.Silu)
            tmv = self.slabA(l, ("tm", 3 + j))
            for s in range(4):
                ts_ = slice(s * C, (s + 1) * C)
                pv_ = self.bank()
                for k in range(8):
                    self.mm(pv_[:, 0:256], self.xb[k][:, ts_], tmv[:, k * 256:(k + 1) * 256], k == 0, k == 7)
                self.copy_any(self.vt16[:, s, :], pv_[:, 0:256])
            if self.cut(2, 1):
                return
            for s in range(4):
                ts_ = slice(s * C, (s + 1) * C)
                lj = self.gl_l[:, s, j * 128:(j + 1) * 128]
                qk = self.quarter()
                self.mm(qk[:], cf[:, CI_TGT, :], lj)
                ek = self.b32()
                self.act(ek[:], qk[:], AF.Exp, scale=-1.0 / 16)
                for hh in range(2):
                    c0_, c1_ = hh * 64, hh * 64 + 64
                    self.tt("dve", self.kpad[hh][:, c0_:c1_], self.gl_kt[:, s, j * 128 + c0_: j * 128 + c1_],
                            ek[:, c0_:c1_], ALU.mult)
                qsf = self.QF[2]
                qb = self.quarter()
                self.mm(qb[:], lj, cf[:, CI_TLE, :])
                Eb = self.b32()
                self.act(Eb[:], qb[:], AF.Exp, scale=-1.0 / 16)
                Ein = self.b32()
                self.act(Ein[:], qb[:], AF.Exp, scale=1.0 / 16)
                qt = self.L16()
                self.stt("dve", qt[:], q[:, ts_], 0.125, Eb[:], ALU.mult, ALU.mult)
                hm = (cf[:, CI_BLK, 0:1], cf[:, CI_BLK, 64:65])
                for hh in range(2):
                    hs = slice(hh * 64, hh * 64 + 64)
                    if hh == 1 and self.cut(6, 1):
                        return
                    kth = self.L16()
                    self.stt("dve", kth[:], k_[:, ts_], hm[hh], Ein[:], ALU.mult, ALU.mult)
                    qth = self.L16()
                    self.ts("dve", qth[:], qt[:], hm[hh], None, ALU.mult)
                    if hh == 1 and self.cut(7, 1):
                        return
                    qa = self.quarter()
                    self.mm(qa[:], kth[:], qt[:])
                    attT = self.b16()
                    self.tt("dve", attT[:], qa[:], cf[:, CI_TLE, :], ALU.mult)
                    if hh == 1 and self.cut(8, 1):
                        return
                    V_ = self.vt16[:, s, hh * 128:(hh + 1) * 128]
                    qo = self.quarter()
                    self.mm(qo[:], V_, attT[:], True, False)
                    self.mm(qo[:], self.Sglb[:, l, j, :], qth[:], False, True)
                    self.copy_any((o0, o1)[hh][:, ts_], qo[:])
                    if hh == 1 and self.cut(9, 1):
                        return
                    self.mm(qsf[:], self.kpad[hh][:], V_, hh == 0, hh == 1)
                Sf = self.Sgl[:, l, j, :]
                self.stt("dve", Sf, Sf, Eb[:, 127:128], qsf[:], ALU.mult, ALU.add)
                self.cp("act", self.Sglb[:, l, j, :], Sf)
            self.head_rms(l, o0, g0, PV_GLAN, self.br[1][2 * j])
            self.head_rms(l, o1, g1, PV_GLAN, self.br[1][2 * j + 1])

    def cut(self, n, b):
        import os
        c = int(os.environ.get("GCUT", "99"))
        if n >= c:
            for j in range(4):
                self.memset("dve", self.br[b][j][:], 0.0)
            return True
        return False

    def head_rms(self, l, o, gate, pvcol, dst):
        sq = self.tmp32()
        self.act(sq[:], o[:], AF.Square)
        p = self.bank()
        self.mm(p[:], self.cf[:, CI_V128, :], sq[:])
        rs = sq
        self.act(rs[:], p[:], AF.Ln, bias=self.epsb[:, 1:2])
        self.act(rs[:], rs[:], AF.Exp, scale=-0.5)
        t = self.tmp32()
        self.stt("dve", t[:], o[:], self.pv[:, l, pvcol:pvcol + 1], rs[:], ALU.mult, ALU.mult)
        self.tt("dve", dst[:], t[:], gate[:], ALU.mult)

    def gdn(self, l):
        cf, cb = self.cf, self.cb
        ones_f = cf[:, CI_ONE, :]
        idb = cb[:, CB_ID, :]
        tmab = self.slabA(l, ("tm", 5))
        sc = self.gd_sc
        for s in range(4):
            ts_ = slice(s * C, (s + 1) * C)
            pa = self.quarter()
            for k in range(8):
                self.mm(pa[:, 0:8], self.xb[k][:, ts_], tmab[:, k * 256:k * 256 + 8], k == 0, k == 7)
            self.cp("dve", self.gd_ab[:, s, :], pa[:, 0:8])
            self.tt("dve", sc[:, s, 0:4], self.gd_ab[:, s, 0:4], self.pv[:, l, PV_DTB:PV_DTB + 4], ALU.add)
            self.act(sc[:, s, 0:4], sc[:, s, 0:4], AF.Exp)
            self.act(sc[:, s, 0:4], sc[:, s, 0:4], AF.Ln, bias=self.epsb[:, 3:4])
            self.tt("dve", sc[:, s, 0:4], sc[:, s, 0:4], self.nega[:, l, :], ALU.mult)
            self.act(sc[:, s, 4:8], self.gd_ab[:, s, 4:8], AF.Sigmoid)
            qg = self.quarter()
            self.mm(qg[:, 0:4], cf[:, CI_TLE, :], sc[:, s, 0:4])
            self.mm(qg[:, 4:8], cf[:, CI_TGT, :], sc[:, s, 0:4])
            self.act(sc[:, s, 8:16], qg[:, 0:8], AF.Exp)
            self.tt("dve", sc[:, s, 16:20], sc[:, s, 4:8], sc[:, s, 8:12], ALU.mult)
            self.ts("dve", sc[:, s, 20:24], sc[:, s, 4:8], -1.0, None, ALU.mult)
        q, k_, v, gate, o = self.pool[0:5]
        for h in range(4):
            for which, dst in ((0, q), (1, k_), (2, v)):
                cidx = which * 4 + h
                r = self.raw_tile(l, 16 + cidx, self.fm_chunk(l, 26 + 4 * h + which))
                t = self.tmp32()
                cw = lambda tap: self.pv[:, l, PV_CONV + tap * 12 + cidx: PV_CONV + tap * 12 + cidx + 1]
                self.ts("dve", t[:], r[:, 0:TT], cw(0), None, ALU.mult)
                for tap in (1, 2, 3):
                    self.stt("dve", t[:], r[:, tap:tap + TT], cw(tap), t[:], ALU.mult, ALU.add)
                self.act(dst[:], t[:], AF.Silu)
            self.act(gate[:], self.fm_chunk(l, 29 + 4 * h)[:], AF.Silu)
            for src, scale in ((q, 128.0 ** -0.5), (k_, 1.0)):
                sq = self.tmp32()
                self.act(sq[:], src[:], AF.Square)
                p = self.bank()
                self.mm(p[:], ones_f, sq[:])
                rs = sq
                self.act(rs[:], p[:], AF.Ln, bias=self.epsb[:, 1:2])
                self.act(rs[:], rs[:], AF.Exp, scale=-0.5)
                self.stt("dve", src[:], src[:], scale, rs[:], ALU.mult, ALU.mult)
            for src, dstt in ((k_, self.kt16), (v, self.vt16)):
                p = self.bank()
                for s in range(4):
                    self.tr(p[:, s * 128:(s + 1) * 128], src[:, s * C:(s + 1) * C], cf[:, CI_ID, :])
                for s in range(4):
                    self.copy_any(dstt[:, s, 0:128], p[:, s * 128:(s + 1) * 128])
            for s in range(4):
                ts_ = slice(s * C, (s + 1) * C)
                kn_tm = self.kt16[:, s, :]
                v_tm = self.vt16[:, s, 0:128]
                knT = self.L16()
                self.copy_any(knT[:], k_[:, ts_])
                GT = self.b32()
                self.ts("dve", GT[:], cf[:, CI_TLE, :], sc[:, s, h:h + 1], None, ALU.mult)
                GB = self.b32()
                self.ts("dve", GB[:], cf[:, CI_ONE, :], sc[:, s, h:h + 1], None, ALU.mult)
                qd = self.quarter()
                self.mm(qd[:], GT[:], cf[:, CI_ONE, :], True, False)
                self.mm(qd[:], GB[:], cf[:, CI_NTLE, :], False, False)
                self.mm(qd[:], idb, cb[:, CB_MBSL, :], False, True)
                Dsl = self.b32()
                self.act(Dsl[:], qd[:], AF.Exp)
                qe = self.quarter()
                self.mm(qe[:], GB[:], cf[:, CI_TLE, :], True, False)
                self.mm(qe[:], GT[:], cf[:, CI_NEGONE, :], False, False)
                self.mm(qe[:], idb, cb[:, CB_MBIU, :], False, True)
                Diu = self.b32()
                self.act(Diu[:], qe[:], AF.Exp)
                qbc = self.quarter()
                self.mm(qbc[:], GB[:], cf[:, CI_TLE, :])
                bcE = self.b32()
                self.act(bcE[:], qbc[:], AF.Exp)
                qG = self.quarter()
                self.mm(qG[:], knT[:], knT[:])
                P = self.L16()
                self.stt("dve", P[:], qG[:], sc[:, s, 20 + h:21 + h], Dsl[:], ALU.mult, ALU.mult)
                tq = self.tbq()
                self.tr(tq[:], P[:], idb)
                X = self.L16()
                self.copy_any(X[:], tq[:])
                Z = self.tri_inverse(X[:], P[:])
                qnT = self.L16()
                self.copy_any(qnT[:], q[:, ts_])
                qa = self.quarter()
                self.mm(qa[:], knT[:], qnT[:])
                attT = self.L16()
                self.tt("dve", attT[:], qa[:], Diu[:], ALU.mult)
                qgT = self.L16()
                self.tt("dve", qgT[:], q[:, ts_], bcE[:], ALU.mult)
                kbg = self.L16()
                self.ts("dve", kbg[:], kn_tm, sc[:, s, 16 + h:17 + h], None, ALU.mult)
                kdec = self.L16()
                self.ts("dve", kdec[:], kn_tm, sc[:, s, 12 + h:13 + h], None, ALU.mult)
                vb = self.L16()
                self.ts("dve", vb[:], v_tm, sc[:, s, 4 + h:5 + h], None, ALU.mult)
                qw = self.quarter()
                self.mm(qw[:], kbg[:], Z[:])
                nwT = self.L16()
                self.ts("dve", nwT[:], qw[:], -1.0, None, ALU.mult)
                Sb = self.Sgdb[:, l, h, :]
                qv = self.quarter()
                self.mm(qv[:], Z[:], vb[:], True, False)
                self.mm(qv[:], nwT[:], Sb, False, True)
                vnew = self.L16()
                self.copy_any(vnew[:], qv[:])
                qo = self.quarter()
                self.mm(qo[:], Sb, qgT[:], True, False)
                self.mm(qo[:], vnew[:], attT[:], False, True)
                self.copy_any(o[:, ts_], qo[:])
                qs = self.quarter()
                self.mm(qs[:], kdec[:], vnew[:])
                Sf = self.Sgd[:, l, h, :]
                self.stt("dve", Sf, Sf, bcE[:, 127:128], qs[:], ALU.mult, ALU.add)
                self.cp("act", self.Sgdb[:, l, h, :], Sf)
            self.head_rms(l, o, gate, PV_GDNN, self.br[2][h])

    def merge(self, l):
        macc = self.z
        for b in range(3):
            for j in range(4):
                self.dump("br", self.br[b][j][:], [128, TT])
        for b in range(3):
            for half in range(2):
                wb = self.slabA(l, ("br", b, half))
                for mm_ in range(2):
                    wg = self.slabA(l, ("gate", b * 4 + half * 2 + mm_))
                    for cc in range(2):
                        m = half * 4 + mm_ * 2 + cc
                        pg = self.fm_chunk_psum(wg, cc, wide=True)
                        pp = self.bankx()
                        for k in range(4):
                            c0 = k * 512 + (mm_ * 2 + cc) * 128
                            self.mm(pp[:], wb[:, c0:c0 + 128], self.br[b][k][:], k == 0, k == 3)
                        t = self.tmp32()
                        self.act(t[:], pg[:], AF.Sigmoid)
                        if b == 0:
                            self.tt("dve", macc[m][:], t[:], pp[:], ALU.mult)
                        else:
                            self.tt("dve", t[:], t[:], pp[:], ALU.mult)
                            if b == 1:
                                self.tt("dve", macc[m][:], macc[m][:], t[:], ALU.add)
                            else:
                                self.tt("dve", self.merged[m][:], macc[m][:], t[:], ALU.add)
        wos = [self.slabA(l, ("wo", i)) for i in range(4)]
        for m in range(8):
            p = self.bankx()
            sl = wos[m // 2]
            for k in range(8):
                self.mm(p[:], sl[:, k * 256 + (m % 2) * 128: k * 256 + (m % 2) * 128 + 128], self.merged[k][:], k == 0, k == 7)
            self.stt("dve", self.z[m][:], self.x[m][:], ALPHA, p[:], ALU.mult, ALU.add)


_CACHE = {}


def prep_weights(inputs, plan):
    inp = {k: np.asarray(v, np.float32) for k, v in inputs.items() if k not in ("x", "p")}
    per = [prep_layer(inp, l, plan) for l in range(2)]
    cf, cb = build_consts()
    return dict(wA=np.ascontiguousarray(np.stack([p[0] for p in per], 0)),
                wB=np.ascontiguousarray(np.stack([p[1] for p in per], 0)),
                pv=np.ascontiguousarray(np.stack([p[2] for p in per], 0)),
                sw=np.ascontiguousarray(np.stack([p[3] for p in per], 0)),
                muv=np.ascontiguousarray(np.stack([p[4] for p in per], 0)),
                cf=cf, cb=cb)


def run(inputs, T, layers=(0, 1), n_cores=8, debug=(), stages=4, mixsel="rgd"):
    x = np.asarray(inputs["x"], np.float32)
    p = np.asarray(inputs["p"], np.float32)
    B = x.shape[0]
    key = (T, tuple(layers), tuple(debug), stages, mixsel)
    if key not in _CACHE:
        _CACHE[key] = Prog(T, list(layers), debug, stages, mixsel)
    prog = _CACHE[key]
    w = prep_weights(inputs, prog.plan)
    in_maps = []
    for c in range(n_cores):
        b = c % B
        m = dict(w)
        m["xT"] = np.ascontiguousarray(x[b, :T].T)
        m["pT"] = np.ascontiguousarray(p[:, b, :T].transpose(0, 2, 1))
        in_maps.append(m)
    res = run_bass_kernel_spmd(prog.nc, in_maps, core_ids=list(range(n_cores)))
    out = np.stack([np.ascontiguousarray(res.results[b]["outT"].T) for b in range(B)], 0)
    return out.astype(np.float32), res, prog


def kernel(**inputs):
    out, _, _ = run(inputs, T=4096)
    return out
```

```python
import contextlib
import numpy as np
import concourse.bass as bass
import concourse.mybir as mybir
from concourse.bass_utils import run_bass_kernel_spmd

F32 = mybir.dt.float32
BF16 = mybir.dt.bfloat16
AF = mybir.ActivationFunctionType
ALU = mybir.AluOpType
AX = mybir.AxisListType

D = 1024
DFF = 2816
TT = 512
C = 128
ALPHA = 4.0 ** 0.25
LN_EPS = 1e-5
SAME_ENGINE_SYNC = True
NSLOT = 5
SLOT = 2816


class V:
    __slots__ = ("b", "ap")

    def __init__(self, b, ap):
        self.b = b
        self.ap = ap


class Buf:
    __slots__ = ("t", "name", "last_write", "reads", "dsem", "dcnt", "c0", "owner", "excl")

    def __init__(self, t, name, c0=None, owner=None):
        self.owner = owner
        self.excl = False
        self.t = t
        self.name = name
        self.last_write = None
        self.reads = []
        self.dsem = None
        self.dcnt = 0
        self.c0 = c0

    def __getitem__(self, idx):
        if self.c0 is None:
            return V(self, self.t[idx])
        if not isinstance(idx, tuple):
            idx = (idx, slice(None))
        r, c = idx
        a = 0 if c.start is None else c.start
        b = 128 if c.stop is None else c.stop
        return V(self.owner or self, self.t[r, self.c0 + a: self.c0 + b])


class Sched:
    ENGS = ("pe", "act", "dve", "pool", "sp")

    def __init__(self, nc):
        self.nc = nc
        self.stack = contextlib.ExitStack()
        self.q = {e: [] for e in self.ENGS}
        self.sem = {}
        self.cnt = {e: 0 for e in self.ENGS}
        self.seen = {e: {} for e in self.ENGS}
        for e in self.ENGS:
            self.sem[e] = self.stack.enter_context(nc.semaphore("s_" + e))
        self.nbuf = 0
        self.ninst = 0
        self.rr = 0

    def sbuf(self, shape, dtype=F32, name=None):
        self.nbuf += 1
        name = (name or "b") + f"_{self.nbuf}"
        t = self.stack.enter_context(self.nc.sbuf_tensor(name, list(shape), dtype))
        return Buf(t, name)

    def psum(self, shape, dtype=F32, name=None):
        self.nbuf += 1
        name = (name or "p") + f"_{self.nbuf}"
        t = self.stack.enter_context(self.nc.psum_tensor(name, list(shape), dtype))
        b = Buf(t, name)
        b.excl = True
        return b

    def dsem_for(self, buf):
        if buf.dsem is None:
            buf.dsem = self.stack.enter_context(self.nc.semaphore("d_" + buf.name))
        return buf.dsem

    def _deps(self, eng, reads, writes):
        waits = {}

        def add(rec):
            if rec is None:
                return
            s, v, owner = rec
            if owner == eng and (eng == "pe" or not SAME_ENGINE_SYNC):
                return
            k = id(s)
            if k not in waits or waits[k][1] < v:
                waits[k] = (s, v)

        for b in reads:
            add(b.last_write)
            if b.excl:
                for r in b.reads:
                    if r[2] != eng:
                        add(r)
        for b in writes:
            add(b.last_write)
            for r in b.reads:
                add(r)
        out = []
        seen = self.seen[eng]
        for k, (s, v) in waits.items():
            if seen.get(k, 0) >= v:
                continue
            seen[k] = v
            out.append((s, v))
        return out

    def op(self, eng, fn, reads=(), writes=()):
        waits = self._deps(eng, reads, writes)
        self.cnt[eng] += 1
        v = self.cnt[eng]
        s = self.sem[eng]
        self.q[eng].append((fn, waits, (s, 1)))
        rec = (s, v, eng)
        for b in reads:
            if len(b.reads) > 24:
                b.reads = b.reads[-24:] if False else b.reads
            b.reads.append(rec)
        for b in writes:
            b.last_write = rec
            b.reads = []
        self.ninst += 1

    def dma(self, eng, out_ap, in_ap, reads=(), writes=(), sembuf=None):
        sembuf = sembuf or (writes[0] if writes else reads[0])
        ds = self.dsem_for(sembuf)
        waits = self._deps(eng, reads, writes)
        sembuf.dcnt += 16
        v = sembuf.dcnt
        self.q[eng].append((lambda e: e.dma_start(out=out_ap, in_=in_ap), waits, (ds, 16)))
        rec = (ds, v, "dma")
        for b in reads:
            b.reads.append(rec)
        for b in writes:
            b.last_write = rec
            b.reads = []
        self.ninst += 1

    def final_wait(self, eng, bufs):
        waits = self._deps(eng, bufs, bufs)
        self.q[eng].append((None, waits, None))

    def finish(self):
        nc = self.nc
        q = self.q

        def replay(name):
            def f(e):
                for fn, waits, inc in q[name]:
                    for s, v in waits:
                        e.wait_ge(s, v)
                    if fn is None:
                        continue
                    ins = fn(e)
                    if inc is not None:
                        ins.then_inc(inc[0], inc[1])
            return f

        with nc.Block() as block:
            block.tensor(replay("pe"))
            block.scalar(replay("act"))
            block.vector(replay("dve"))
            block.gpsimd(replay("pool"))
            block.sync(replay("sp"))
        self.stack.close()


def _bufs(*xs):
    out = []
    for x in xs:
        if isinstance(x, V) and x.b not in out:
            out.append(x.b)
    return out


def _ap(x):
    return x.ap if isinstance(x, V) else x


def slabify(W, wc):
    K, N = W.shape
    kc = K // 128
    return np.ascontiguousarray(
        W.reshape(kc, 128, N // wc, wc).transpose(2, 1, 0, 3).reshape(N // wc, 128, kc * wc))


def colvec(v):
    return np.ascontiguousarray(v.reshape(-1, 128).T)


def pad_cols(W, n):
    out = np.zeros((W.shape[0], n), np.float32)
    out[:, :W.shape[1]] = W
    return out


RW0 = 0
GLA0 = 1824
GDN0 = 3376
GATE0 = 5432
NFM = 42
PV_LNG, PV_LNB = 0, 32
PV_MU = 64
PV_A0, PV_KK, PV_KA, PV_RK, PV_GNG, PV_GNB = 80, 84, 88, 92, 96, 100
PV_GLAN, PV_GDNN = 104, 105
PV_CONV = 106
PV_ALOG, PV_DTB = 154, 158
NPV = 162
CI_ID, CI_ONE, CI_NEGONE, CI_LN, CI_V128, CI_BLK, CI_TLE, CI_TLT, CI_TGT, CI_NTLE, CI_MBSL, CI_MBIU = range(12)
NCI = 12
CB_ID, CB_MBSL, CB_MBIU = 0, 1, 2


def build_consts():
    p = np.arange(128)[:, None]
    j = np.arange(128)[None, :]
    m = np.zeros((NCI, 128, 128), np.float32)
    m[CI_ID] = (p == j)
    m[CI_ONE] = 1.0
    m[CI_NEGONE] = -1.0
    m[CI_LN] = 1.0 / 1024
    m[CI_V128] = 1.0 / 128
    m[CI_BLK] = ((p // 64) == (j // 64))
    m[CI_TLE] = (p <= j)
    m[CI_TLT] = (p < j)
    m[CI_TGT] = (p > j)
    m[CI_NTLE] = -(p <= j).astype(np.float32)
    m[CI_MBSL] = np.where(p > j, 0.0, -30000.0)
    m[CI_MBIU] = np.where(p <= j, 0.0, -30000.0)
    cf = np.ascontiguousarray(m.transpose(1, 0, 2).reshape(128, NCI * 128))
    cbm = np.stack([m[CI_ID], m[CI_MBSL], m[CI_MBIU]], 0)
    cb = np.ascontiguousarray(cbm.transpose(1, 0, 2).reshape(128, 3 * 128))
    return cf, cb


def regroup_win(w_in):
    fm = np.zeros((1024, NFM * 128), np.float32)

    def put(ch, src, n):
        fm[:, ch * 128: ch * 128 + n] = w_in[:, src: src + n]
    put(0, RW0 + 1536, 128)
    put(1, RW0 + 1664, 128)
    put(2, RW0 + 1792, 32)
    for j in range(4):
        put(4 + 3 * j, RW0 + j * 128, 128)
        put(5 + 3 * j, RW0 + 512 + j * 128, 128)
        put(6 + 3 * j, RW0 + 1024 + j * 128, 128)
    put(16, GLA0 + 1024, 16)
    for j in range(2):
        put(18 + 4 * j, GLA0 + j * 128, 128)
        put(19 + 4 * j, GLA0 + 256 + j * 128, 128)
        put(20 + 4 * j, GLA0 + 1040 + (2 * j) * 128, 128)
        put(21 + 4 * j, GLA0 + 1040 + (2 * j + 1) * 128, 128)
    for h in range(4):
        put(26 + 4 * h, GDN0 + h * 128, 128)
        put(27 + 4 * h, GDN0 + 512 + h * 128, 128)
        put(28 + 4 * h, GDN0 + 1024 + h * 128, 128)
        put(29 + 4 * h, GDN0 + 1544 + h * 128, 128)
    tm = np.zeros((1024, 6 * 256), np.float32)
    tm[:, 0:512] = w_in[:, RW0 + 1024: RW0 + 1536]
    tm[:, 512:768] = w_in[:, GLA0 + 256: GLA0 + 512]
    tm[:, 768:1280] = w_in[:, GLA0 + 512: GLA0 + 1024]
    tm[:, 1280:1288] = w_in[:, GDN0 + 1536: GDN0 + 1544]
    gate = w_in[:, GATE0: GATE0 + 3072]
    return fm, tm, gate


def prep_layer(inp, l, plan):
    A = {}
    for f in range(2):
        s1 = slabify(inp["ffn_w1"][l, f], 256)
        s3 = slabify(inp["ffn_w3"][l, f], 256)
        for g in range(11):
            A[("w1", f, g)] = s1[g]
            A[("w3", f, g)] = s3[g]
    fm, tm, gate = regroup_win(inp["w_in"][l])
    for i, s in enumerate(slabify(fm, 256)):
        A[("fm", i)] = s
    for i, s in enumerate(slabify(tm, 256)):
        A[("tm", i)] = s
    for i, s in enumerate(slabify(gate, 256)):
        A[("gate", i)] = s
    for b in range(3):
        sb = slabify(inp["w_branch"][l, b], 512)
        for half in range(2):
            A[("br", b, half)] = sb[half]
    for i, s in enumerate(slabify(inp["w_o"][l], 256)):
        A[("wo", i)] = s
    for i, s in enumerate(slabify(inp["ple_w_gate"][l], 256)):
        A[("pg", i)] = s
    A[("pp", 0)] = slabify(inp["ple_w_proj"][l], 1024)[0]
    lst = [A[k] for k in plan]
    while len(lst) < NA_PER_LAYER:
        lst.append(lst[0])
    wA = np.ascontiguousarray(np.stack(lst, 0))
    wB = np.ascontiguousarray(np.concatenate([slabify(inp["ffn_w2"][l, f], 128) for f in range(2)], 0))
    pv = np.zeros((128, NPV), np.float32)
    for i in range(4):
        pv[:, PV_LNG + i * 8: PV_LNG + i * 8 + 8] = colvec(inp["ln_g"][l, i])
        pv[:, PV_LNB + i * 8: PV_LNB + i * 8 + 8] = colvec(inp["ln_b"][l, i])
    mu = inp["rw_mu"][l]
    pv[:, PV_MU + 0] = mu[1536:1664]
    pv[:, PV_MU + 1] = mu[1664:1792]
    pv[0:32, PV_MU + 2] = mu[1792:1824]
    for j in range(4):
        pv[:, PV_MU + 4 + 3 * j] = mu[j * 128:(j + 1) * 128]
        pv[:, PV_MU + 5 + 3 * j] = mu[512 + j * 128: 512 + (j + 1) * 128]
        pv[:, PV_MU + 6 + 3 * j] = mu[1024 + j * 128: 1024 + (j + 1) * 128]
    pv[:, PV_A0: PV_A0 + 4] = colvec(inp["rw_a0"][l])
    pv[:, PV_KK: PV_KK + 4] = colvec(inp["rw_k_k"][l])
    pv[:, PV_KA: PV_KA + 4] = colvec(inp["rw_k_a"][l])
    pv[:, PV_RK: PV_RK + 4] = colvec(inp["rw_r_k"][l].reshape(-1))
    pv[:, PV_GNG: PV_GNG + 4] = colvec(inp["rw_gn_g"][l])
    pv[:, PV_GNB: PV_GNB + 4] = colvec(inp["rw_gn_b"][l])
    pv[:, PV_GLAN] = inp["gla_norm_g"][l]
    pv[:, PV_GDNN] = inp["gdn_norm_g"][l]
    for tap in range(4):
        pv[:, PV_CONV + tap * 12: PV_CONV + tap * 12 + 12] = colvec(inp["gdn_conv_w"][l, tap])
    pv[:, PV_ALOG: PV_ALOG + 4] = inp["gdn_a_log"][l][None, :]
    pv[:, PV_DTB: PV_DTB + 4] = inp["gdn_dt_bias"][l][None, :]
    sw = np.zeros((128, 2304), np.float32)
    sw[0:64, 0:512] = inp["rw_w2"][l]
    sw[64, 0:512] = inp["rw_w0"][l]
    sw[64:128, 512:1024] = inp["rw_a2"][l]
    sw[0:128, 1024:1536] = inp["rw_g2"][l][0:128]
    sw[0:32, 1536:2048] = inp["rw_g2"][l][128:160]
    sw[0:16, 2048:2304] = inp["gla_gk_w2"][l]
    sw[32, 2048:2304] = inp["gla_gk_b"][l]
    muv = np.ascontiguousarray(np.broadcast_to(mu[1024:1536][None, :], (128, 512))).astype(np.float32)
    return wA, wB, pv, sw, muv


NA_PER_LAYER = 22 + 21 + 6 + 18 + 4 + 22 + 4 + 1


class Prog:
    def __init__(self, T, layers, debug=(), stages=4, mixsel="rgd"):
        self.stages = stages
        self.mixsel = mixsel
        self.T = T
        self.NT = T // TT
        self.layers = list(layers)
        self.debug = set(debug)
        nc = self.nc = bass.Bass("TRN2", target_bir_lowering=False)
        self.xT = nc.dram_tensor("xT", [D, T], F32, kind="ExternalInput").ap()
        self.pT = nc.dram_tensor("pT", [2, 256, T], F32, kind="ExternalInput").ap()
        self.wA = nc.dram_tensor("wA", [2, NA_PER_LAYER, 128, 2048], F32, kind="ExternalInput").ap()
        self.wB = nc.dram_tensor("wB", [2, 16, 128, 2816], F32, kind="ExternalInput").ap()
        self.pvd = nc.dram_tensor("pv", [2, 128, NPV], F32, kind="ExternalInput").ap()
        self.swd = nc.dram_tensor("sw", [2, 128, 2304], F32, kind="ExternalInput").ap()
        self.muvd = nc.dram_tensor("muv", [2, 128, 512], F32, kind="ExternalInput").ap()
        self.cfd = nc.dram_tensor("cf", [128, NCI * 128], F32, kind="ExternalInput").ap()
        self.cbd = nc.dram_tensor("cb", [128, 3 * 128], F32, kind="ExternalInput").ap()
        self.outT = nc.dram_tensor("outT", [D, T], F32, kind="ExternalOutput").ap()
        self.dbg_out = {}
        self.plan = []
        self.S = Sched(nc)
        self.build()

    def mm(self, out, lhsT, rhs, start=True, stop=True):
        o, a, b = out.ap, lhsT.ap, rhs.ap
        self.S.op("pe", lambda e: e.matmul(o, lhsT=a, rhs=b, start=start, stop=stop),
                  reads=_bufs(lhsT, rhs), writes=[out.b])

    def tr(self, out, in_, ident):
        o, a, b = out.ap, in_.ap, ident.ap
        self.S.op("pe", lambda e: e.transpose(o, a, b), reads=_bufs(in_, ident), writes=[out.b])

    def act(self, out, in_, func, bias=0.0, scale=1.0):
        o, a, bi, sc = out.ap, in_.ap, _ap(bias), _ap(scale)
        self.S.op("act", lambda e: e.activation(out=o, in_=a, func=func, bias=bi, scale=sc),
                  reads=_bufs(in_, bias, scale), writes=[out.b])

    def tt(self, eng, out, a, b, op):
        o, x, y = out.ap, a.ap, b.ap
        self.S.op(eng, lambda e: e.tensor_tensor(out=o, in0=x, in1=y, op=op), reads=_bufs(a, b), writes=[out.b])

    def ts(self, eng, out, a, s1, s2, op0, op1=None):
        o, x, p1, p2 = out.ap, a.ap, _ap(s1), _ap(s2)
        if op1 is None:
            self.S.op(eng, lambda e: e.tensor_scalar(out=o, in0=x, scalar1=p1, scalar2=None, op0=op0),
                      reads=_bufs(a, s1), writes=[out.b])
        else:
            self.S.op(eng, lambda e: e.tensor_scalar(out=o, in0=x, scalar1=p1, scalar2=p2, op0=op0, op1=op1),
                      reads=_bufs(a, s1, s2), writes=[out.b])

    def stt(self, eng, out, a, s, b, op0, op1):
        o, x, sc, y = out.ap, a.ap, _ap(s), b.ap
        self.S.op(eng, lambda e: e.scalar_tensor_tensor(out=o, in0=x, scalar=sc, in1=y, op0=op0, op1=op1),
                  reads=_bufs(a, s, b), writes=[out.b])

    def cp(self, eng, out, a):
        o, x = out.ap, a.ap
        if eng == "act":
            self.S.op("act", lambda e: e.copy(out=o, in_=x), reads=[a.b], writes=[out.b])
        else:
            self.S.op(eng, lambda e: e.tensor_copy(out=o, in_=x), reads=[a.b], writes=[out.b])

    def memset(self, eng, out, val):
        o = out.ap
        self.S.op(eng, lambda e: e.memset(o, val), writes=[out.b])

    def recip(self, out, a):
        o, x = out.ap, a.ap
        self.S.op("dve", lambda e: e.reciprocal(out=o, in_=x), reads=[a.b], writes=[out.b])

    def ev(self):
        self.S.rr ^= 1
        return "dve" if self.S.rr else "act"

    def copy_any(self, out, a):
        self.cp(self.ev(), out, a)

    def bank(self):
        self._bi = (self._bi + 1) % len(self.MM)
        return self.MM[self._bi]

    def bankx(self):
        import os
        if os.environ.get("NO_BANKX"):
            return self.bank()
        self._bxi = (self._bxi + 1) % len(self.MMX)
        return self.MMX[self._bxi]

    def quarter(self):
        self._qi = (self._qi + 1) % len(self.Q)
        return self.Q[self._qi]

    def tbq(self):
        self._ti = (self._ti + 1) % len(self.TB)
        return self.TB[self._ti]

    def slabA(self, l, key):
        idx = self._ai[l]
        self._ai[l] += 1
        if idx >= len(self.plan):
            self.plan.append(key)
        assert self.plan[idx] == key, (idx, key, self.plan[idx])
        slot = self.ring[self._ri % NSLOT]
        self._ri += 1
        self.S.dma("pool", slot.t[:, 0:2048], self.wA[l, idx], writes=[slot])
        return slot

    def slabB(self, l, idx):
        slot = self.ring[self._ri % NSLOT]
        self._ri += 1
        self.S.dma("pool", slot.t[:, 0:2816], self.wB[l, idx], writes=[slot])
        return slot

    def fm_slab(self, l, g):
        if self._fmg[0] != (l, g):
            self._fmg = ((l, g), self.slabA(l, ("fm", g)))
        return self._fmg[1]

    def fm_chunk(self, l, ch):
        slab = self.fm_slab(l, ch // 2)
        return self.fm_chunk_psum(slab, ch % 2)

    def dump(self, name, view, shape):
        if name not in self.debug:
            return
        key = f"dbg_{name}_{len(self.dbg_out)}"
        d = self.nc.dram_tensor(key, list(shape), F32, kind="ExternalOutput").ap()
        self.dbg_out[key] = name
        if not self._dbg_bufs:
            self._dbg_bufs.append(self.S.sbuf(shape, F32, "dbgt"))
        tmp = self._dbg_bufs[0]
        self.cp("dve", tmp[:], view)
        self.S.dma("sp", d, tmp.t[:], reads=[tmp], sembuf=tmp)

    def build(self):
        S = self.S
        self._bi = self._qi = self._ti = 0
        self._ri = 0
        self._fmg = (None, None)
        self._dbg_bufs = []
        self.MM = [S.psum([128, 512], F32, "mm") for _ in range(3)]
        qbanks = [S.psum([128, 512], F32, "qb") for _ in range(4)]
        self.MMX = [self.MM[0], qbanks[0], self.MM[1], qbanks[1], self.MM[2], qbanks[2]]
        self._bxi = 0
        self.Q = [Buf(qbanks[i % 3].t, qbanks[i % 3].name + f"_{i}", c0=(i // 3) * 128, owner=qbanks[i % 3])
                  for i in range(12)]
        self.QF = [Buf(qbanks[3].t, qbanks[3].name + f"_f{i}", c0=i * 128, owner=qbanks[3]) for i in range(4)]
        tbb = S.psum([128, 1024], BF16, "tbb")
        self.TB = [Buf(tbb.t, tbb.name + f"_{i}", c0=i * 128, owner=tbb) for i in range(4)]
        cf = self.cf = S.sbuf([128, NCI, 128], F32, "cf")
        cb = self.cb = S.sbuf([128, 3, 128], BF16, "cb")
        S.dma("sp", cf.t[:], self.cfd.rearrange("p (n j) -> p n j", n=NCI), writes=[cf])
        S.dma("pool", cb.t[:], self.cbd.rearrange("p (n j) -> p n j", n=3), writes=[cb])
        self.pv = S.sbuf([128, 2, NPV], F32, "pv")
        S.dma("sp", self.pv.t[:], self.pvd.rearrange("l p n -> p l n"), writes=[self.pv])
        self.sw = S.sbuf([128, 2, 2304], BF16, "sw")
        S.dma("pool", self.sw.t[:], self.swd.rearrange("l p n -> p l n"), writes=[self.sw])
        self.epsb = S.sbuf([128, 4], F32, "epsb")
        self.memset("dve", self.epsb[:, 0:1], LN_EPS)
        self.memset("dve", self.epsb[:, 1:2], 1e-6)
        self.memset("dve", self.epsb[:, 2:3], 64e-5)
        self.memset("dve", self.epsb[:, 3:4], 1.0)
        self.nega = S.sbuf([128, 2, 4], F32, "nega")
        for l in range(2):
            self.act(self.nega[:, l, :], self.pv[:, l, PV_ALOG:PV_ALOG + 4], AF.Exp)
            self.ts("dve", self.nega[:, l, :], self.nega[:, l, :], -1.0, None, ALU.mult)
        self.ring = [S.sbuf([128, SLOT], BF16, "ring") for _ in range(NSLOT)]
        self.x = [S.sbuf([128, TT], F32, "x") for _ in range(8)]
        self.pool = [S.sbuf([128, TT], F32, "pool") for _ in range(8)]
        self.z = self.pool
        self.xb = [S.sbuf([128, TT], BF16, "xb") for _ in range(8)]
        self.G = [S.sbuf([128, TT], BF16, "G") for _ in range(22)]
        self.br = [self.G[0:4], self.G[4:8], self.G[8:12]]
        self.merged = self.G[12:20]
        self.dxb = self.G[12:20]
        self.pbt = self.G[20:22]
        self.t32 = [S.sbuf([128, TT], F32, "t32") for _ in range(5)]
        self._t32i = 0
        self.xcar = S.sbuf([128, 2, 8], BF16, "xcar")
        self.memset("dve", self.xcar[:], 0.0)
        self.hcar = S.sbuf([128, 2, 28, 3], F32, "hcar")
        self.memset("dve", self.hcar[:], 0.0)
        self.Srw = S.sbuf([128, 2, 4, 128], F32, "Srw")
        self.Srwb = S.sbuf([128, 2, 4, 128], BF16, "Srwb")
        self.Sgl = S.sbuf([128, 2, 2, 128], F32, "Sgl")
        self.Sglb = S.sbuf([128, 2, 2, 128], BF16, "Sglb")
        self.Sgd = S.sbuf([128, 2, 4, 128], F32, "Sgd")
        self.Sgdb = S.sbuf([128, 2, 4, 128], BF16, "Sgdb")
        for s_ in (self.Srw, self.Srwb, self.Sgl, self.Sglb, self.Sgd, self.Sgdb):
            self.memset("dve", s_[:], 0.0)
        self.alloc_mixer()
        self._ai = {0: 0, 1: 0}
        for ti in range(self.NT):
            t0 = ti * TT
            for m in range(8):
                S.dma("sp", self.x[m].t[:], self.xT[m * 128:(m + 1) * 128, t0:t0 + TT], writes=[self.x[m]])
            for l in self.layers:
                self._ai[l] = 0
                self.layer_tile(l, ti)
                assert self._ai[l] == NA_PER_LAYER or self.mixsel != "rgd", self._ai[l]
            for m in range(8):
                S.dma("sp", self.outT[m * 128:(m + 1) * 128, t0:t0 + TT], self.x[m].t[:], reads=[self.x[m]],
                      sembuf=self.x[m])
        S.final_wait("sp", self.x + self._dbg_bufs)
        S.finish()

    def tmp32(self):
        self._t32i = (self._t32i + 1) % len(self.t32)
        return self.t32[self._t32i]

    def make_xb(self):
        for m in range(8):
            self.copy_any(self.xb[m][:], self.x[m][:])

    def ffn(self, l, f):
        for g in range(11):
            w1 = self.slabA(l, ("w1", f, g))
            w3 = self.slabA(l, ("w3", f, g))
            for cc in range(2):
                c = g * 2 + cc
                p1 = self.bankx()
                p3 = self.bankx()
                for k in range(8):
                    self.mm(p1[:], w1[:, k * 256 + cc * 128: k * 256 + cc * 128 + 128], self.xb[k][:], k == 0, k == 7)
                for k in range(8):
                    self.mm(p3[:], w3[:, k * 256 + cc * 128: k * 256 + cc * 128 + 128], self.xb[k][:], k == 0, k == 7)
                t = self.tmp32()
                self.act(t[:], p1[:], AF.Silu)
                self.stt("dve", self.G[c][:], t[:], 0.5, p3[:], ALU.mult, ALU.mult)
        for m in range(8):
            w2 = self.slabB(l, f * 8 + m)
            p = self.bankx()
            for c in range(22):
                self.mm(p[:], w2[:, c * 128:(c + 1) * 128], self.G[c][:], c == 0, c == 21)
            self.stt("dve", self.z[m][:], self.x[m][:], ALPHA, p[:], ALU.mult, ALU.add)

    def ln(self, l, i):
        pm = self.bank()
        pq = self.bank()
        lnm = self.cf[:, CI_LN, :]
        for m in range(8):
            self.mm(pm[:], lnm, self.z[m][:], m == 0, m == 7)
        for m in range(8):
            t = self.tmp32()
            self.act(t[:], self.z[m][:], AF.Square)
            self.mm(pq[:], lnm, t[:], m == 0, m == 7)
        mean = self.lnmean
        rstd = self.lnrstd
        self.cp("act", mean[:], pm[:])
        self.tt("dve", rstd[:], mean[:], mean[:], ALU.mult)
        self.tt("dve", rstd[:], pq[:], rstd[:], ALU.subtract)
        self.act(rstd[:], rstd[:], AF.Ln, bias=self.epsb[:, 0:1])
        self.act(rstd[:], rstd[:], AF.Exp, scale=-0.5)
        for m in range(8):
            t = self.tmp32()
            self.tt("dve", t[:], self.z[m][:], mean[:], ALU.subtract)
            self.tt("dve", t[:], t[:], rstd[:], ALU.mult)
            self.act(self.x[m][:], t[:], AF.Identity, bias=self.pv[:, l, PV_LNB + i * 8 + m: PV_LNB + i * 8 + m + 1],
                     scale=self.pv[:, l, PV_LNG + i * 8 + m: PV_LNG + i * 8 + m + 1])

    def layer_tile(self, l, ti):
        st = self.stages
        self.make_xb()
        self.ffn(l, 0)
        self.ln(l, 0)
        if st < 2:
            self._ai[l] = NA_PER_LAYER
            return
        self.make_xb()
        self.mixers(l, ti)
        self.ln(l, 1)
        if st < 3:
            self._ai[l] = NA_PER_LAYER
            return
        self.make_xb()
        self.ffn(l, 1)
        self.ln(l, 2)
        if st < 4:
            self._ai[l] = NA_PER_LAYER
            return
        self.make_xb()
        self.ple(l, ti)
        self.ln(l, 3)

    def ple(self, l, ti):
        t0 = ti * TT
        for k in range(2):
            pin = self.tmp32()
            self.S.dma("sp", pin.t[:], self.pT[l, k * 128:(k + 1) * 128, t0:t0 + TT], writes=[pin])
            self.copy_any(self.pbt[k][:], pin[:])
        slabs = [self.slabA(l, ("pg", i)) for i in range(4)]
        wp = self.slabA(l, ("pp", 0))
        for m in range(8):
            pg = self.bankx()
            pp = self.bankx()
            sl = slabs[m // 2]
            for k in range(8):
                self.mm(pg[:], sl[:, k * 256 + (m % 2) * 128: k * 256 + (m % 2) * 128 + 128], self.xb[k][:], k == 0, k == 7)
            for k in range(2):
                self.mm(pp[:], wp[:, k * 1024 + m * 128: k * 1024 + m * 128 + 128], self.pbt[k][:], k == 0, k == 1)
            t = self.tmp32()
            self.act(t[:], pg[:], AF.Sigmoid)
            self.tt("dve", t[:], t[:], pp[:], ALU.mult)
            self.stt("dve", self.z[m][:], self.x[m][:], ALPHA, t[:], ALU.mult, ALU.add)

    def alloc_mixer(self):
        S = self.S
        self.lnmean = S.sbuf([128, TT], F32, "lnmean")
        self.lnrstd = S.sbuf([128, TT], F32, "lnrstd")
        self.raw = [S.sbuf([128, 3 + TT], F32, "raw") for _ in range(2)]
        self._rawi = 0
        self.rw_tw = S.sbuf([128, TT], BF16, "rw_tw")
        self.rw_al = S.sbuf([128, TT], BF16, "rw_al")
        self.rw_gs = S.sbuf([128, 2, TT], BF16, "rw_gs")
        self.rw_kk = S.sbuf([128, TT], BF16, "rw_kk")
        self.rw_b = S.sbuf([128, TT], BF16, "rw_b")
        self.vt16 = S.sbuf([128, 4, 256], BF16, "vt16")
        self.kt16 = S.sbuf([128, 4, 128], BF16, "kt16")
        self.muvj = S.sbuf([128, 128], F32, "muvj")
        self.vpad = S.sbuf([128, 4, 2, 128], BF16, "vpad")
        self.memset("dve", self.vpad[:], 0.0)
        self.kpad = [S.sbuf([128, 128], BF16, "kpad") for _ in range(2)]
        for u_ in self.kpad:
            self.memset("dve", u_[:], 0.0)
        self.upad = [S.sbuf([128, 128], BF16, "upad") for _ in range(2)]
        for u_ in self.upad:
            self.memset("dve", u_[:], 0.0)
        self.gl_lo = S.sbuf([128, TT], BF16, "gl_lo")
        self.gl_l = S.sbuf([128, 4, 256], F32, "gl_l")
        self.gl_kt = S.sbuf([128, 4, 256], F32, "gl_kt")
        self.gd_ab = S.sbuf([128, 4, 8], F32, "gd_ab")
        self.gd_sc = S.sbuf([128, 4, 24], F32, "gd_sc")
        self.s16 = [S.sbuf([128, 128], BF16, "s16") for _ in range(26)]
        self.l16 = [S.sbuf([128, 128], BF16, "l16") for _ in range(26)]
        self.s32 = [S.sbuf([128, 128], F32, "s32") for _ in range(12)]
        self._s16i = self._l16i = self._s32i = 0
        self.s16w = [S.sbuf([128, 256], BF16, "s16w") for _ in range(4)]
        self._s16wi = 0

    def b16(self):
        self._s16i = (self._s16i + 1) % len(self.s16)
        return self.s16[self._s16i]

    def L16(self):
        self._l16i = (self._l16i + 1) % len(self.l16)
        return self.l16[self._l16i]

    def b32(self):
        self._s32i = (self._s32i + 1) % len(self.s32)
        return self.s32[self._s32i]

    def b16w(self):
        self._s16wi = (self._s16wi + 1) % len(self.s16w)
        return self.s16w[self._s16wi]

    def fm_chunk_psum(self, slab, cc, wide=False):
        p = self.bankx() if wide else self.bank()
        for k in range(8):
            self.mm(p[:], slab[:, k * 256 + cc * 128: k * 256 + cc * 128 + 128], self.xb[k][:], k == 0, k == 7)
        return p

    def raw_tile(self, l, ci, p):
        self._rawi = (self._rawi + 1) % len(self.raw)
        r = self.raw[self._rawi]
        self.cp("dve", r[:, 0:3], self.hcar[:, l, ci, :])
        self.cp("act", r[:, 3:3 + TT], p[:])
        self.cp("dve", self.hcar[:, l, ci, :], r[:, TT:TT + 3])
        return r

    def shift_chunk(self, l, ci, p, out):
        r = self.raw_tile(l, ci, p)
        t = self.tmp32()
        self.tt("dve", t[:], r[:, 2:2 + TT], r[:, 3:3 + TT], ALU.subtract)
        mu = self.pv[:, l, PV_MU + ci: PV_MU + ci + 1]
        self.stt("dve", out, t[:], mu, r[:, 3:3 + TT], ALU.mult, ALU.add)

    def mixers(self, l, ti):
        for k in range(8):
            self.tt("dve", self.dxb[k][:, 1:TT], self.xb[k][:, 0:TT - 1], self.xb[k][:, 1:TT], ALU.subtract)
            self.tt("dve", self.dxb[k][:, 0:1], self.xcar[:, l, k:k + 1], self.xb[k][:, 0:1], ALU.subtract)
            self.cp("dve", self.xcar[:, l, k:k + 1], self.xb[k][:, TT - 1:TT])
        for name, fn, b in (("r", self.rwkv, 0), ("g", self.gla, 1), ("d", self.gdn, 2)):
            if name in self.mixsel:
                fn(l)
            else:
                for j in range(4):
                    self.memset("dve", self.br[b][j][:], 0.0)
        if self.mixsel != "rgd":
            pass
        self.merge(l)

    def tri_inverse_multi(self, XPs):
        ident_f = self.cf[:, CI_ID, :]
        st = []
        for X, P in XPs:
            Z = self.b16()
            self.tt("dve", Z[:], X, ident_f, ALU.add)
            st.append([X, P, Z])
        for lvl in range(7):
            need_p = lvl <= 5
            need_x = lvl <= 4
            need_z = lvl >= 1
            pqs, xqs, zqs = [], [], []
            for c in st:
                if need_z:
                    zq = self.quarter()
                    self.mm(zq[:], c[1], c[2][:])
                    zqs.append(zq)
                if need_p:
                    pq = self.quarter()
                    self.mm(pq[:], c[0], c[1])
                    pqs.append(pq)
                if need_x:
                    xq = self.quarter()
                    self.mm(xq[:], c[1], c[0])
                    xqs.append(xq)
            for ci, c in enumerate(st):
                if need_z:
                    Zn = self.b16()
                    self.tt("dve", Zn[:], zqs[ci][:], c[2][:], ALU.add)
                    c[2] = Zn
                if need_p:
                    Pn = self.b16()
                    self.cp("act", Pn[:], pqs[ci][:])
                if need_x:
                    Xn = self.b16()
                    self.cp("act" if need_z else "dve", Xn[:], xqs[ci][:])
                    c[0] = Xn[:]
                if need_p:
                    c[1] = Pn[:]
        return [c[2] for c in st]

    def tri_inverse(self, X, P):
        return self.tri_inverse_multi([(X, P)])[0]

    def rwkv(self, l):
        cf, cb, sw = self.cf, self.cb, self.sw
        pvl = lambda c0, j: self.pv[:, l, c0 + j: c0 + j + 1]
        blk = cf[:, CI_BLK, :]
        idb = cb[:, CB_ID, :]
        KD = -float(np.exp(-0.5))
        t = self.tmp32()
        self.shift_chunk(l, 0, self.fm_chunk(l, 0), t[:])
        self.act(self.rw_tw[0:64, :], t[0:64, :], AF.Tanh)
        self.memset("dve", self.rw_tw[64:65, :], 1.0)
        self.cp("dve", self.rw_al[:], t[:])
        for jj in range(2):
            t = self.tmp32()
            self.shift_chunk(l, 1 + jj, self.fm_chunk(l, 1 + jj), t[:])
            self.act(self.rw_gs[:, jj, :], t[:], AF.Sigmoid)
        r, k, v, y = self.pool[0], self.pool[1], self.pool[2], self.pool[3]
        tmv = None
        for j in range(4):
            self.shift_chunk(l, 4 + 3 * j, self.fm_chunk(l, 4 + 3 * j), r[:])
            self.shift_chunk(l, 5 + 3 * j, self.fm_chunk(l, 5 + 3 * j), k[:])
            self.shift_chunk(l, 6 + 3 * j, self.fm_chunk(l, 6 + 3 * j), v[:])
            if j % 2 == 0:
                tmv = self.slabA(l, ("tm", j // 2))
            self.S.dma("sp", self.muvj.t[:], self.muvd[l, :, j * 128:(j + 1) * 128], writes=[self.muvj])
            for s in range(4):
                ts_ = slice(s * C, (s + 1) * C)
                q1 = self.quarter()
                q2 = self.quarter()
                c0 = (j % 2) * 128
                for kk_ in range(8):
                    self.mm(q1[:], self.xb[kk_][:, ts_], tmv[:, kk_ * 256 + c0: kk_ * 256 + c0 + 128], kk_ == 0, kk_ == 7)
                for kk_ in range(8):
                    self.mm(q2[:], self.dxb[kk_][:, ts_], tmv[:, kk_ * 256 + c0: kk_ * 256 + c0 + 128], kk_ == 0, kk_ == 7)
                tq_ = self.b32()
                self.tt("dve", tq_[:], q2[:], self.muvj[:], ALU.mult)
                self.tt("dve", self.vpad[:, s, 0, 0:64], tq_[:, 0:64], q1[:, 0:64], ALU.add)
                self.tt("dve", self.vpad[:, s, 1, 64:128], tq_[:, 64:128], q1[:, 64:128], ALU.add)
            p = self.bank()
            self.mm(p[:], sw[:, l, 512 + j * 128: 512 + (j + 1) * 128], self.rw_al[:])
            a = self.tmp32()
            self.act(a[:], p[:], AF.Sigmoid, bias=pvl(PV_A0, j))
            t1 = self.tmp32()
            self.ts("dve", t1[:], k[:], pvl(PV_KK, j), None, ALU.mult)
            sq = self.tmp32()
            self.act(sq[:], t1[:], AF.Square)
            p = self.bank()
            self.mm(p[:], blk, sq[:])
            rn = sq
            self.act(rn[:], p[:], AF.Ln, bias=self.epsb[:, 1:2])
            self.act(rn[:], rn[:], AF.Exp, scale=-0.5)
            self.tt("dve", self.rw_kk[:], t1[:], rn[:], ALU.mult)
            self.tt("dve", self.rw_b[:], self.rw_kk[:], a[:], ALU.mult)
            self.ts("dve", a[:], a[:], -1.0, pvl(PV_KA, j), ALU.add, ALU.mult)
            self.stt("dve", k[:], a[:], 1.0, k[:], ALU.add, ALU.mult)
            t3 = self.tmp32()
            self.stt("dve", t3[:], r[:], pvl(PV_RK, j), k[:], ALU.mult, ALU.mult)
            p = self.bank()
            self.mm(p[:], blk, t3[:])
            self.tt("dve", v[:], p[:], v[:], ALU.mult)
            for s in range(4):
                ts_ = slice(s * C, (s + 1) * C)
                qz = self.quarter()
                self.mm(qz[:], self.rw_tw[0:65, ts_], sw[0:65, l, j * 128:(j + 1) * 128])
                sg = self.b32()
                self.act(sg[:], qz[:], AF.Sigmoid)
                qc = self.quarter()
                self.mm(qc[:], sg[:], cf[:, CI_TLE, :])
                qp = self.quarter()
                self.mm(qp[:], sg[:], cf[:, CI_TLT, :])
                E1 = self.b32()
                self.act(E1[:], qc[:], AF.Exp, scale=KD)
                Ei = self.b32()
                self.act(Ei[:], qc[:], AF.Exp, scale=-KD)
                E0 = self.b32()
                self.act(E0[:], qp[:], AF.Exp, scale=KD)
                ARt = self.b16w()
                self.stt("dve", ARt[:, 0:128], self.rw_kk[:, ts_], -1.0, E0[:], ALU.mult, ALU.mult)
                self.tt("dve", ARt[:, 128:256], r[:, ts_], E1[:], ALU.mult)
                Bt = self.L16()
                self.tt("dve", Bt[:], self.rw_b[:, ts_], Ei[:], ALU.mult)
                Kt = self.L16()
                self.tt("dve", Kt[:], k[:, ts_], Ei[:], ALU.mult)
                tq = self.tbq()
                self.tr(tq[:], Bt[:], idb)
                Btm = self.L16()
                self.copy_any(Btm[:], tq[:])
                tq = self.tbq()
                self.tr(tq[:], Kt[:], idb)
                Ktm = self.L16()
                self.copy_any(Ktm[:], tq[:])
                hm = (cf[:, CI_BLK, 0:1], cf[:, CI_BLK, 64:65])
                Sblk = self.Srwb[:, l, j, :]
                qy = self.QF[0]
                self.mm(qy[:], Sblk, ARt[:, 128:256], True, False)
                pre = []
                for hh in range(2):
                    Bth = self.L16()
                    self.ts("dve", Bth[:], Bt[:], hm[hh], None, ALU.mult)
                    Kth = self.L16()
                    self.ts("dve", Kth[:], Kt[:], hm[hh], None, ALU.mult)
                    Ath = self.L16()
                    self.ts("dve", Ath[:], ARt[:, 0:128], hm[hh], None, ALU.mult)
                    pb_ = self.bank()
                    self.mm(pb_[:, 0:256], Bth[:], ARt[:])
                    self.mm(pb_[:, 256:512], Kth[:], ARt[:])
                    qP = self.quarter()
                    self.mm(qP[:], Ath[:], Bt[:])
                    X = self.L16()
                    self.tt("dve", X[:], pb_[:, 0:128], cf[:, CI_TLT, :], ALU.mult)
                    NBT = self.L16()
                    self.tt("dve", NBT[:], pb_[:, 128:256], cf[:, CI_TLE, :], ALU.mult)
                    MKT = self.L16()
                    self.tt("dve", MKT[:], pb_[:, 256:384], cf[:, CI_TLT, :], ALU.mult)
                    NKT = self.L16()
                    self.tt("dve", NKT[:], pb_[:, 384:512], cf[:, CI_TLE, :], ALU.mult)
                    P = self.L16()
                    self.tt("dve", P[:], qP[:], cf[:, CI_TGT, :], ALU.mult)
                    pre.append((X, P, NBT, MKT, NKT))
                Zs = self.tri_inverse_multi([(pr[0][:], pr[1][:]) for pr in pre])
                for hh in range(2):
                    hs = slice(hh * 64, hh * 64 + 64)
                    X, P, NBT, MKT, NKT = pre[hh]
                    Z = Zs[hh]
                    V_ = self.vpad[:, s, hh, hs]
                    qr = self.quarter()
                    self.mm(qr[:, 0:64], ARt[:, 0:128], self.Srwb[:, l, j, hs], True, False)
                    self.mm(qr[:, 0:64], MKT[:], V_, False, True)
                    RHS = self.L16()
                    self.copy_any(RHS[:, 0:64], qr[:, 0:64])
                    qu = self.quarter()
                    self.mm(qu[:, 0:64], Z[:], RHS[:, 0:64])
                    self.copy_any(self.upad[hh][:, hs], qu[:, 0:64])
                    self.mm(qy[:], self.upad[hh][:], NBT[:], False, False)
                    self.mm(qy[:], self.vpad[:, s, hh, :], NKT[:], False, hh == 1)
                self.copy_any(y[:, ts_], qy[:])
                qs = self.QF[1]
                self.mm(qs[:], Btm[:], self.upad[0][:], True, False)
                self.mm(qs[:], Btm[:], self.upad[1][:], False, False)
                self.mm(qs[:], Ktm[:], self.vpad[:, s, 0, :], False, False)
                self.mm(qs[:], Ktm[:], self.vpad[:, s, 1, :], False, True)
                tmpS = self.b32()
                self.tt("dve", tmpS[:], qs[:], cf[:, CI_BLK, :], ALU.mult)
                Sf = self.Srw[:, l, j, :]
                self.tt("dve", Sf, tmpS[:], Sf, ALU.add)
                self.ts("dve", Sf, Sf, E1[:, 127:128], None, ALU.mult)
                self.cp("act", self.Srwb[:, l, j, :], Sf)
            pg_ = self.bank()
            self.mm(pg_[:], sw[:, l, 1024 + j * 128: 1024 + (j + 1) * 128], self.rw_gs[:, 0, :], True, False)
            self.mm(pg_[:], sw[0:32, l, 1536 + j * 128: 1536 + (j + 1) * 128], self.rw_gs[0:32, 1, :], False, True)
            p = self.bank()
            self.mm(p[:], blk, y[:])
            d = self.tmp32()
            self.stt("dve", d[:], p[:], -1.0 / 64, y[:], ALU.mult, ALU.add)
            sq = self.tmp32()
            self.act(sq[:], d[:], AF.Square)
            p2 = self.bank()
            self.mm(p2[:], blk, sq[:])
            rs = sq
            self.act(rs[:], p2[:], AF.Ln, bias=self.epsb[:, 2:3], scale=1.0 / 64)
            self.act(rs[:], rs[:], AF.Exp, scale=-0.5)
            self.tt("dve", d[:], d[:], rs[:], ALU.mult)
            self.act(d[:], d[:], AF.Identity, bias=pvl(PV_GNB, j), scale=pvl(PV_GNG, j))
            self.tt("dve", d[:], d[:], v[:], ALU.add)
            self.tt("dve", self.br[0][j][:], d[:], pg_[:], ALU.mult)
            self.dump("o_rw", d[:], [128, TT])

    def gla(self, l):
        cf, cb, sw = self.cf, self.cb, self.sw
        p = self.fm_chunk(l, 16)
        self.cp("act", self.gl_lo[0:32, :], p[0:32, :])
        self.memset("dve", self.gl_lo[32:33, :], 1.0)
        if self.cut(0, 1):
            return
        tmk = self.slabA(l, ("tm", 2))
        for s in range(4):
            ts_ = slice(s * C, (s + 1) * C)
            p = self.bank()
            self.mm(p[:, 0:256], self.gl_lo[0:33, ts_], sw[0:33, l, 2048:2304])
            e = self.tmp32()
            self.act(e[:, 0:256], p[:, 0:256], AF.Exp, scale=-1.0)
            self.act(self.gl_l[:, s, :], e[:, 0:256], AF.Ln, bias=self.epsb[:, 3:4])
            pk = self.bank()
            for k in range(8):
                self.mm(pk[:, 0:256], self.xb[k][:, ts_], tmk[:, k * 256:(k + 1) * 256], k == 0, k == 7)
            self.copy_any(self.gl_kt[:, s, :], pk[:, 0:256])
        if self.cut(1, 1):
            return
        q, k_, g0, g1, o0, o1 = self.pool[0:6]
        for j in range(2):
            self.copy_any(q[:], self.fm_chunk(l, 18 + 4 * j)[:])
            self.copy_any(k_[:], self.fm_chunk(l, 19 + 4 * j)[:])
            self.act(g0[:], self.fm_chunk(l, 20 + 4 * j)[:], AF.Silu)
            self.act(g1[:], self.fm_chunk(l, 21 + 4 * j)[:], AF.Silu)
            tmv = self.slabA(l, ("tm", 3 + j))
            for s in range(4):
                ts_ = slice(s * C, (s + 1) * C)
                pv_ = self.bank()
                for k in range(8):
                    self.mm(pv_[:, 0:256], self.xb[k][:, ts_], tmv[:, k * 256:(k + 1) * 256], k == 0, k == 7)
                self.copy_any(self.vt16[:, s, :], pv_[:, 0:256])
            if self.cut(2, 1):
                return
            for s in range(4):
                ts_ = slice(s * C, (s + 1) * C)
                lj = self.gl_l[:, s, j * 128:(j + 1) * 128]
                qk = self.quarter()
                self.mm(qk[:], cf[:, CI_TGT, :], lj)
                ek = self.b32()
                self.act(ek[:], qk[:], AF.Exp, scale=-1.0 / 16)
                for hh in range(2):
                    c0_, c1_ = hh * 64, hh * 64 + 64
                    self.tt("dve", self.kpad[hh][:, c0_:c1_], self.gl_kt[:, s, j * 128 + c0_: j * 128 + c1_],
                            ek[:, c0_:c1_], ALU.mult)
                qsf = self.QF[2]
                qb = self.quarter()
                self.mm(qb[:], lj, cf[:, CI_TLE, :])
                Eb = self.b32()
                self.act(Eb[:], qb[:], AF.Exp, scale=-1.0 / 16)
                Ein = self.b32()
                self.act(Ein[:], qb[:], AF.Exp, scale=1.0 / 16)
                qt = self.L16()
                self.stt("dve", qt[:], q[:, ts_], 0.125, Eb[:], ALU.mult, ALU.mult)
                hm = (cf[:, CI_BLK, 0:1], cf[:, CI_BLK, 64:65])
                for hh in range(2):
                    hs = slice(hh * 64, hh * 64 + 64)
                    if hh == 1 and self.cut(6, 1):
                        return
                    kth = self.L16()
                    self.stt("dve", kth[:], k_[:, ts_], hm[hh], Ein[:], ALU.mult, ALU.mult)
                    qth = self.L16()
                    self.ts("dve", qth[:], qt[:], hm[hh], None, ALU.mult)
                    if hh == 1 and self.cut(7, 1):
                        return
                    qa = self.quarter()
                    self.mm(qa[:], kth[:], qt[:])
                    attT = self.b16()
                    self.tt("dve", attT[:], qa[:], cf[:, CI_TLE, :], ALU.mult)
                    if hh == 1 and self.cut(8, 1):
                        return
                    V_ = self.vt16[:, s, hh * 128:(hh + 1) * 128]
                    qo = self.quarter()
                    self.mm(qo[:], V_, attT[:], True, False)
                    self.mm(qo[:], self.Sglb[:, l, j, :], qth[:], False, True)
                    self.copy_any((o0, o1)[hh][:, ts_], qo[:])
                    if hh == 1 and self.cut(9, 1):
                        return
                    self.mm(qsf[:], self.kpad[hh][:], V_, hh == 0, hh == 1)
                Sf = self.Sgl[:, l, j, :]
                self.stt("dve", Sf, Sf, Eb[:, 127:128], qsf[:], ALU.mult, ALU.add)
                self.cp("act", self.Sglb[:, l, j, :], Sf)
            self.head_rms(l, o0, g0, PV_GLAN, self.br[1][2 * j])
            self.head_rms(l, o1, g1, PV_GLAN, self.br[1][2 * j + 1])

    def cut(self, n, b):
        import os
        c = int(os.environ.get("GCUT", "99"))
        if n >= c:
            for j in range(4):
                self.memset("dve", self.br[b][j][:], 0.0)
            return True
        return False

    def head_rms(self, l, o, gate, pvcol, dst):
        sq = self.tmp32()
        self.act(sq[:], o[:], AF.Square)
        p = self.bank()
        self.mm(p[:], self.cf[:, CI_V128, :], sq[:])
        rs = sq
        self.act(rs[:], p[:], AF.Ln, bias=self.epsb[:, 1:2])
        self.act(rs[:], rs[:], AF.Exp, scale=-0.5)
        t = self.tmp32()
        self.stt("dve", t[:], o[:], self.pv[:, l, pvcol:pvcol + 1], rs[:], ALU.mult, ALU.mult)
        self.tt("dve", dst[:], t[:], gate[:], ALU.mult)

    def gdn(self, l):
        cf, cb = self.cf, self.cb
        ones_f = cf[:, CI_ONE, :]
        idb = cb[:, CB_ID, :]
        tmab = self.slabA(l, ("tm", 5))
        sc = self.gd_sc
        for s in range(4):
            ts_ = slice(s * C, (s + 1) * C)
            pa = self.quarter()
            for k in range(8):
                self.mm(pa[:, 0:8], self.xb[k][:, ts_], tmab[:, k * 256:k * 256 + 8], k == 0, k == 7)
            self.cp("dve", self.gd_ab[:, s, :], pa[:, 0:8])
            self.tt("dve", sc[:, s, 0:4], self.gd_ab[:, s, 0:4], self.pv[:, l, PV_DTB:PV_DTB + 4], ALU.add)
            self.act(sc[:, s, 0:4], sc[:, s, 0:4], AF.Exp)
            self.act(sc[:, s, 0:4], sc[:, s, 0:4], AF.Ln, bias=self.epsb[:, 3:4])
            self.tt("dve", sc[:, s, 0:4], sc[:, s, 0:4], self.nega[:, l, :], ALU.mult)
            self.act(sc[:, s, 4:8], self.gd_ab[:, s, 4:8], AF.Sigmoid)
            qg = self.quarter()
            self.mm(qg[:, 0:4], cf[:, CI_TLE, :], sc[:, s, 0:4])
            self.mm(qg[:, 4:8], cf[:, CI_TGT, :], sc[:, s, 0:4])
            self.act(sc[:, s, 8:16], qg[:, 0:8], AF.Exp)
            self.tt("dve", sc[:, s, 16:20], sc[:, s, 4:8], sc[:, s, 8:12], ALU.mult)
            self.ts("dve", sc[:, s, 20:24], sc[:, s, 4:8], -1.0, None, ALU.mult)
        q, k_, v, gate, o = self.pool[0:5]
        for h in range(4):
            for which, dst in ((0, q), (1, k_), (2, v)):
                cidx = which * 4 + h
                r = self.raw_tile(l, 16 + cidx, self.fm_chunk(l, 26 + 4 * h + which))
                t = self.tmp32()
                cw = lambda tap: self.pv[:, l, PV_CONV + tap * 12 + cidx: PV_CONV + tap * 12 + cidx + 1]
                self.ts("dve", t[:], r[:, 0:TT], cw(0), None, ALU.mult)
                for tap in (1, 2, 3):
                    self.stt("dve", t[:], r[:, tap:tap + TT], cw(tap), t[:], ALU.mult, ALU.add)
                self.act(dst[:], t[:], AF.Silu)
            self.act(gate[:], self.fm_chunk(l, 29 + 4 * h)[:], AF.Silu)
            for src, scale in ((q, 128.0 ** -0.5), (k_, 1.0)):
                sq = self.tmp32()
                self.act(sq[:], src[:], AF.Square)
                p = self.bank()
                self.mm(p[:], ones_f, sq[:])
                rs = sq
                self.act(rs[:], p[:], AF.Ln, bias=self.epsb[:, 1:2])
                self.act(rs[:], rs[:], AF.Exp, scale=-0.5)
                self.stt("dve", src[:], src[:], scale, rs[:], ALU.mult, ALU.mult)
            for src, dstt in ((k_, self.kt16), (v, self.vt16)):
                p = self.bank()
                for s in range(4):
                    self.tr(p[:, s * 128:(s + 1) * 128], src[:, s * C:(s + 1) * C], cf[:, CI_ID, :])
                for s in range(4):
                    self.copy_any(dstt[:, s, 0:128], p[:, s * 128:(s + 1) * 128])
            for s0 in (0, 2):
                cx = []
                for s in (s0, s0 + 1):
                    ts_ = slice(s * C, (s + 1) * C)
                    c = dict(s=s, ts=ts_)
                    c["knT"] = self.L16()
                    self.copy_any(c["knT"][:], k_[:, ts_])
                    c["GT"] = self.b32()
                    self.ts("dve", c["GT"][:], cf[:, CI_TLE, :], sc[:, s, h:h + 1], None, ALU.mult)
                    c["GB"] = self.b32()
                    self.ts("dve", c["GB"][:], cf[:, CI_ONE, :], sc[:, s, h:h + 1], None, ALU.mult)
                    cx.append(c)
                for c in cx:
                    GT, GB, knT = c["GT"], c["GB"], c["knT"]
                    qd = self.quarter()
                    self.mm(qd[:], GT[:], cf[:, CI_ONE, :], True, False)
                    self.mm(qd[:], GB[:], cf[:, CI_NTLE, :], False, False)
                    self.mm(qd[:], idb, cb[:, CB_MBSL, :], False, True)
                    qe = self.quarter()
                    self.mm(qe[:], GB[:], cf[:, CI_TLE, :], True, False)
                    self.mm(qe[:], GT[:], cf[:, CI_NEGONE, :], False, False)
                    self.mm(qe[:], idb, cb[:, CB_MBIU, :], False, True)
                    qbc = self.quarter()
                    self.mm(qbc[:], GB[:], cf[:, CI_TLE, :])
                    qG = self.quarter()
                    self.mm(qG[:], knT[:], knT[:])
                    c.update(qd=qd, qe=qe, qbc=qbc, qG=qG)
                for c in cx:
                    s = c["s"]
                    c["Dsl"] = self.b32()
                    self.act(c["Dsl"][:], c["qd"][:], AF.Exp)
                    c["Diu"] = self.b32()
                    self.act(c["Diu"][:], c["qe"][:], AF.Exp)
                    c["bcE"] = self.b32()
                    self.act(c["bcE"][:], c["qbc"][:], AF.Exp)
                    c["P"] = self.L16()
                    self.stt("dve", c["P"][:], c["qG"][:], sc[:, s, 20 + h:21 + h], c["Dsl"][:], ALU.mult, ALU.mult)
                for c in cx:
                    tq = self.tbq()
                    self.tr(tq[:], c["P"][:], idb)
                    c["tq"] = tq
                for c in cx:
                    c["X"] = self.L16()
                    self.copy_any(c["X"][:], c["tq"][:])
                Zs = self.tri_inverse_multi([(c["X"][:], c["P"][:]) for c in cx])
                for ci, c in enumerate(cx):
                    s, ts_ = c["s"], c["ts"]
                    c["Z"] = Zs[ci]
                    kn_tm = self.kt16[:, s, :]
                    v_tm = self.vt16[:, s, 0:128]
                    c["qnT"] = self.L16()
                    self.copy_any(c["qnT"][:], q[:, ts_])
                    c["qgT"] = self.L16()
                    self.tt("dve", c["qgT"][:], q[:, ts_], c["bcE"][:], ALU.mult)
                    c["kbg"] = self.L16()
                    self.ts("dve", c["kbg"][:], kn_tm, sc[:, s, 16 + h:17 + h], None, ALU.mult)
                    c["kdec"] = self.L16()
                    self.ts("dve", c["kdec"][:], kn_tm, sc[:, s, 12 + h:13 + h], None, ALU.mult)
                    c["vb"] = self.L16()
                    self.ts("dve", c["vb"][:], v_tm, sc[:, s, 4 + h:5 + h], None, ALU.mult)
                for c in cx:
                    qa = self.quarter()
                    self.mm(qa[:], c["knT"][:], c["qnT"][:])
                    qw = self.quarter()
                    self.mm(qw[:], c["kbg"][:], c["Z"][:])
                    c.update(qa=qa, qw=qw)
                for c in cx:
                    c["attT"] = self.L16()
                    self.tt("dve", c["attT"][:], c["qa"][:], c["Diu"][:], ALU.mult)
                    c["nwT"] = self.L16()
                    self.act(c["nwT"][:], c["qw"][:], AF.Copy, scale=-1.0)
                for c in cx:
                    ts_ = c["ts"]
                    Sb = self.Sgdb[:, l, h, :]
                    qv = self.quarter()
                    self.mm(qv[:], c["Z"][:], c["vb"][:], True, False)
                    self.mm(qv[:], c["nwT"][:], Sb, False, True)
                    vnew = self.L16()
                    self.copy_any(vnew[:], qv[:])
                    qo = self.quarter()
                    self.mm(qo[:], Sb, c["qgT"][:], True, False)
                    self.mm(qo[:], vnew[:], c["attT"][:], False, True)
                    self.copy_any(o[:, ts_], qo[:])
                    qs = self.quarter()
                    self.mm(qs[:], c["kdec"][:], vnew[:])
                    Sf = self.Sgd[:, l, h, :]
                    self.stt("dve", Sf, Sf, c["bcE"][:, 127:128], qs[:], ALU.mult, ALU.add)
                    self.cp("act", self.Sgdb[:, l, h, :], Sf)
            self.head_rms(l, o, gate, PV_GDNN, self.br[2][h])

    def merge(self, l):
        macc = self.z
        for b in range(3):
            for j in range(4):
                self.dump("br", self.br[b][j][:], [128, TT])
        for b in range(3):
            for half in range(2):
                wb = self.slabA(l, ("br", b, half))
                for mm_ in range(2):
                    wg = self.slabA(l, ("gate", b * 4 + half * 2 + mm_))
                    for cc in range(2):
                        m = half * 4 + mm_ * 2 + cc
                        pg = self.fm_chunk_psum(wg, cc, wide=True)
                        pp = self.bankx()
                        for k in range(4):
                            c0 = k * 512 + (mm_ * 2 + cc) * 128
                            self.mm(pp[:], wb[:, c0:c0 + 128], self.br[b][k][:], k == 0, k == 3)
                        t = self.tmp32()
                        self.act(t[:], pg[:], AF.Sigmoid)
                        if b == 0:
                            self.tt("dve", macc[m][:], t[:], pp[:], ALU.mult)
                        else:
                            self.tt("dve", t[:], t[:], pp[:], ALU.mult)
                            if b == 1:
                                self.tt("dve", macc[m][:], macc[m][:], t[:], ALU.add)
                            else:
                                self.tt("dve", self.merged[m][:], macc[m][:], t[:], ALU.add)
        wos = [self.slabA(l, ("wo", i)) for i in range(4)]
        for m in range(8):
            p = self.bankx()
            sl = wos[m // 2]
            for k in range(8):
                self.mm(p[:], sl[:, k * 256 + (m % 2) * 128: k * 256 + (m % 2) * 128 + 128], self.merged[k][:], k == 0, k == 7)
            self.stt("dve", self.z[m][:], self.x[m][:], ALPHA, p[:], ALU.mult, ALU.add)


_CACHE = {}


def prep_weights(inputs, plan):
    inp = {k: np.asarray(v, np.float32) for k, v in inputs.items() if k not in ("x", "p")}
    per = [prep_layer(inp, l, plan) for l in range(2)]
    cf, cb = build_consts()
    return dict(wA=np.ascontiguousarray(np.stack([p[0] for p in per], 0)),
                wB=np.ascontiguousarray(np.stack([p[1] for p in per], 0)),
                pv=np.ascontiguousarray(np.stack([p[2] for p in per], 0)),
                sw=np.ascontiguousarray(np.stack([p[3] for p in per], 0)),
                muv=np.ascontiguousarray(np.stack([p[4] for p in per], 0)),
                cf=cf, cb=cb)


def run(inputs, T, layers=(0, 1), n_cores=8, debug=(), stages=4, mixsel="rgd"):
    x = np.asarray(inputs["x"], np.float32)
    p = np.asarray(inputs["p"], np.float32)
    B = x.shape[0]
    key = (T, tuple(layers), tuple(debug), stages, mixsel)
    if key not in _CACHE:
        _CACHE[key] = Prog(T, list(layers), debug, stages, mixsel)
    prog = _CACHE[key]
    w = prep_weights(inputs, prog.plan)
    in_maps = []
    for c in range(n_cores):
        b = c % B
        m = dict(w)
        m["xT"] = np.ascontiguousarray(x[b, :T].T)
        m["pT"] = np.ascontiguousarray(p[:, b, :T].transpose(0, 2, 1))
        in_maps.append(m)
    res = run_bass_kernel_spmd(prog.nc, in_maps, core_ids=list(range(n_cores)))
    out = np.stack([np.ascontiguousarray(res.results[b]["outT"].T) for b in range(B)], 0)
    return out.astype(np.float32), res, prog


def kernel(**inputs):
    out, _, _ = run(inputs, T=4096)
    return out
```

```python
import contextlib
import numpy as np
import concourse.bass as bass
import concourse.mybir as mybir
from concourse.bass_utils import run_bass_kernel_spmd

F32 = mybir.dt.float32
BF16 = mybir.dt.bfloat16
AF = mybir.ActivationFunctionType
ALU = mybir.AluOpType
AX = mybir.AxisListType

D = 1024
DFF = 2816
TT = 512
C = 128
ALPHA = 4.0 ** 0.25
LN_EPS = 1e-5
SAME_ENGINE_SYNC = True
NSLOT = 5
SLOT = 2816


class V:
    __slots__ = ("b", "ap")

    def __init__(self, b, ap):
        self.b = b
        self.ap = ap


class Buf:
    __slots__ = ("t", "name", "last_write", "reads", "dsem", "dcnt", "c0", "owner", "excl")

    def __init__(self, t, name, c0=None, owner=None):
        self.owner = owner
        self.excl = False
        self.t = t
        self.name = name
        self.last_write = None
        self.reads = []
        self.dsem = None
        self.dcnt = 0
        self.c0 = c0

    def __getitem__(self, idx):
        if self.c0 is None:
            return V(self, self.t[idx])
        if not isinstance(idx, tuple):
            idx = (idx, slice(None))
        r, c = idx
        a = 0 if c.start is None else c.start
        b = 128 if c.stop is None else c.stop
        return V(self.owner or self, self.t[r, self.c0 + a: self.c0 + b])


class Sched:
    ENGS = ("pe", "act", "dve", "pool", "sp")

    def __init__(self, nc):
        self.nc = nc
        self.stack = contextlib.ExitStack()
        self.q = {e: [] for e in self.ENGS}
        self.sem = {}
        self.cnt = {e: 0 for e in self.ENGS}
        self.seen = {e: {} for e in self.ENGS}
        for e in self.ENGS:
            self.sem[e] = self.stack.enter_context(nc.semaphore("s_" + e))
        self.nbuf = 0
        self.ninst = 0
        self.rr = 0

    def sbuf(self, shape, dtype=F32, name=None):
        self.nbuf += 1
        name = (name or "b") + f"_{self.nbuf}"
        t = self.stack.enter_context(self.nc.sbuf_tensor(name, list(shape), dtype))
        return Buf(t, name)

    def psum(self, shape, dtype=F32, name=None):
        self.nbuf += 1
        name = (name or "p") + f"_{self.nbuf}"
        t = self.stack.enter_context(self.nc.psum_tensor(name, list(shape), dtype))
        b = Buf(t, name)
        b.excl = True
        return b

    def dsem_for(self, buf):
        if buf.dsem is None:
            buf.dsem = self.stack.enter_context(self.nc.semaphore("d_" + buf.name))
        return buf.dsem

    def _deps(self, eng, reads, writes):
        waits = {}

        def add(rec):
            if rec is None:
                return
            s, v, owner = rec
            if owner == eng and (eng == "pe" or not SAME_ENGINE_SYNC):
                return
            k = id(s)
            if k not in waits or waits[k][1] < v:
                waits[k] = (s, v)

        for b in reads:
            add(b.last_write)
            if b.excl:
                for r in b.reads:
                    if r[2] != eng:
                        add(r)
        for b in writes:
            add(b.last_write)
            for r in b.reads:
                add(r)
        out = []
        seen = self.seen[eng]
        for k, (s, v) in waits.items():
            if seen.get(k, 0) >= v:
                continue
            seen[k] = v
            out.append((s, v))
        return out

    def op(self, eng, fn, reads=(), writes=()):
        waits = self._deps(eng, reads, writes)
        self.cnt[eng] += 1
        v = self.cnt[eng]
        s = self.sem[eng]
        self.q[eng].append((fn, waits, (s, 1)))
        rec = (s, v, eng)
        for b in reads:
            if len(b.reads) > 24:
                b.reads = b.reads[-24:] if False else b.reads
            b.reads.append(rec)
        for b in writes:
            b.last_write = rec
            b.reads = []
        self.ninst += 1

    def dma(self, eng, out_ap, in_ap, reads=(), writes=(), sembuf=None):
        sembuf = sembuf or (writes[0] if writes else reads[0])
        ds = self.dsem_for(sembuf)
        waits = self._deps(eng, reads, writes)
        sembuf.dcnt += 16
        v = sembuf.dcnt
        self.q[eng].append((lambda e: e.dma_start(out=out_ap, in_=in_ap), waits, (ds, 16)))
        rec = (ds, v, "dma")
        for b in reads:
            b.reads.append(rec)
        for b in writes:
            b.last_write = rec
            b.reads = []
        self.ninst += 1

    def final_wait(self, eng, bufs):
        waits = self._deps(eng, bufs, bufs)
        self.q[eng].append((None, waits, None))

    def finish(self):
        nc = self.nc
        q = self.q

        def replay(name):
            def f(e):
                for fn, waits, inc in q[name]:
                    for s, v in waits:
                        e.wait_ge(s, v)
                    if fn is None:
                        continue
                    ins = fn(e)
                    if inc is not None:
                        ins.then_inc(inc[0], inc[1])
            return f

        with nc.Block() as block:
            block.tensor(replay("pe"))
            block.scalar(replay("act"))
            block.vector(replay("dve"))
            block.gpsimd(replay("pool"))
            block.sync(replay("sp"))
        self.stack.close()


def _bufs(*xs):
    out = []
    for x in xs:
        if isinstance(x, V) and x.b not in out:
            out.append(x.b)
    return out


def _ap(x):
    return x.ap if isinstance(x, V) else x


def slabify(W, wc):
    K, N = W.shape
    kc = K // 128
    return np.ascontiguousarray(
        W.reshape(kc, 128, N // wc, wc).transpose(2, 1, 0, 3).reshape(N // wc, 128, kc * wc))


def colvec(v):
    return np.ascontiguousarray(v.reshape(-1, 128).T)


def pad_cols(W, n):
    out = np.zeros((W.shape[0], n), np.float32)
    out[:, :W.shape[1]] = W
    return out


RW0 = 0
GLA0 = 1824
GDN0 = 3376
GATE0 = 5432
NFM = 42
PV_LNG, PV_LNB = 0, 32
PV_MU = 64
PV_A0, PV_KK, PV_KA, PV_RK, PV_GNG, PV_GNB = 80, 84, 88, 92, 96, 100
PV_GLAN, PV_GDNN = 104, 105
PV_CONV = 106
PV_ALOG, PV_DTB = 154, 158
NPV = 162
CI_ID, CI_ONE, CI_NEGONE, CI_LN, CI_V128, CI_BLK, CI_TLE, CI_TLT, CI_TGT, CI_NTLE, CI_MBSL, CI_MBIU = range(12)
NCI = 12
CB_ID, CB_MBSL, CB_MBIU = 0, 1, 2


def build_consts():
    p = np.arange(128)[:, None]
    j = np.arange(128)[None, :]
    m = np.zeros((NCI, 128, 128), np.float32)
    m[CI_ID] = (p == j)
    m[CI_ONE] = 1.0
    m[CI_NEGONE] = -1.0
    m[CI_LN] = 1.0 / 1024
    m[CI_V128] = 1.0 / 128
    m[CI_BLK] = ((p // 64) == (j // 64))
    m[CI_TLE] = (p <= j)
    m[CI_TLT] = (p < j)
    m[CI_TGT] = (p > j)
    m[CI_NTLE] = -(p <= j).astype(np.float32)
    m[CI_MBSL] = np.where(p > j, 0.0, -30000.0)
    m[CI_MBIU] = np.where(p <= j, 0.0, -30000.0)
    cf = np.ascontiguousarray(m.transpose(1, 0, 2).reshape(128, NCI * 128))
    cbm = np.stack([m[CI_ID], m[CI_MBSL], m[CI_MBIU]], 0)
    cb = np.ascontiguousarray(cbm.transpose(1, 0, 2).reshape(128, 3 * 128))
    return cf, cb


def regroup_win(w_in):
    fm = np.zeros((1024, NFM * 128), np.float32)

    def put(ch, src, n):
        fm[:, ch * 128: ch * 128 + n] = w_in[:, src: src + n]
    put(0, RW0 + 1536, 128)
    put(1, RW0 + 1664, 128)
    put(2, RW0 + 1792, 32)
    for j in range(4):
        put(4 + 3 * j, RW0 + j * 128, 128)
        put(5 + 3 * j, RW0 + 512 + j * 128, 128)
        put(6 + 3 * j, RW0 + 1024 + j * 128, 128)
    put(16, GLA0 + 1024, 16)
    for j in range(2):
        put(18 + 4 * j, GLA0 + j * 128, 128)
        put(19 + 4 * j, GLA0 + 256 + j * 128, 128)
        put(20 + 4 * j, GLA0 + 1040 + (2 * j) * 128, 128)
        put(21 + 4 * j, GLA0 + 1040 + (2 * j + 1) * 128, 128)
    for h in range(4):
        put(26 + 4 * h, GDN0 + h * 128, 128)
        put(27 + 4 * h, GDN0 + 512 + h * 128, 128)
        put(28 + 4 * h, GDN0 + 1024 + h * 128, 128)
        put(29 + 4 * h, GDN0 + 1544 + h * 128, 128)
    tm = np.zeros((1024, 6 * 256), np.float32)
    tm[:, 0:512] = w_in[:, RW0 + 1024: RW0 + 1536]
    tm[:, 512:768] = w_in[:, GLA0 + 256: GLA0 + 512]
    tm[:, 768:1280] = w_in[:, GLA0 + 512: GLA0 + 1024]
    tm[:, 1280:1288] = w_in[:, GDN0 + 1536: GDN0 + 1544]
    gate = w_in[:, GATE0: GATE0 + 3072]
    return fm, tm, gate


def prep_layer(inp, l, plan):
    A = {}
    for f in range(2):
        s1 = slabify(inp["ffn_w1"][l, f], 256)
        s3 = slabify(inp["ffn_w3"][l, f], 256)
        for g in range(11):
            A[("w1", f, g)] = s1[g]
            A[("w3", f, g)] = s3[g]
    fm, tm, gate = regroup_win(inp["w_in"][l])
    for i, s in enumerate(slabify(fm, 256)):
        A[("fm", i)] = s
    for i, s in enumerate(slabify(tm, 256)):
        A[("tm", i)] = s
    for i, s in enumerate(slabify(gate, 256)):
        A[("gate", i)] = s
    for b in range(3):
        sb = slabify(inp["w_branch"][l, b], 512)
        for half in range(2):
            A[("br", b, half)] = sb[half]
    for i, s in enumerate(slabify(inp["w_o"][l], 256)):
        A[("wo", i)] = s
    for i, s in enumerate(slabify(inp["ple_w_gate"][l], 256)):
        A[("pg", i)] = s
    A[("pp", 0)] = slabify(inp["ple_w_proj"][l], 1024)[0]
    lst = [A[k] for k in plan]
    while len(lst) < NA_PER_LAYER:
        lst.append(lst[0])
    wA = np.ascontiguousarray(np.stack(lst, 0))
    wB = np.ascontiguousarray(np.concatenate([slabify(inp["ffn_w2"][l, f], 128) for f in range(2)], 0))
    pv = np.zeros((128, NPV), np.float32)
    for i in range(4):
        pv[:, PV_LNG + i * 8: PV_LNG + i * 8 + 8] = colvec(inp["ln_g"][l, i])
        pv[:, PV_LNB + i * 8: PV_LNB + i * 8 + 8] = colvec(inp["ln_b"][l, i])
    mu = inp["rw_mu"][l]
    pv[:, PV_MU + 0] = mu[1536:1664]
    pv[:, PV_MU + 1] = mu[1664:1792]
    pv[0:32, PV_MU + 2] = mu[1792:1824]
    for j in range(4):
        pv[:, PV_MU + 4 + 3 * j] = mu[j * 128:(j + 1) * 128]
        pv[:, PV_MU + 5 + 3 * j] = mu[512 + j * 128: 512 + (j + 1) * 128]
        pv[:, PV_MU + 6 + 3 * j] = mu[1024 + j * 128: 1024 + (j + 1) * 128]
    pv[:, PV_A0: PV_A0 + 4] = colvec(inp["rw_a0"][l])
    pv[:, PV_KK: PV_KK + 4] = colvec(inp["rw_k_k"][l])
    pv[:, PV_KA: PV_KA + 4] = colvec(inp["rw_k_a"][l])
    pv[:, PV_RK: PV_RK + 4] = colvec(inp["rw_r_k"][l].reshape(-1))
    pv[:, PV_GNG: PV_GNG + 4] = colvec(inp["rw_gn_g"][l])
    pv[:, PV_GNB: PV_GNB + 4] = colvec(inp["rw_gn_b"][l])
    pv[:, PV_GLAN] = inp["gla_norm_g"][l]
    pv[:, PV_GDNN] = inp["gdn_norm_g"][l]
    for tap in range(4):
        pv[:, PV_CONV + tap * 12: PV_CONV + tap * 12 + 12] = colvec(inp["gdn_conv_w"][l, tap])
    pv[:, PV_ALOG: PV_ALOG + 4] = inp["gdn_a_log"][l][None, :]
    pv[:, PV_DTB: PV_DTB + 4] = inp["gdn_dt_bias"][l][None, :]
    sw = np.zeros((128, 2304), np.float32)
    sw[0:64, 0:512] = inp["rw_w2"][l]
    sw[64, 0:512] = inp["rw_w0"][l]
    sw[64:128, 512:1024] = inp["rw_a2"][l]
    sw[0:128, 1024:1536] = inp["rw_g2"][l][0:128]
    sw[0:32, 1536:2048] = inp["rw_g2"][l][128:160]
    sw[0:16, 2048:2304] = inp["gla_gk_w2"][l]
    sw[32, 2048:2304] = inp["gla_gk_b"][l]
    muv = np.ascontiguousarray(np.broadcast_to(mu[1024:1536][None, :], (128, 512))).astype(np.float32)
    return wA, wB, pv, sw, muv


NA_PER_LAYER = 22 + 21 + 6 + 18 + 4 + 22 + 4 + 1


class Prog:
    def __init__(self, T, layers, debug=(), stages=4, mixsel="rgd"):
        self.stages = stages
        self.mixsel = mixsel
        self.T = T
        self.NT = T // TT
        self.layers = list(layers)
        self.debug = set(debug)
        nc = self.nc = bass.Bass("TRN2", target_bir_lowering=False)
        self.xT = nc.dram_tensor("xT", [D, T], F32, kind="ExternalInput").ap()
        self.pT = nc.dram_tensor("pT", [2, 256, T], F32, kind="ExternalInput").ap()
        self.wA = nc.dram_tensor("wA", [2, NA_PER_LAYER, 128, 2048], F32, kind="ExternalInput").ap()
        self.wB = nc.dram_tensor("wB", [2, 16, 128, 2816], F32, kind="ExternalInput").ap()
        self.pvd = nc.dram_tensor("pv", [2, 128, NPV], F32, kind="ExternalInput").ap()
        self.swd = nc.dram_tensor("sw", [2, 128, 2304], F32, kind="ExternalInput").ap()
        self.muvd = nc.dram_tensor("muv", [2, 128, 512], F32, kind="ExternalInput").ap()
        self.cfd = nc.dram_tensor("cf", [128, NCI * 128], F32, kind="ExternalInput").ap()
        self.cbd = nc.dram_tensor("cb", [128, 3 * 128], F32, kind="ExternalInput").ap()
        self.outT = nc.dram_tensor("outT", [D, T], F32, kind="ExternalOutput").ap()
        self.dbg_out = {}
        self.plan = []
        self.S = Sched(nc)
        self.build()

    def mm(self, out, lhsT, rhs, start=True, stop=True):
        o, a, b = out.ap, lhsT.ap, rhs.ap
        self.S.op("pe", lambda e: e.matmul(o, lhsT=a, rhs=b, start=start, stop=stop),
                  reads=_bufs(lhsT, rhs), writes=[out.b])

    def tr(self, out, in_, ident):
        o, a, b = out.ap, in_.ap, ident.ap
        self.S.op("pe", lambda e: e.transpose(o, a, b), reads=_bufs(in_, ident), writes=[out.b])

    def act(self, out, in_, func, bias=0.0, scale=1.0):
        o, a, bi, sc = out.ap, in_.ap, _ap(bias), _ap(scale)
        self.S.op("act", lambda e: e.activation(out=o, in_=a, func=func, bias=bi, scale=sc),
                  reads=_bufs(in_, bias, scale), writes=[out.b])

    def tt(self, eng, out, a, b, op):
        o, x, y = out.ap, a.ap, b.ap
        self.S.op(eng, lambda e: e.tensor_tensor(out=o, in0=x, in1=y, op=op), reads=_bufs(a, b), writes=[out.b])

    def ts(self, eng, out, a, s1, s2, op0, op1=None):
        o, x, p1, p2 = out.ap, a.ap, _ap(s1), _ap(s2)
        if op1 is None:
            self.S.op(eng, lambda e: e.tensor_scalar(out=o, in0=x, scalar1=p1, scalar2=None, op0=op0),
                      reads=_bufs(a, s1), writes=[out.b])
        else:
            self.S.op(eng, lambda e: e.tensor_scalar(out=o, in0=x, scalar1=p1, scalar2=p2, op0=op0, op1=op1),
                      reads=_bufs(a, s1, s2), writes=[out.b])

    def stt(self, eng, out, a, s, b, op0, op1):
        o, x, sc, y = out.ap, a.ap, _ap(s), b.ap
        self.S.op(eng, lambda e: e.scalar_tensor_tensor(out=o, in0=x, scalar=sc, in1=y, op0=op0, op1=op1),
                  reads=_bufs(a, s, b), writes=[out.b])

    def cp(self, eng, out, a):
        o, x = out.ap, a.ap
        if eng == "act":
            self.S.op("act", lambda e: e.copy(out=o, in_=x), reads=[a.b], writes=[out.b])
        else:
            self.S.op(eng, lambda e: e.tensor_copy(out=o, in_=x), reads=[a.b], writes=[out.b])

    def memset(self, eng, out, val):
        o = out.ap
        self.S.op(eng, lambda e: e.memset(o, val), writes=[out.b])

    def recip(self, out, a):
        o, x = out.ap, a.ap
        self.S.op("dve", lambda e: e.reciprocal(out=o, in_=x), reads=[a.b], writes=[out.b])

    def ev(self):
        self.S.rr ^= 1
        return "dve" if self.S.rr else "act"

    def copy_any(self, out, a):
        self.cp(self.ev(), out, a)

    def bank(self):
        self._bi = (self._bi + 1) % len(self.MM)
        return self.MM[self._bi]

    def bankx(self):
        import os
        if os.environ.get("NO_BANKX"):
            return self.bank()
        self._bxi = (self._bxi + 1) % len(self.MMX)
        return self.MMX[self._bxi]

    def quarter(self):
        self._qi = (self._qi + 1) % len(self.Q)
        return self.Q[self._qi]

    def tbq(self):
        self._ti = (self._ti + 1) % len(self.TB)
        return self.TB[self._ti]

    def slabA(self, l, key):
        idx = self._ai[l]
        self._ai[l] += 1
        if idx >= len(self.plan):
            self.plan.append(key)
        assert self.plan[idx] == key, (idx, key, self.plan[idx])
        slot = self.ring[self._ri % NSLOT]
        self._ri += 1
        self.S.dma("pool", slot.t[:, 0:2048], self.wA[l, idx], writes=[slot])
        return slot

    def slabB(self, l, idx):
        slot = self.ring[self._ri % NSLOT]
        self._ri += 1
        self.S.dma("pool", slot.t[:, 0:2816], self.wB[l, idx], writes=[slot])
        return slot

    def fm_slab(self, l, g):
        if self._fmg[0] != (l, g):
            self._fmg = ((l, g), self.slabA(l, ("fm", g)))
        return self._fmg[1]

    def fm_chunk(self, l, ch):
        slab = self.fm_slab(l, ch // 2)
        return self.fm_chunk_psum(slab, ch % 2)

    def dump(self, name, view, shape):
        if name not in self.debug:
            return
        key = f"dbg_{name}_{len(self.dbg_out)}"
        d = self.nc.dram_tensor(key, list(shape), F32, kind="ExternalOutput").ap()
        self.dbg_out[key] = name
        if not self._dbg_bufs:
            self._dbg_bufs.append(self.S.sbuf(shape, F32, "dbgt"))
        tmp = self._dbg_bufs[0]
        self.cp("dve", tmp[:], view)
        self.S.dma("sp", d, tmp.t[:], reads=[tmp], sembuf=tmp)

    def build(self):
        S = self.S
        self._bi = self._qi = self._ti = 0
        self._ri = 0
        self._fmg = (None, None)
        self._dbg_bufs = []
        self.MM = [S.psum([128, 512], F32, "mm") for _ in range(2)]
        qbanks = [S.psum([128, 512], F32, "qb") for _ in range(5)]
        self.MMX = [self.MM[0], qbanks[0], self.MM[1], qbanks[1], qbanks[2], qbanks[3]]
        self._bxi = 0
        self.Q = [Buf(qbanks[i % 4].t, qbanks[i % 4].name + f"_{i}", c0=(i // 4) * 128, owner=qbanks[i % 4])
                  for i in range(16)]
        self.QF = [Buf(qbanks[4].t, qbanks[4].name + f"_f{i}", c0=i * 128, owner=qbanks[4]) for i in range(4)]
        tbb = S.psum([128, 1024], BF16, "tbb")
        self.TB = [Buf(tbb.t, tbb.name + f"_{i}", c0=i * 128, owner=tbb) for i in range(4)]
        cf = self.cf = S.sbuf([128, NCI, 128], F32, "cf")
        cb = self.cb = S.sbuf([128, 3, 128], BF16, "cb")
        S.dma("sp", cf.t[:], self.cfd.rearrange("p (n j) -> p n j", n=NCI), writes=[cf])
        S.dma("pool", cb.t[:], self.cbd.rearrange("p (n j) -> p n j", n=3), writes=[cb])
        self.pv = S.sbuf([128, 2, NPV], F32, "pv")
        S.dma("sp", self.pv.t[:], self.pvd.rearrange("l p n -> p l n"), writes=[self.pv])
        self.sw = S.sbuf([128, 2, 2304], BF16, "sw")
        S.dma("pool", self.sw.t[:], self.swd.rearrange("l p n -> p l n"), writes=[self.sw])
        self.epsb = S.sbuf([128, 4], F32, "epsb")
        self.memset("dve", self.epsb[:, 0:1], LN_EPS)
        self.memset("dve", self.epsb[:, 1:2], 1e-6)
        self.memset("dve", self.epsb[:, 2:3], 64e-5)
        self.memset("dve", self.epsb[:, 3:4], 1.0)
        self.nega = S.sbuf([128, 2, 4], F32, "nega")
        for l in range(2):
            self.act(self.nega[:, l, :], self.pv[:, l, PV_ALOG:PV_ALOG + 4], AF.Exp)
            self.ts("dve", self.nega[:, l, :], self.nega[:, l, :], -1.0, None, ALU.mult)
        self.ring = [S.sbuf([128, SLOT], BF16, "ring") for _ in range(NSLOT)]
        self.x = [S.sbuf([128, TT], F32, "x") for _ in range(8)]
        self.pool = [S.sbuf([128, TT], F32, "pool") for _ in range(8)]
        self.z = self.pool
        self.xb = [S.sbuf([128, TT], BF16, "xb") for _ in range(8)]
        self.G = [S.sbuf([128, TT], BF16, "G") for _ in range(22)]
        self.br = [self.G[0:4], self.G[4:8], self.G[8:12]]
        self.merged = self.G[12:20]
        self.dxb = self.G[12:20]
        self.pbt = self.G[20:22]
        self.t32 = [S.sbuf([128, TT], F32, "t32") for _ in range(5)]
        self._t32i = 0
        self.xcar = S.sbuf([128, 2, 8], BF16, "xcar")
        self.memset("dve", self.xcar[:], 0.0)
        self.hcar = S.sbuf([128, 2, 28, 3], F32, "hcar")
        self.memset("dve", self.hcar[:], 0.0)
        self.Srw = S.sbuf([128, 2, 4, 128], F32, "Srw")
        self.Srwb = S.sbuf([128, 2, 4, 128], BF16, "Srwb")
        self.Sgl = S.sbuf([128, 2, 2, 128], F32, "Sgl")
        self.Sglb = S.sbuf([128, 2, 2, 128], BF16, "Sglb")
        self.Sgd = S.sbuf([128, 2, 4, 128], F32, "Sgd")
        self.Sgdb = S.sbuf([128, 2, 4, 128], BF16, "Sgdb")
        for s_ in (self.Srw, self.Srwb, self.Sgl, self.Sglb, self.Sgd, self.Sgdb):
            self.memset("dve", s_[:], 0.0)
        self.alloc_mixer()
        self._ai = {0: 0, 1: 0}
        for ti in range(self.NT):
            t0 = ti * TT
            for m in range(8):
                S.dma("sp", self.x[m].t[:], self.xT[m * 128:(m + 1) * 128, t0:t0 + TT], writes=[self.x[m]])
            for l in self.layers:
                self._ai[l] = 0
                self.layer_tile(l, ti)
                assert self._ai[l] == NA_PER_LAYER or self.mixsel != "rgd", self._ai[l]
            for m in range(8):
                S.dma("sp", self.outT[m * 128:(m + 1) * 128, t0:t0 + TT], self.x[m].t[:], reads=[self.x[m]],
                      sembuf=self.x[m])
        S.final_wait("sp", self.x + self._dbg_bufs)
        S.finish()

    def tmp32(self):
        self._t32i = (self._t32i + 1) % len(self.t32)
        return self.t32[self._t32i]

    def make_xb(self):
        for m in range(8):
            self.copy_any(self.xb[m][:], self.x[m][:])

    def ffn(self, l, f):
        for g in range(11):
            w1 = self.slabA(l, ("w1", f, g))
            w3 = self.slabA(l, ("w3", f, g))
            for cc in range(2):
                c = g * 2 + cc
                p1 = self.bankx()
                p3 = self.bankx()
                for k in range(8):
                    self.mm(p1[:], w1[:, k * 256 + cc * 128: k * 256 + cc * 128 + 128], self.xb[k][:], k == 0, k == 7)
                for k in range(8):
                    self.mm(p3[:], w3[:, k * 256 + cc * 128: k * 256 + cc * 128 + 128], self.xb[k][:], k == 0, k == 7)
                t = self.tmp32()
                self.act(t[:], p1[:], AF.Silu)
                self.stt("dve", self.G[c][:], t[:], 0.5, p3[:], ALU.mult, ALU.mult)
        for m in range(8):
            w2 = self.slabB(l, f * 8 + m)
            p = self.bankx()
            for c in range(22):
                self.mm(p[:], w2[:, c * 128:(c + 1) * 128], self.G[c][:], c == 0, c == 21)
            self.stt("dve", self.z[m][:], self.x[m][:], ALPHA, p[:], ALU.mult, ALU.add)

    def ln(self, l, i):
        pm = self.bank()
        pq = self.bank()
        lnm = self.cf[:, CI_LN, :]
        for m in range(8):
            self.mm(pm[:], lnm, self.z[m][:], m == 0, m == 7)
        for m in range(8):
            t = self.tmp32()
            self.act(t[:], self.z[m][:], AF.Square)
            self.mm(pq[:], lnm, t[:], m == 0, m == 7)
        mean = self.lnmean
        rstd = self.lnrstd
        self.cp("act", mean[:], pm[:])
        self.tt("dve", rstd[:], mean[:], mean[:], ALU.mult)
        self.tt("dve", rstd[:], pq[:], rstd[:], ALU.subtract)
        self.act(rstd[:], rstd[:], AF.Ln, bias=self.epsb[:, 0:1])
        self.act(rstd[:], rstd[:], AF.Exp, scale=-0.5)
        for m in range(8):
            t = self.tmp32()
            self.tt("dve", t[:], self.z[m][:], mean[:], ALU.subtract)
            self.tt("dve", t[:], t[:], rstd[:], ALU.mult)
            self.act(self.x[m][:], t[:], AF.Identity, bias=self.pv[:, l, PV_LNB + i * 8 + m: PV_LNB + i * 8 + m + 1],
                     scale=self.pv[:, l, PV_LNG + i * 8 + m: PV_LNG + i * 8 + m + 1])

    def layer_tile(self, l, ti):
        st = self.stages
        self.make_xb()
        self.ffn(l, 0)
        self.ln(l, 0)
        if st < 2:
            self._ai[l] = NA_PER_LAYER
            return
        self.make_xb()
        self.mixers(l, ti)
        self.ln(l, 1)
        if st < 3:
            self._ai[l] = NA_PER_LAYER
            return
        self.make_xb()
        self.ffn(l, 1)
        self.ln(l, 2)
        if st < 4:
            self._ai[l] = NA_PER_LAYER
            return
        self.make_xb()
        self.ple(l, ti)
        self.ln(l, 3)

    def ple(self, l, ti):
        t0 = ti * TT
        for k in range(2):
            pin = self.tmp32()
            self.S.dma("sp", pin.t[:], self.pT[l, k * 128:(k + 1) * 128, t0:t0 + TT], writes=[pin])
            self.copy_any(self.pbt[k][:], pin[:])
        slabs = [self.slabA(l, ("pg", i)) for i in range(4)]
        wp = self.slabA(l, ("pp", 0))
        for m in range(8):
            pg = self.bankx()
            pp = self.bankx()
            sl = slabs[m // 2]
            for k in range(8):
                self.mm(pg[:], sl[:, k * 256 + (m % 2) * 128: k * 256 + (m % 2) * 128 + 128], self.xb[k][:], k == 0, k == 7)
            for k in range(2):
                self.mm(pp[:], wp[:, k * 1024 + m * 128: k * 1024 + m * 128 + 128], self.pbt[k][:], k == 0, k == 1)
            t = self.tmp32()
            self.act(t[:], pg[:], AF.Sigmoid)
            self.tt("dve", t[:], t[:], pp[:], ALU.mult)
            self.stt("dve", self.z[m][:], self.x[m][:], ALPHA, t[:], ALU.mult, ALU.add)

    def alloc_mixer(self):
        S = self.S
        self.lnmean = S.sbuf([128, TT], F32, "lnmean")
        self.lnrstd = S.sbuf([128, TT], F32, "lnrstd")
        self.raw = [S.sbuf([128, 3 + TT], F32, "raw") for _ in range(2)]
        self._rawi = 0
        self.rw_tw = S.sbuf([128, TT], BF16, "rw_tw")
        self.rw_al = S.sbuf([128, TT], BF16, "rw_al")
        self.rw_gs = S.sbuf([128, 2, TT], BF16, "rw_gs")
        self.rw_kk = S.sbuf([128, TT], BF16, "rw_kk")
        self.rw_b = S.sbuf([128, TT], BF16, "rw_b")
        self.vt16 = S.sbuf([128, 4, 256], BF16, "vt16")
        self.kt16 = S.sbuf([128, 4, 128], BF16, "kt16")
        self.muvj = S.sbuf([128, 128], F32, "muvj")
        self.vpad = S.sbuf([128, 4, 2, 128], BF16, "vpad")
        self.memset("dve", self.vpad[:], 0.0)
        self.kpad = [S.sbuf([128, 128], BF16, "kpad") for _ in range(2)]
        for u_ in self.kpad:
            self.memset("dve", u_[:], 0.0)
        self.upad = [S.sbuf([128, 128], BF16, "upad") for _ in range(2)]
        for u_ in self.upad:
            self.memset("dve", u_[:], 0.0)
        self.gl_lo = S.sbuf([128, TT], BF16, "gl_lo")
        self.gl_l = S.sbuf([128, 4, 256], F32, "gl_l")
        self.gl_kt = S.sbuf([128, 4, 256], F32, "gl_kt")
        self.gd_ab = S.sbuf([128, 4, 8], F32, "gd_ab")
        self.gd_sc = S.sbuf([128, 4, 24], F32, "gd_sc")
        self.s16 = [S.sbuf([128, 128], BF16, "s16") for _ in range(26)]
        self.l16 = [S.sbuf([128, 128], BF16, "l16") for _ in range(26)]
        self.s32 = [S.sbuf([128, 128], F32, "s32") for _ in range(12)]
        self._s16i = self._l16i = self._s32i = 0
        self.s16w = [S.sbuf([128, 256], BF16, "s16w") for _ in range(4)]
        self._s16wi = 0

    def b16(self):
        self._s16i = (self._s16i + 1) % len(self.s16)
        return self.s16[self._s16i]

    def L16(self):
        self._l16i = (self._l16i + 1) % len(self.l16)
        return self.l16[self._l16i]

    def b32(self):
        self._s32i = (self._s32i + 1) % len(self.s32)
        return self.s32[self._s32i]

    def b16w(self):
        self._s16wi = (self._s16wi + 1) % len(self.s16w)
        return self.s16w[self._s16wi]

    def fm_chunk_psum(self, slab, cc, wide=False):
        p = self.bankx() if wide else self.bank()
        for k in range(8):
            self.mm(p[:], slab[:, k * 256 + cc * 128: k * 256 + cc * 128 + 128], self.xb[k][:], k == 0, k == 7)
        return p

    def raw_tile(self, l, ci, p):
        self._rawi = (self._rawi + 1) % len(self.raw)
        r = self.raw[self._rawi]
        self.cp("dve", r[:, 0:3], self.hcar[:, l, ci, :])
        self.cp("act", r[:, 3:3 + TT], p[:])
        self.cp("dve", self.hcar[:, l, ci, :], r[:, TT:TT + 3])
        return r

    def shift_chunk(self, l, ci, p, out):
        r = self.raw_tile(l, ci, p)
        t = self.tmp32()
        self.tt("dve", t[:], r[:, 2:2 + TT], r[:, 3:3 + TT], ALU.subtract)
        mu = self.pv[:, l, PV_MU + ci: PV_MU + ci + 1]
        self.stt("dve", out, t[:], mu, r[:, 3:3 + TT], ALU.mult, ALU.add)

    def mixers(self, l, ti):
        for k in range(8):
            self.tt("dve", self.dxb[k][:, 1:TT], self.xb[k][:, 0:TT - 1], self.xb[k][:, 1:TT], ALU.subtract)
            self.tt("dve", self.dxb[k][:, 0:1], self.xcar[:, l, k:k + 1], self.xb[k][:, 0:1], ALU.subtract)
            self.cp("dve", self.xcar[:, l, k:k + 1], self.xb[k][:, TT - 1:TT])
        for name, fn, b in (("r", self.rwkv, 0), ("g", self.gla, 1), ("d", self.gdn, 2)):
            if name in self.mixsel:
                fn(l)
            else:
                for j in range(4):
                    self.memset("dve", self.br[b][j][:], 0.0)
        if self.mixsel != "rgd":
            pass
        self.merge(l)

    def tri_inverse_multi(self, XPs):
        ident_f = self.cf[:, CI_ID, :]
        st = []
        for X, P in XPs:
            Z = self.b16()
            self.tt("dve", Z[:], X, ident_f, ALU.add)
            st.append([X, P, Z])
        for lvl in range(7):
            need_p = lvl <= 5
            need_x = lvl <= 4
            need_z = lvl >= 1
            pqs, xqs, zqs = [], [], []
            for c in st:
                if need_z:
                    zq = self.quarter()
                    self.mm(zq[:], c[1], c[2][:])
                    zqs.append(zq)
                if need_p:
                    pq = self.quarter()
                    self.mm(pq[:], c[0], c[1])
                    pqs.append(pq)
                if need_x:
                    xq = self.quarter()
                    self.mm(xq[:], c[1], c[0])
                    xqs.append(xq)
            for ci, c in enumerate(st):
                if need_z:
                    Zn = self.b16()
                    self.tt("dve", Zn[:], zqs[ci][:], c[2][:], ALU.add)
                    c[2] = Zn
                if need_p:
                    Pn = self.b16()
                    self.cp("act", Pn[:], pqs[ci][:])
                if need_x:
                    Xn = self.b16()
                    self.cp("act" if need_z else "dve", Xn[:], xqs[ci][:])
                    c[0] = Xn[:]
                if need_p:
                    c[1] = Pn[:]
        return [c[2] for c in st]

    def tri_inverse(self, X, P):
        return self.tri_inverse_multi([(X, P)])[0]

    def rwkv(self, l):
        cf, cb, sw = self.cf, self.cb, self.sw
        pvl = lambda c0, j: self.pv[:, l, c0 + j: c0 + j + 1]
        blk = cf[:, CI_BLK, :]
        idb = cb[:, CB_ID, :]
        KD = -float(np.exp(-0.5))
        t = self.tmp32()
        self.shift_chunk(l, 0, self.fm_chunk(l, 0), t[:])
        self.act(self.rw_tw[0:64, :], t[0:64, :], AF.Tanh)
        self.memset("dve", self.rw_tw[64:65, :], 1.0)
        self.cp("dve", self.rw_al[:], t[:])
        for jj in range(2):
            t = self.tmp32()
            self.shift_chunk(l, 1 + jj, self.fm_chunk(l, 1 + jj), t[:])
            self.act(self.rw_gs[:, jj, :], t[:], AF.Sigmoid)
        r, k, v, y = self.pool[0], self.pool[1], self.pool[2], self.pool[3]
        tmv = None
        for j in range(4):
            self.shift_chunk(l, 4 + 3 * j, self.fm_chunk(l, 4 + 3 * j), r[:])
            self.shift_chunk(l, 5 + 3 * j, self.fm_chunk(l, 5 + 3 * j), k[:])
            self.shift_chunk(l, 6 + 3 * j, self.fm_chunk(l, 6 + 3 * j), v[:])
            if j % 2 == 0:
                tmv = self.slabA(l, ("tm", j // 2))
            self.S.dma("sp", self.muvj.t[:], self.muvd[l, :, j * 128:(j + 1) * 128], writes=[self.muvj])
            for s in range(4):
                ts_ = slice(s * C, (s + 1) * C)
                q1 = self.quarter()
                q2 = self.quarter()
                c0 = (j % 2) * 128
                for kk_ in range(8):
                    self.mm(q1[:], self.xb[kk_][:, ts_], tmv[:, kk_ * 256 + c0: kk_ * 256 + c0 + 128], kk_ == 0, kk_ == 7)
                for kk_ in range(8):
                    self.mm(q2[:], self.dxb[kk_][:, ts_], tmv[:, kk_ * 256 + c0: kk_ * 256 + c0 + 128], kk_ == 0, kk_ == 7)
                tq_ = self.b32()
                self.tt("dve", tq_[:], q2[:], self.muvj[:], ALU.mult)
                self.tt("dve", self.vpad[:, s, 0, 0:64], tq_[:, 0:64], q1[:, 0:64], ALU.add)
                self.tt("dve", self.vpad[:, s, 1, 64:128], tq_[:, 64:128], q1[:, 64:128], ALU.add)
            p = self.bank()
            self.mm(p[:], sw[:, l, 512 + j * 128: 512 + (j + 1) * 128], self.rw_al[:])
            a = self.tmp32()
            self.act(a[:], p[:], AF.Sigmoid, bias=pvl(PV_A0, j))
            t1 = self.tmp32()
            self.ts("dve", t1[:], k[:], pvl(PV_KK, j), None, ALU.mult)
            sq = self.tmp32()
            self.act(sq[:], t1[:], AF.Square)
            p = self.bank()
            self.mm(p[:], blk, sq[:])
            rn = sq
            self.act(rn[:], p[:], AF.Ln, bias=self.epsb[:, 1:2])
            self.act(rn[:], rn[:], AF.Exp, scale=-0.5)
            self.tt("dve", self.rw_kk[:], t1[:], rn[:], ALU.mult)
            self.tt("dve", self.rw_b[:], self.rw_kk[:], a[:], ALU.mult)
            self.ts("dve", a[:], a[:], -1.0, pvl(PV_KA, j), ALU.add, ALU.mult)
            self.stt("dve", k[:], a[:], 1.0, k[:], ALU.add, ALU.mult)
            t3 = self.tmp32()
            self.stt("dve", t3[:], r[:], pvl(PV_RK, j), k[:], ALU.mult, ALU.mult)
            p = self.bank()
            self.mm(p[:], blk, t3[:])
            self.tt("dve", v[:], p[:], v[:], ALU.mult)
            for s in range(4):
                ts_ = slice(s * C, (s + 1) * C)
                qz = self.quarter()
                self.mm(qz[:], self.rw_tw[0:65, ts_], sw[0:65, l, j * 128:(j + 1) * 128])
                sg = self.b32()
                self.act(sg[:], qz[:], AF.Sigmoid)
                qc = self.quarter()
                self.mm(qc[:], sg[:], cf[:, CI_TLE, :])
                qp = self.quarter()
                self.mm(qp[:], sg[:], cf[:, CI_TLT, :])
                E1 = self.b32()
                self.act(E1[:], qc[:], AF.Exp, scale=KD)
                Ei = self.b32()
                self.act(Ei[:], qc[:], AF.Exp, scale=-KD)
                E0 = self.b32()
                self.act(E0[:], qp[:], AF.Exp, scale=KD)
                ARt = self.b16w()
                self.stt("dve", ARt[:, 0:128], self.rw_kk[:, ts_], -1.0, E0[:], ALU.mult, ALU.mult)
                self.tt("dve", ARt[:, 128:256], r[:, ts_], E1[:], ALU.mult)
                Bt = self.L16()
                self.tt("dve", Bt[:], self.rw_b[:, ts_], Ei[:], ALU.mult)
                Kt = self.L16()
                self.tt("dve", Kt[:], k[:, ts_], Ei[:], ALU.mult)
                tq = self.tbq()
                self.tr(tq[:], Bt[:], idb)
                Btm = self.L16()
                self.copy_any(Btm[:], tq[:])
                tq = self.tbq()
                self.tr(tq[:], Kt[:], idb)
                Ktm = self.L16()
                self.copy_any(Ktm[:], tq[:])
                hm = (cf[:, CI_BLK, 0:1], cf[:, CI_BLK, 64:65])
                Sblk = self.Srwb[:, l, j, :]
                qy = self.QF[0]
                self.mm(qy[:], Sblk, ARt[:, 128:256], True, False)
                pre = []
                for hh in range(2):
                    Bth = self.L16()
                    self.ts("dve", Bth[:], Bt[:], hm[hh], None, ALU.mult)
                    Kth = self.L16()
                    self.ts("dve", Kth[:], Kt[:], hm[hh], None, ALU.mult)
                    Ath = self.L16()
                    self.ts("dve", Ath[:], ARt[:, 0:128], hm[hh], None, ALU.mult)
                    pb_ = self.bank()
                    self.mm(pb_[:, 0:256], Bth[:], ARt[:])
                    self.mm(pb_[:, 256:512], Kth[:], ARt[:])
                    qP = self.quarter()
                    self.mm(qP[:], Ath[:], Bt[:])
                    X = self.L16()
                    self.tt("dve", X[:], pb_[:, 0:128], cf[:, CI_TLT, :], ALU.mult)
                    NBT = self.L16()
                    self.tt("dve", NBT[:], pb_[:, 128:256], cf[:, CI_TLE, :], ALU.mult)
                    MKT = self.L16()
                    self.tt("dve", MKT[:], pb_[:, 256:384], cf[:, CI_TLT, :], ALU.mult)
                    NKT = self.L16()
                    self.tt("dve", NKT[:], pb_[:, 384:512], cf[:, CI_TLE, :], ALU.mult)
                    P = self.L16()
                    self.tt("dve", P[:], qP[:], cf[:, CI_TGT, :], ALU.mult)
                    pre.append((X, P, NBT, MKT, NKT))
                Zs = self.tri_inverse_multi([(pr[0][:], pr[1][:]) for pr in pre])
                for hh in range(2):
                    hs = slice(hh * 64, hh * 64 + 64)
                    X, P, NBT, MKT, NKT = pre[hh]
                    Z = Zs[hh]
                    V_ = self.vpad[:, s, hh, hs]
                    qr = self.quarter()
                    self.mm(qr[:, 0:64], ARt[:, 0:128], self.Srwb[:, l, j, hs], True, False)
                    self.mm(qr[:, 0:64], MKT[:], V_, False, True)
                    RHS = self.L16()
                    self.copy_any(RHS[:, 0:64], qr[:, 0:64])
                    qu = self.quarter()
                    self.mm(qu[:, 0:64], Z[:], RHS[:, 0:64])
                    self.copy_any(self.upad[hh][:, hs], qu[:, 0:64])
                    self.mm(qy[:], self.upad[hh][:], NBT[:], False, False)
                    self.mm(qy[:], self.vpad[:, s, hh, :], NKT[:], False, hh == 1)
                self.copy_any(y[:, ts_], qy[:])
                qs = self.QF[1]
                self.mm(qs[:], Btm[:], self.upad[0][:], True, False)
                self.mm(qs[:], Btm[:], self.upad[1][:], False, False)
                self.mm(qs[:], Ktm[:], self.vpad[:, s, 0, :], False, False)
                self.mm(qs[:], Ktm[:], self.vpad[:, s, 1, :], False, True)
                tmpS = self.b32()
                self.tt("dve", tmpS[:], qs[:], cf[:, CI_BLK, :], ALU.mult)
                Sf = self.Srw[:, l, j, :]
                self.tt("dve", Sf, tmpS[:], Sf, ALU.add)
                self.ts("dve", Sf, Sf, E1[:, 127:128], None, ALU.mult)
                self.cp("act", self.Srwb[:, l, j, :], Sf)
            p = self.bank()
            self.mm(p[:], blk, y[:])
            d = self.tmp32()
            self.stt("dve", d[:], p[:], -1.0 / 64, y[:], ALU.mult, ALU.add)
            sq = self.tmp32()
            self.act(sq[:], d[:], AF.Square)
            p2 = self.bank()
            self.mm(p2[:], blk, sq[:])
            rs = sq
            self.act(rs[:], p2[:], AF.Ln, bias=self.epsb[:, 2:3], scale=1.0 / 64)
            self.act(rs[:], rs[:], AF.Exp, scale=-0.5)
            self.tt("dve", d[:], d[:], rs[:], ALU.mult)
            self.act(d[:], d[:], AF.Identity, bias=pvl(PV_GNB, j), scale=pvl(PV_GNG, j))
            self.tt("dve", d[:], d[:], v[:], ALU.add)
            pg_ = self.bank()
            self.mm(pg_[:], sw[:, l, 1024 + j * 128: 1024 + (j + 1) * 128], self.rw_gs[:, 0, :], True, False)
            self.mm(pg_[:], sw[0:32, l, 1536 + j * 128: 1536 + (j + 1) * 128], self.rw_gs[0:32, 1, :], False, True)
            self.tt("dve", self.br[0][j][:], d[:], pg_[:], ALU.mult)
            self.dump("o_rw", d[:], [128, TT])

    def gla(self, l):
        cf, cb, sw = self.cf, self.cb, self.sw
        p = self.fm_chunk(l, 16)
        self.cp("act", self.gl_lo[0:32, :], p[0:32, :])
        self.memset("dve", self.gl_lo[32:33, :], 1.0)
        if self.cut(0, 1):
            return
        tmk = self.slabA(l, ("tm", 2))
        for s in range(4):
            ts_ = slice(s * C, (s + 1) * C)
            p = self.bank()
            self.mm(p[:, 0:256], self.gl_lo[0:33, ts_], sw[0:33, l, 2048:2304])
            e = self.tmp32()
            self.act(e[:, 0:256], p[:, 0:256], AF.Exp, scale=-1.0)
            self.act(self.gl_l[:, s, :], e[:, 0:256], AF.Ln, bias=self.epsb[:, 3:4])
            pk = self.bank()
            for k in range(8):
                self.mm(pk[:, 0:256], self.xb[k][:, ts_], tmk[:, k * 256:(k + 1) * 256], k == 0, k == 7)
            self.copy_any(self.gl_kt[:, s, :], pk[:, 0:256])
        if self.cut(1, 1):
            return
        q, k_, g0, g1, o0, o1 = self.pool[0:6]
        for j in range(2):
            self.copy_any(q[:], self.fm_chunk(l, 18 + 4 * j)[:])
            self.copy_any(k_[:], self.fm_chunk(l, 19 + 4 * j)[:])
            self.act(g0[:], self.fm_chunk(l, 20 + 4 * j)[:], AF.Silu)
            self.act(g1[:], self.fm_chunk(l, 21 + 4 * j)[:], AF.Silu)
            tmv = self.slabA(l, ("tm", 3 + j))
            for s in range(4):
                ts_ = slice(s * C, (s + 1) * C)
                pv_ = self.bank()
                for k in range(8):
                    self.mm(pv_[:, 0:256], self.xb[k][:, ts_], tmv[:, k * 256:(k + 1) * 256], k == 0, k == 7)
                self.copy_any(self.vt16[:, s, :], pv_[:, 0:256])
            if self.cut(2, 1):
                return
            for s in range(4):
                ts_ = slice(s * C, (s + 1) * C)
                lj = self.gl_l[:, s, j * 128:(j + 1) * 128]
                qk = self.quarter()
                self.mm(qk[:], cf[:, CI_TGT, :], lj)
                ek = self.b32()
                self.act(ek[:], qk[:], AF.Exp, scale=-1.0 / 16)
                for hh in range(2):
                    c0_, c1_ = hh * 64, hh * 64 + 64
                    self.tt("dve", self.kpad[hh][:, c0_:c1_], self.gl_kt[:, s, j * 128 + c0_: j * 128 + c1_],
                            ek[:, c0_:c1_], ALU.mult)
                qsf = self.QF[2]
                qb = self.quarter()
                self.mm(qb[:], lj, cf[:, CI_TLE, :])
                Eb = self.b32()
                self.act(Eb[:], qb[:], AF.Exp, scale=-1.0 / 16)
                Ein = self.b32()
                self.act(Ein[:], qb[:], AF.Exp, scale=1.0 / 16)
                qt = self.L16()
                self.stt("dve", qt[:], q[:, ts_], 0.125, Eb[:], ALU.mult, ALU.mult)
                hm = (cf[:, CI_BLK, 0:1], cf[:, CI_BLK, 64:65])
                for hh in range(2):
                    hs = slice(hh * 64, hh * 64 + 64)
                    if hh == 1 and self.cut(6, 1):
                        return
                    kth = self.L16()
                    self.stt("dve", kth[:], k_[:, ts_], hm[hh], Ein[:], ALU.mult, ALU.mult)
                    qth = self.L16()
                    self.ts("dve", qth[:], qt[:], hm[hh], None, ALU.mult)
                    if hh == 1 and self.cut(7, 1):
                        return
                    qa = self.quarter()
                    self.mm(qa[:], kth[:], qt[:])
                    attT = self.b16()
                    self.tt("dve", attT[:], qa[:], cf[:, CI_TLE, :], ALU.mult)
                    if hh == 1 and self.cut(8, 1):
                        return
                    V_ = self.vt16[:, s, hh * 128:(hh + 1) * 128]
                    qo = self.quarter()
                    self.mm(qo[:], V_, attT[:], True, False)
                    self.mm(qo[:], self.Sglb[:, l, j, :], qth[:], False, True)
                    self.copy_any((o0, o1)[hh][:, ts_], qo[:])
                    if hh == 1 and self.cut(9, 1):
                        return
                    self.mm(qsf[:], self.kpad[hh][:], V_, hh == 0, hh == 1)
                Sf = self.Sgl[:, l, j, :]
                self.stt("dve", Sf, Sf, Eb[:, 127:128], qsf[:], ALU.mult, ALU.add)
                self.cp("act", self.Sglb[:, l, j, :], Sf)
            self.head_rms(l, o0, g0, PV_GLAN, self.br[1][2 * j])
            self.head_rms(l, o1, g1, PV_GLAN, self.br[1][2 * j + 1])

    def cut(self, n, b):
        import os
        c = int(os.environ.get("GCUT", "99"))
        if n >= c:
            for j in range(4):
                self.memset("dve", self.br[b][j][:], 0.0)
            return True
        return False

    def head_rms(self, l, o, gate, pvcol, dst):
        sq = self.tmp32()
        self.act(sq[:], o[:], AF.Square)
        p = self.bank()
        self.mm(p[:], self.cf[:, CI_V128, :], sq[:])
        rs = sq
        self.act(rs[:], p[:], AF.Ln, bias=self.epsb[:, 1:2])
        self.act(rs[:], rs[:], AF.Exp, scale=-0.5)
        t = self.tmp32()
        self.stt("dve", t[:], o[:], self.pv[:, l, pvcol:pvcol + 1], rs[:], ALU.mult, ALU.mult)
        self.tt("dve", dst[:], t[:], gate[:], ALU.mult)

    def gdn(self, l):
        cf, cb = self.cf, self.cb
        ones_f = cf[:, CI_ONE, :]
        idb = cb[:, CB_ID, :]
        tmab = self.slabA(l, ("tm", 5))
        sc = self.gd_sc
        for s in range(4):
            ts_ = slice(s * C, (s + 1) * C)
            pa = self.quarter()
            for k in range(8):
                self.mm(pa[:, 0:8], self.xb[k][:, ts_], tmab[:, k * 256:k * 256 + 8], k == 0, k == 7)
            self.cp("dve", self.gd_ab[:, s, :], pa[:, 0:8])
            self.tt("dve", sc[:, s, 0:4], self.gd_ab[:, s, 0:4], self.pv[:, l, PV_DTB:PV_DTB + 4], ALU.add)
            self.act(sc[:, s, 0:4], sc[:, s, 0:4], AF.Exp)
            self.act(sc[:, s, 0:4], sc[:, s, 0:4], AF.Ln, bias=self.epsb[:, 3:4])
            self.tt("dve", sc[:, s, 0:4], sc[:, s, 0:4], self.nega[:, l, :], ALU.mult)
            self.act(sc[:, s, 4:8], self.gd_ab[:, s, 4:8], AF.Sigmoid)
            qg = self.quarter()
            self.mm(qg[:, 0:4], cf[:, CI_TLE, :], sc[:, s, 0:4])
            self.mm(qg[:, 4:8], cf[:, CI_TGT, :], sc[:, s, 0:4])
            self.act(sc[:, s, 8:16], qg[:, 0:8], AF.Exp)
            self.tt("dve", sc[:, s, 16:20], sc[:, s, 4:8], sc[:, s, 8:12], ALU.mult)
            self.ts("dve", sc[:, s, 20:24], sc[:, s, 4:8], -1.0, None, ALU.mult)
        q, k_, v, gate, o = self.pool[0:5]
        for h in range(4):
            for which, dst in ((0, q), (1, k_), (2, v)):
                cidx = which * 4 + h
                r = self.raw_tile(l, 16 + cidx, self.fm_chunk(l, 26 + 4 * h + which))
                t = self.tmp32()
                cw = lambda tap: self.pv[:, l, PV_CONV + tap * 12 + cidx: PV_CONV + tap * 12 + cidx + 1]
                self.ts("dve", t[:], r[:, 0:TT], cw(0), None, ALU.mult)
                for tap in (1, 2, 3):
                    self.stt("dve", t[:], r[:, tap:tap + TT], cw(tap), t[:], ALU.mult, ALU.add)
                self.act(dst[:], t[:], AF.Silu)
            self.act(gate[:], self.fm_chunk(l, 29 + 4 * h)[:], AF.Silu)
            for src, scale in ((q, 128.0 ** -0.5), (k_, 1.0)):
                sq = self.tmp32()
                self.act(sq[:], src[:], AF.Square)
                p = self.bank()
                self.mm(p[:], ones_f, sq[:])
                rs = sq
                self.act(rs[:], p[:], AF.Ln, bias=self.epsb[:, 1:2])
                self.act(rs[:], rs[:], AF.Exp, scale=-0.5)
                self.stt("dve", src[:], src[:], scale, rs[:], ALU.mult, ALU.mult)
            for src, dstt in ((k_, self.kt16), (v, self.vt16)):
                p = self.bank()
                for s in range(4):
                    self.tr(p[:, s * 128:(s + 1) * 128], src[:, s * C:(s + 1) * C], cf[:, CI_ID, :])
                for s in range(4):
                    self.copy_any(dstt[:, s, 0:128], p[:, s * 128:(s + 1) * 128])
            for s0 in (0, 2):
                cx = []
                for s in (s0, s0 + 1):
                    ts_ = slice(s * C, (s + 1) * C)
                    c = dict(s=s, ts=ts_)
                    c["knT"] = self.L16()
                    self.copy_any(c["knT"][:], k_[:, ts_])
                    c["GT"] = self.b32()
                    self.ts("dve", c["GT"][:], cf[:, CI_TLE, :], sc[:, s, h:h + 1], None, ALU.mult)
                    c["GB"] = self.b32()
                    self.ts("dve", c["GB"][:], cf[:, CI_ONE, :], sc[:, s, h:h + 1], None, ALU.mult)
                    cx.append(c)
                for c in cx:
                    GT, GB, knT = c["GT"], c["GB"], c["knT"]
                    qd = self.quarter()
                    self.mm(qd[:], GT[:], cf[:, CI_ONE, :], True, False)
                    self.mm(qd[:], GB[:], cf[:, CI_NTLE, :], False, False)
                    self.mm(qd[:], idb, cb[:, CB_MBSL, :], False, True)
                    qe = self.quarter()
                    self.mm(qe[:], GB[:], cf[:, CI_TLE, :], True, False)
                    self.mm(qe[:], GT[:], cf[:, CI_NEGONE, :], False, False)
                    self.mm(qe[:], idb, cb[:, CB_MBIU, :], False, True)
                    qbc = self.quarter()
                    self.mm(qbc[:], GB[:], cf[:, CI_TLE, :])
                    qG = self.quarter()
                    self.mm(qG[:], knT[:], knT[:])
                    c.update(qd=qd, qe=qe, qbc=qbc, qG=qG)
                for c in cx:
                    s = c["s"]
                    c["Dsl"] = self.b32()
                    self.act(c["Dsl"][:], c["qd"][:], AF.Exp)
                    c["Diu"] = self.b32()
                    self.act(c["Diu"][:], c["qe"][:], AF.Exp)
                    c["bcE"] = self.b32()
                    self.act(c["bcE"][:], c["qbc"][:], AF.Exp)
                    c["P"] = self.L16()
                    self.stt("dve", c["P"][:], c["qG"][:], sc[:, s, 20 + h:21 + h], c["Dsl"][:], ALU.mult, ALU.mult)
                for c in cx:
                    tq = self.tbq()
                    self.tr(tq[:], c["P"][:], idb)
                    c["tq"] = tq
                for c in cx:
                    c["X"] = self.L16()
                    self.copy_any(c["X"][:], c["tq"][:])
                Zs = self.tri_inverse_multi([(c["X"][:], c["P"][:]) for c in cx])
                for ci, c in enumerate(cx):
                    s, ts_ = c["s"], c["ts"]
                    c["Z"] = Zs[ci]
                    kn_tm = self.kt16[:, s, :]
                    v_tm = self.vt16[:, s, 0:128]
                    c["qnT"] = self.L16()
                    self.copy_any(c["qnT"][:], q[:, ts_])
                    c["qgT"] = self.L16()
                    self.tt("dve", c["qgT"][:], q[:, ts_], c["bcE"][:], ALU.mult)
                    c["kbg"] = self.L16()
                    self.ts("dve", c["kbg"][:], kn_tm, sc[:, s, 16 + h:17 + h], None, ALU.mult)
                    c["kdec"] = self.L16()
                    self.ts("dve", c["kdec"][:], kn_tm, sc[:, s, 12 + h:13 + h], None, ALU.mult)
                    c["vb"] = self.L16()
                    self.ts("dve", c["vb"][:], v_tm, sc[:, s, 4 + h:5 + h], None, ALU.mult)
                for c in cx:
                    qa = self.quarter()
                    self.mm(qa[:], c["knT"][:], c["qnT"][:])
                    qw = self.quarter()
                    self.mm(qw[:], c["kbg"][:], c["Z"][:])
                    c.update(qa=qa, qw=qw)
                for c in cx:
                    c["attT"] = self.L16()
                    self.tt("dve", c["attT"][:], c["qa"][:], c["Diu"][:], ALU.mult)
                    c["nwT"] = self.L16()
                    self.act(c["nwT"][:], c["qw"][:], AF.Copy, scale=-1.0)
                for c in cx:
                    ts_ = c["ts"]
                    Sb = self.Sgdb[:, l, h, :]
                    qv = self.quarter()
                    self.mm(qv[:], c["Z"][:], c["vb"][:], True, False)
                    self.mm(qv[:], c["nwT"][:], Sb, False, True)
                    vnew = self.L16()
                    self.copy_any(vnew[:], qv[:])
                    qo = self.quarter()
                    self.mm(qo[:], Sb, c["qgT"][:], True, False)
                    self.mm(qo[:], vnew[:], c["attT"][:], False, True)
                    self.copy_any(o[:, ts_], qo[:])
                    qs = self.quarter()
                    self.mm(qs[:], c["kdec"][:], vnew[:])
                    Sf = self.Sgd[:, l, h, :]
                    self.stt("dve", Sf, Sf, c["bcE"][:, 127:128], qs[:], ALU.mult, ALU.add)
                    self.cp("act", self.Sgdb[:, l, h, :], Sf)
            self.head_rms(l, o, gate, PV_GDNN, self.br[2][h])

    def merge(self, l):
        macc = self.z
        for b in range(3):
            for j in range(4):
                self.dump("br", self.br[b][j][:], [128, TT])
        for b in range(3):
            for half in range(2):
                wb = self.slabA(l, ("br", b, half))
                for mm_ in range(2):
                    wg = self.slabA(l, ("gate", b * 4 + half * 2 + mm_))
                    for cc in range(2):
                        m = half * 4 + mm_ * 2 + cc
                        pg = self.fm_chunk_psum(wg, cc, wide=True)
                        pp = self.bankx()
                        for k in range(4):
                            c0 = k * 512 + (mm_ * 2 + cc) * 128
                            self.mm(pp[:], wb[:, c0:c0 + 128], self.br[b][k][:], k == 0, k == 3)
                        t = self.tmp32()
                        self.act(t[:], pg[:], AF.Sigmoid)
                        if b == 0:
                            self.tt("dve", macc[m][:], t[:], pp[:], ALU.mult)
                        else:
                            self.tt("dve", t[:], t[:], pp[:], ALU.mult)
                            if b == 1:
                                self.tt("dve", macc[m][:], macc[m][:], t[:], ALU.add)
                            else:
                                self.tt("dve", self.merged[m][:], macc[m][:], t[:], ALU.add)
        wos = [self.slabA(l, ("wo", i)) for i in range(4)]
        for m in range(8):
            p = self.bankx()
            sl = wos[m // 2]
            for k in range(8):
                self.mm(p[:], sl[:, k * 256 + (m % 2) * 128: k * 256 + (m % 2) * 128 + 128], self.merged[k][:], k == 0, k == 7)
            self.stt("dve", self.z[m][:], self.x[m][:], ALPHA, p[:], ALU.mult, ALU.add)


_CACHE = {}


def prep_weights(inputs, plan):
    inp = {k: np.asarray(v, np.float32) for k, v in inputs.items() if k not in ("x", "p")}
    per = [prep_layer(inp, l, plan) for l in range(2)]
    cf, cb = build_consts()
    return dict(wA=np.ascontiguousarray(np.stack([p[0] for p in per], 0)),
                wB=np.ascontiguousarray(np.stack([p[1] for p in per], 0)),
                pv=np.ascontiguousarray(np.stack([p[2] for p in per], 0)),
                sw=np.ascontiguousarray(np.stack([p[3] for p in per], 0)),
                muv=np.ascontiguousarray(np.stack([p[4] for p in per], 0)),
                cf=cf, cb=cb)


def run(inputs, T, layers=(0, 1), n_cores=8, debug=(), stages=4, mixsel="rgd"):
    x = np.asarray(inputs["x"], np.float32)
    p = np.asarray(inputs["p"], np.float32)
    B = x.shape[0]
    key = (T, tuple(layers), tuple(debug), stages, mixsel)
    if key not in _CACHE:
        _CACHE[key] = Prog(T, list(layers), debug, stages, mixsel)
    prog = _CACHE[key]
    w = prep_weights(inputs, prog.plan)
    in_maps = []
    for c in range(n_cores):
        b = c % B
        m = dict(w)
        m["xT"] = np.ascontiguousarray(x[b, :T].T)
        m["pT"] = np.ascontiguousarray(p[:, b, :T].transpose(0, 2, 1))
        in_maps.append(m)
    res = run_bass_kernel_spmd(prog.nc, in_maps, core_ids=list(range(n_cores)))
    out = np.stack([np.ascontiguousarray(res.results[b]["outT"].T) for b in range(B)], 0)
    return out.astype(np.float32), res, prog


def kernel(**inputs):
    out, _, _ = run(inputs, T=4096)
    return out
```
